# Optimizing a Trainium2 kernel written in Bass

```python
import jax, jax.numpy as jnp
from jax import lax
import numpy as np

D_MODEL = 2048
BATCH = 2
SEQ = 4096
DEPTH = 1
DEC_BATCH = 32
DEC_SEQ = 1
PAST_LEN = 16384
PAGE_SIZE = 128

ATT_HEADS = 8
ATT_KV_HEADS = 2
ATT_HEAD_DIM = 128
ATT_WIDTH = ATT_HEADS * ATT_HEAD_DIM
ATT_KV_WIDTH = ATT_KV_HEADS * ATT_HEAD_DIM
ATT_SCALE = ATT_HEAD_DIM ** -0.5
IDX_HEADS = 16
IDX_DIM = 64
IDX_SCALE = (IDX_HEADS * IDX_DIM) ** -0.5
TOPK_MAX = 256
Q_BLOCK = 128
ML_HEADS = 4
ML_QK_DIM = 128
ML_V_DIM = 256
ML_WIDTH = ML_HEADS * ML_V_DIM
ML_CHUNK = 64
MIX_WIDTH = ATT_WIDTH + ML_WIDTH
RMS_EPS = 1e-6
IN_SIZES = (ATT_WIDTH, ATT_KV_WIDTH, ATT_KV_WIDTH, IDX_HEADS * IDX_DIM, IDX_DIM, IDX_HEADS, ATT_WIDTH,
            ML_HEADS * ML_QK_DIM, ML_HEADS * ML_QK_DIM, ML_WIDTH, ML_HEADS, ML_HEADS, ML_WIDTH, ML_WIDTH)
IN_WIDTH = sum(IN_SIZES)

kernel_name = "hymba_dsa_mlstm_decoder_step"


def rmsnorm(x, w):
    xf = x.astype(jnp.float32)
    y = xf * lax.rsqrt(jnp.mean(xf * xf, axis=-1, keepdims=True) + RMS_EPS) * w.astype(jnp.float32)
    return y.astype(x.dtype)


def project(h, w_in, b_i, b_f):
    B, T, _ = h.shape
    split_points = [int(s) for s in np.cumsum(IN_SIZES)[:-1]]
    (aq, ak, av, iq, ik, iw, az, mq, mk, mv, mi, mf, mo, mz) = jnp.split(h @ w_in, split_points, axis=-1)
    f32 = jnp.float32
    return dict(
        q=aq.reshape(B, T, ATT_HEADS, ATT_HEAD_DIM),
        k=ak.reshape(B, T, ATT_KV_HEADS, ATT_HEAD_DIM),
        v=av.reshape(B, T, ATT_KV_HEADS, ATT_HEAD_DIM),
        qi=iq.reshape(B, T, IDX_HEADS, IDX_DIM),
        ki=ik,
        wi=iw,
        az=az,
        mq=mq.reshape(B, T, ML_HEADS, ML_QK_DIM),
        mk=mk.reshape(B, T, ML_HEADS, ML_QK_DIM) * (ML_QK_DIM ** -0.5),
        mv=mv.reshape(B, T, ML_HEADS, ML_V_DIM),
        ig=mi.astype(f32) + b_i.astype(f32),
        lf=jax.nn.log_sigmoid(mf.astype(f32) + b_f.astype(f32)),
        mo=mo, mz=mz)


def indexer_scores(qi, ki, wi):
    s = jax.nn.relu(jnp.einsum('bqhd,bld->bqlh', qi.astype(jnp.float32), ki.astype(jnp.float32)))
    return jnp.einsum('bqlh,bqh->bql', s, wi.astype(jnp.float32)) * IDX_SCALE


def sparse_attend(q, kg, vg, valid):
    B, Q = q.shape[:2]
    G = ATT_HEADS // ATT_KV_HEADS
    qg = q.reshape(B, Q, ATT_KV_HEADS, G, ATT_HEAD_DIM).astype(jnp.float32)
    logits = jnp.einsum('bqhgd,bqjhd->bqhgj', qg, kg.astype(jnp.float32)) * ATT_SCALE
    logits = jnp.where(valid[:, :, None, None, :], logits, -jnp.inf)
    p = jax.nn.softmax(logits, axis=-1)
    o = jnp.einsum('bqhgj,bqjhd->bqhgd', p, vg.astype(jnp.float32))
    return o.reshape(B, Q, ATT_WIDTH)


def gather_rows(a, idx):
    return jax.vmap(lambda aa, ii: aa[ii])(a, idx)


def dsa_prompt(q, k, v, qi, ki, wi):
    B, S = q.shape[:2]
    topk = min(TOPK_MAX, S // 4)
    nb = S // Q_BLOCK
    key_pos = jnp.arange(S)

    def block(args):
        i, qb, qib, wib = args
        pos = i * Q_BLOCK + jnp.arange(Q_BLOCK)
        scores = indexer_scores(qib, ki, wib)
        scores = jnp.where((key_pos[None, :] <= pos[:, None])[None], scores, -jnp.inf)
        _, idx = lax.top_k(scores, topk)
        valid = idx <= pos[None, :, None]
        return sparse_attend(qb, gather_rows(k, idx), gather_rows(v, idx), valid)

    qb = q.reshape(B, nb, Q_BLOCK, ATT_HEADS, ATT_HEAD_DIM).swapaxes(0, 1)
    qib = qi.reshape(B, nb, Q_BLOCK, IDX_HEADS, IDX_DIM).swapaxes(0, 1)
    wib = wi.reshape(B, nb, Q_BLOCK, IDX_HEADS).swapaxes(0, 1)
    out = lax.map(block, (jnp.arange(nb), qb, qib, wib))
    return out.swapaxes(0, 1).reshape(B, S, ATT_WIDTH)


def dsa_sample(q, k_new, v_new, qi, ki_new, wi, cache_k, cache_v, cache_ik, page_table):
    Bd, T = q.shape[:2]
    past = page_table.shape[1] * PAGE_SIZE
    L = past + T
    topk = min(TOPK_MAX, L // 4)
    ki_past = cache_ik[page_table].reshape(Bd, past, IDX_DIM)
    ki_all = jnp.concatenate([ki_past, ki_new.astype(ki_past.dtype)], axis=1)
    scores = indexer_scores(qi, ki_all, wi)
    pos = past + jnp.arange(T)
    scores = jnp.where((jnp.arange(L)[None, :] <= pos[:, None])[None], scores, -jnp.inf)
    _, idx = lax.top_k(scores, topk)
    valid = idx <= pos[None, :, None]
    in_past = idx < past
    ip = jnp.minimum(idx, past - 1)
    phys = jax.vmap(lambda pt, ii: pt[ii])(page_table, ip // PAGE_SIZE)
    off = ip % PAGE_SIZE
    inew = jnp.clip(idx - past, 0, T - 1)
    kg = jnp.where(in_past[..., None, None], cache_k[phys, off], gather_rows(k_new.astype(cache_k.dtype), inew))
    vg = jnp.where(in_past[..., None, None], cache_v[phys, off], gather_rows(v_new.astype(cache_v.dtype), inew))
    return sparse_attend(q, kg, vg, valid)


def mlstm_chunk(carry, xs):
    C, n, m = carry
    q, k, v, ig, lf = xs
    L = q.shape[2]
    b = jnp.cumsum(lf, axis=-1)
    causal = jnp.tril(jnp.ones((L, L), dtype=bool))
    log_d = jnp.where(causal, b[..., :, None] - b[..., None, :] + ig[..., None, :], -jnp.inf)
    log_a = b + m[..., None]
    m_t = jnp.maximum(log_a, jnp.max(log_d, axis=-1))
    d = jnp.exp(log_d - m_t[..., None])
    a = jnp.exp(log_a - m_t)
    s = jnp.einsum('bhtd,bhsd->bhts', q, k) * d
    num = a[..., None] * jnp.einsum('bhvd,bhtd->bhtv', C, q) + jnp.einsum('bhts,bhsv->bhtv', s, v)
    den = a * jnp.einsum('bhd,bhtd->bht', n, q) + jnp.sum(s, axis=-1)
    h = num / jnp.maximum(jnp.abs(den), jnp.exp(-m_t))[..., None]
    m_new = m_t[..., -1]
    w = jnp.exp(b[..., -1:] - b + ig - m_new[..., None])
    a_end = a[..., -1]
    C_new = a_end[..., None, None] * C + jnp.einsum('bhs,bhsv,bhsd->bhvd', w, v, k)
    n_new = a_end[..., None] * n + jnp.einsum('bhs,bhsd->bhd', w, k)
    return (C_new, n_new, m_new), h


def mlstm_run(q, k, v, ig, lf, C0, n0, m0, chunk):
    B, T, H, _ = q.shape
    nc = T // chunk
    f32 = jnp.float32
    seq4 = lambda a: a.astype(f32).reshape(B, nc, chunk, H, a.shape[-1]).transpose(1, 0, 3, 2, 4)
    seq3 = lambda a: a.astype(f32).reshape(B, nc, chunk, H).transpose(1, 0, 3, 2)
    (C, n, m), h = lax.scan(mlstm_chunk, (C0.astype(f32), n0.astype(f32), m0.astype(f32)),
                            (seq4(q), seq4(k), seq4(v), seq3(ig), seq3(lf)))
    h = h.transpose(1, 0, 3, 2, 4).reshape(B, T, H, ML_V_DIM)
    return h, C, n, m


def merge(x, att_o, p, h_ml, ml_norm_w, w_out):
    B, T, _ = x.shape
    a = att_o.astype(x.dtype) * jax.nn.silu(p['az'])
    hn = h_ml * lax.rsqrt(jnp.mean(h_ml * h_ml, axis=-1, keepdims=True) + RMS_EPS)
    hn = hn * ml_norm_w.astype(jnp.float32).reshape(ML_HEADS, ML_V_DIM)
    mo = hn.reshape(B, T, ML_WIDTH).astype(x.dtype) * jax.nn.sigmoid(p['mo']) * jax.nn.silu(p['mz'])
    return x + jnp.concatenate([a, mo], axis=-1) @ w_out


def setup_inputs(seed: int = 0) -> dict:
    key = jax.random.key(seed)
    ks = jax.random.split(key, 20)
    n_pages = PAST_LEN // PAGE_SIZE
    used = DEC_BATCH * n_pages
    n_pool = used + max(1, used // 4)
    f32 = jnp.float32
    page_table = jax.random.permutation(ks[0], n_pool)[:used].reshape(DEC_BATCH, n_pages).astype(jnp.int32)
    return {
        'x_prompt': jax.random.normal(ks[1], (BATCH, SEQ, D_MODEL), f32),
        'x_sample': jax.random.normal(ks[2], (DEC_BATCH, DEC_SEQ, D_MODEL), f32),
        'cache_k': jax.random.normal(ks[3], (DEPTH, n_pool, PAGE_SIZE, ATT_KV_HEADS, ATT_HEAD_DIM), f32),
        'cache_v': jax.random.normal(ks[4], (DEPTH, n_pool, PAGE_SIZE, ATT_KV_HEADS, ATT_HEAD_DIM), f32),
        'cache_idx_k': jax.random.normal(ks[5], (DEPTH, n_pool, PAGE_SIZE, IDX_DIM), f32),
        'state_C': 0.1 * jax.random.normal(ks[6], (DEPTH, DEC_BATCH, ML_HEADS, ML_V_DIM, ML_QK_DIM), f32),
        'state_n': 0.5 * jax.random.normal(ks[7], (DEPTH, DEC_BATCH, ML_HEADS, ML_QK_DIM), f32),
        'state_m': jax.random.normal(ks[8], (DEPTH, DEC_BATCH, ML_HEADS), f32),
        'page_table': page_table,
        'norm_w': 1.0 + 0.01 * jax.random.normal(ks[9], (DEPTH, D_MODEL), f32),
        'w_in': jax.random.normal(ks[10], (DEPTH, D_MODEL, IN_WIDTH), f32) * D_MODEL ** -0.5,
        'b_igate': 0.1 * jax.random.normal(ks[11], (DEPTH, ML_HEADS), f32),
        'b_fgate': 3.0 + 0.1 * jax.random.normal(ks[12], (DEPTH, ML_HEADS), f32),
        'ml_norm_w': 1.0 + 0.01 * jax.random.normal(ks[13], (DEPTH, ML_WIDTH), f32),
        'w_out': jax.random.normal(ks[14], (DEPTH, MIX_WIDTH, D_MODEL), f32) * MIX_WIDTH ** -0.5,
        'final_norm_w': 1.0 + 0.01 * jax.random.normal(ks[15], (D_MODEL,), f32),
    }


def reference(x_prompt, x_sample, cache_k, cache_v, cache_idx_k, state_C, state_n, state_m, page_table,
              norm_w, w_in, b_igate, b_fgate, ml_norm_w, w_out, final_norm_w):
    xp, xs = x_prompt, x_sample
    Bp = xp.shape[0]
    kp_l, vp_l, ikp_l, Cp_l, np_l, mp_l = [], [], [], [], [], []
    ks_l, vs_l, iks_l, Cs_l, ns_l, ms_l = [], [], [], [], [], []
    for l in range(DEPTH):
        p = project(rmsnorm(xp, norm_w[l]), w_in[l], b_igate[l], b_fgate[l])
        att = dsa_prompt(p['q'], p['k'], p['v'], p['qi'], p['ki'], p['wi'])
        zC = jnp.zeros((Bp, ML_HEADS, ML_V_DIM, ML_QK_DIM), jnp.float32)
        zn = jnp.zeros((Bp, ML_HEADS, ML_QK_DIM), jnp.float32)
        zm = jnp.zeros((Bp, ML_HEADS), jnp.float32)
        h_ml, C1, n1, m1 = mlstm_run(p['mq'], p['mk'], p['mv'], p['ig'], p['lf'], zC, zn, zm, ML_CHUNK)
        xp = merge(xp, att, p, h_ml, ml_norm_w[l], w_out[l])
        kp_l.append(p['k']); vp_l.append(p['v']); ikp_l.append(p['ki'])
        Cp_l.append(C1); np_l.append(n1); mp_l.append(m1)
        s = project(rmsnorm(xs, norm_w[l]), w_in[l], b_igate[l], b_fgate[l])
        att_s = dsa_sample(s['q'], s['k'], s['v'], s['qi'], s['ki'], s['wi'],
                           cache_k[l], cache_v[l], cache_idx_k[l], page_table)
        h_s, C2, n2, m2 = mlstm_run(s['mq'], s['mk'], s['mv'], s['ig'], s['lf'],
                                    state_C[l], state_n[l], state_m[l], xs.shape[1])
        xs = merge(xs, att_s, s, h_s, ml_norm_w[l], w_out[l])
        ks_l.append(s['k']); vs_l.append(s['v']); iks_l.append(s['ki'])
        Cs_l.append(C2); ns_l.append(n2); ms_l.append(m2)
    y_prompt = rmsnorm(xp, final_norm_w)
    y_sample = rmsnorm(xs, final_norm_w)
    return (y_prompt, y_sample,
            jnp.stack(kp_l), jnp.stack(vp_l), jnp.stack(ikp_l), jnp.stack(Cp_l), jnp.stack(np_l), jnp.stack(mp_l),
            jnp.stack(ks_l), jnp.stack(vs_l), jnp.stack(iks_l), jnp.stack(Cs_l), jnp.stack(ns_l), jnp.stack(ms_l))
```

```python
import contextlib
import numpy as np
import ml_dtypes
import concourse.bass as bass
import concourse.mybir as mybir
from concourse.bass_utils import run_bass_kernel_spmd

F32 = mybir.dt.float32
BF16 = mybir.dt.bfloat16
I32 = mybir.dt.int32
ALU = mybir.AluOpType
AF = mybir.ActivationFunctionType
AX = mybir.AxisListType

ENGS = ("pe", "act", "dve", "pool", "sp")

D = 2048
NCH = 16
INW = 7768
C_AQ, C_AK, C_AV, C_IQ, C_IK, C_IW, C_AZ = 0, 1024, 1280, 1536, 2560, 2624, 2640
C_MQ, C_MK, C_MV, C_MI, C_MF, C_MO, C_MZ = 3664, 4176, 4688, 5712, 5716, 5720, 6744
EPS = 1e-6
NPRE = 24
NOWN = 8
NT1 = NPRE + NOWN
NEG = -1.0e30
LN_DK = float(np.log(128.0 ** -0.5))
NITER = 18
import os as _os
KMAXOPS = int(_os.environ.get('KMAXOPS', '100000000'))
KLOG = bool(_os.environ.get('KLOG'))


class Buf:
    __slots__ = ("name", "writer", "readers", "excl")

    def __init__(self, name=""):
        self.name = name
        self.excl = False
        self.writer = None
        self.readers = {}


class Sched:
    NDMA = 6

    def __init__(self, nc):
        self.nc = nc
        self.ops = {e: [] for e in ENGS}
        self.cnt = {e: 0 for e in ENGS}
        self.waited = {e: {} for e in ENGS}
        self.dma_rr = {e: 0 for e in ENGS}
        self.dma_cnt = {}
        self.cap = None

    def capture(self, f):
        prev, self.cap = self.cap, []
        f()
        out, self.cap = self.cap, prev
        return out

    def replay_rr(self, lists, weights=None):
        pos = [0] * len(lists)
        weights = weights or [1] * len(lists)
        live = True
        while live:
            live = False
            for k, l in enumerate(lists):
                for _ in range(weights[k]):
                    if pos[k] < len(l):
                        kind, eng, fn, reads, writes = l[pos[k]]
                        pos[k] += 1
                        live = True
                        (self.op if kind == "op" else self.dma)(eng, fn, reads, writes)

    def _need(self, eng, dep, waits):
        if dep is None:
            return
        key, val, deng = dep
        if deng == eng and eng == "pe":
            return
        if self.waited[eng].get(key, 0) >= val:
            return
        waits[key] = max(waits.get(key, 0), val)

    def _deps(self, eng, reads, writes, is_dma):
        waits = {}
        for b in reads:
            self._need(eng, b.writer, waits)
        for b in writes:
            self._need(eng, b.writer, waits)
            for rk, (rv, re_) in b.readers.items():
                self._need(eng, (rk, rv, re_), waits)
        for k, v in waits.items():
            self.waited[eng][k] = v
        return waits

    def _mark(self, reads, writes, tok):
        for b in reads:
            if b.readers.get(tok[0], (0,))[0] < tok[1]:
                b.readers[tok[0]] = (tok[1], tok[2])
        for b in writes:
            b.writer = tok
            b.readers = {}

    def op(self, eng, fn, reads=(), writes=()):
        if self.cap is not None:
            self.cap.append(("op", eng, fn, list(reads), list(writes)))
            return
        self.total = getattr(self, 'total', 0) + 1
        if self.total > KMAXOPS:
            return
        if KLOG:
            print('OP', self.total, eng, [b.name for b in reads], '->', [b.name for b in writes])
        if any(b.excl for b in reads):
            writes = list(writes) + [b for b in reads if b.excl]
            reads = [b for b in reads if not b.excl]
        waits = self._deps(eng, reads, writes, False)
        self.cnt[eng] += 1
        tok = ("c_" + eng, self.cnt[eng], eng)
        self.ops[eng].append((list(waits.items()), fn, ("c_" + eng, 1)))
        self._mark(reads, writes, tok)

    def dma(self, eng, fn, reads=(), writes=()):
        if self.cap is not None:
            self.cap.append(("dma", eng, fn, list(reads), list(writes)))
            return
        self.total = getattr(self, 'total', 0) + 1
        if self.total > KMAXOPS:
            return
        if KLOG:
            print('DMA', self.total, eng, [b.name for b in reads], '->', [b.name for b in writes])
        waits = self._deps(eng, reads, writes, True)
        i = self.dma_rr[eng]
        self.dma_rr[eng] = (i + 1) % self.NDMA
        key = "d_%s_%d" % (eng, i)
        prev = self.dma_cnt.get(key, 0)
        if prev > 0 and self.waited[eng].get(key, 0) < prev:
            waits[key] = max(waits.get(key, 0), prev)
            self.waited[eng][key] = prev
        self.dma_cnt[key] = prev + 16
        tok = (key, prev + 16, "dma_" + eng)
        self.ops[eng].append((list(waits.items()), fn, (key, 16)))
        self._mark(reads, writes, tok)

    def barrier(self):
        if _os.environ.get('KBAR'):
            print('BARRIER at op', getattr(self, 'total', 0), {e: self.cnt[e] for e in ENGS})
        tgt = {"c_" + e: self.cnt[e] for e in ENGS if self.cnt[e] > 0}
        tgt.update(self.dma_cnt)
        for eng in ENGS:
            waits = []
            for key, val in tgt.items():
                if self.waited[eng].get(key, 0) < val:
                    waits.append((key, val))
                    self.waited[eng][key] = val
            self.ops[eng].append((waits, None, None))

    def finish(self, eng="sp"):
        waits = []
        for key, val in self.dma_cnt.items():
            if self.waited[eng].get(key, 0) < val:
                waits.append((key, val))
                self.waited[eng][key] = val
        self.ops[eng].append((waits, None, None))

    def emit(self):
        nc = self.nc
        keys = ["c_" + e for e in ENGS if self.cnt[e] > 0] + list(self.dma_cnt.keys())
        with contextlib.ExitStack() as st:
            sems = {k: st.enter_context(nc.semaphore(k)) for k in keys}
            block = st.enter_context(nc.Block())

            def run(eng_name):
                def body(eng):
                    for waits, fn, inc in self.ops[eng_name]:
                        for k, v in waits:
                            eng.wait_ge(sems[k], v)
                        if fn is not None:
                            fn(eng).then_inc(sems[inc[0]], inc[1])
                return body

            block.sync(run("sp"))
            block.tensor(run("pe"))
            block.scalar(run("act"))
            block.vector(run("dve"))
            block.gpsimd(run("pool"))


class TB:
    def __init__(self, t, name):
        self.t = t
        self.name = name
        self.bufs = {}

    def b(self, key=0):
        if key not in self.bufs:
            self.bufs[key] = Buf("%s/%s" % (self.name, key))
        return self.bufs[key]

    def __getitem__(self, k):
        return self.t[k]


class TView:
    def __init__(self, tb, lo, hi):
        self.ap = tb.t[:, lo:hi]
        self.buf = tb.b()

    def __getitem__(self, k):
        if isinstance(k, slice) and k == slice(None):
            return self.ap
        return self.ap[k]

    def b(self, key=0):
        return self.buf


def build_program(ntile2=NOWN + 1, debug=False):
    nc = bass.Bass("TRN2", target_bir_lowering=False)
    NT2 = ntile2
    TOK2 = NT2 * 128

    def din(name, shape, dt=F32):
        return nc.dram_tensor(name, shape, dt, kind="ExternalInput").ap()

    def dout(name, shape, dt=F32):
        return nc.dram_tensor(name, shape, dt, kind="ExternalOutput").ap()

    xk = din("xk", [NT1 * 128, D])
    xs = din("xs", [128, D])
    st_C = din("st_C", [4, 4, 256, 128])
    ptabT = din("ptabT", [128, 4], I32)
    cidx = din("cidx", [5120, 8192])
    ck = din("ck", [655360, 256])
    cv = din("cv", [655360, 256])
    c_sut = din("c_sut", [128, 128])
    c_iota = din("c_iota", [1, 17 + 16 + 128])
    st_n = din("st_n", [4, 512])
    st_m = din("st_m", [4, 4])
    valid = din("valid", [1, NT1 * 128])
    w_in = din("w_in", [D, INW])
    w_out = din("w_out", [D, D])
    norm_w = din("norm_w", [1, D])
    fnorm_w = din("fnorm_w", [1, D])
    mlnorm_w = din("mlnorm_w", [1, 1024])
    b_ig = din("b_ig", [4, 1])
    b_fg = din("b_fg", [4, 1])
    c_ident = din("c_ident", [128, 128])
    c_tri = din("c_tri", [128, 128])
    c_tribias = din("c_tribias", [128, 128])
    c_pow = din("c_pow", [1, NITER])

    y_out = dout("y_out", [NOWN * 128, D])
    ys_out = dout("ys_out", [128, D])
    Cs_out = dout("Cs_out", [4, 4, 256, 128])
    ks_out = dout("ks_out", [4, 256])
    vs_out = dout("vs_out", [4, 256])
    kis_out = dout("kis_out", [4, 64])
    scr = nc.dram_tensor("scr_idx", [4, 256], I32, kind="Internal").ap()
    ns_out = dout("ns_out", [4, 512])
    ms_out = dout("ms_out", [4, 4])
    k_out = dout("k_out", [NOWN * 128, 256])
    v_out = dout("v_out", [NOWN * 128, 256])
    ki_out = dout("ki_out", [NOWN * 128, 64])
    C_out = dout("C_out", [4, 256, 128])
    n_out = dout("n_out", [4, 128])
    m_out = dout("m_out", [4, 1])

    w_in_v = w_in.rearrange("(c p) n -> p c n", p=128)
    w_out_v = w_out.rearrange("(c p) n -> p c n", p=128)

    S = Sched(nc)
    st = contextlib.ExitStack()

    _uid = [0]

    def sb(name, shape, dt, stack=None):
        _uid[0] += 1
        name = "%s_%d" % (name, _uid[0])
        return TB((stack or st).enter_context(nc.sbuf_tensor(name, shape, dt)), name)

    def ps(name, shape, dt, stack=None):
        return TB((stack or st).enter_context(nc.psum_tensor(name, shape, dt)), name)

    with st:
        PB = [ps("pb%d" % i, [128, 512], F32) for i in range(8)]
        for pb_ in PB:
            pb_.b().excl = True

        ident = sb("ident", [128, 128], BF16)
        identf = sb("identf", [128, 128], F32)
        tri = sb("tri", [128, 128], BF16)
        tribias = sb("tribias", [128, 128], F32)
        ones_bf = sb("ones_bf", [128, 128], BF16)
        ones4f = sb("ones4f", [4, 128], F32)
        identf_ones = sb("onesf", [128, 128], F32)
        big = sb("big", [4, 1], F32)
        bi_sb = sb("bi_sb", [4, 1], F32)
        nbf_sb = sb("nbf_sb", [4, 1], F32)
        powr = sb("powr", [128, NITER], F32)
        Cst = sb("Cst", [128, 4, 257], F32)
        Brow = sb("Brow", [4, 1], F32)
        Mrow = sb("Mrow", [4, 1], F32)
        ATT_T = sb("ATT_T", [128, NT2, 8, 128], BF16)
        if NT2 > NOWN:
            S.op("pool", lambda e: e.memset(ATT_T[:, NOWN, :, :], 0.0), writes=[ATT_T.b(NOWN)])

        S.dma("pool", lambda e: e.dma_start(out=ident[:], in_=c_ident), writes=[ident.b()])
        S.dma("sp", lambda e: e.dma_start(out=identf[:], in_=c_ident), writes=[identf.b()])
        S.dma("pool", lambda e: e.dma_start(out=tri[:], in_=c_tri), writes=[tri.b()])
        S.dma("sp", lambda e: e.dma_start(out=tribias[:], in_=c_tribias), writes=[tribias.b()])
        S.dma("sp", lambda e: e.dma_start(out=bi_sb[:], in_=b_ig), writes=[bi_sb.b()])
        S.dma("sp", lambda e: e.dma_start(out=nbf_sb[:], in_=b_fg), writes=[nbf_sb.b()])
        S.dma("sp", lambda e: e.dma_start(out=powr[:], in_=c_pow.partition_broadcast(128)), writes=[powr.b()])
        S.op("dve", lambda e: e.memset(ones_bf[:], 1.0), writes=[ones_bf.b()])
        S.op("dve", lambda e: e.memset(ones4f[:], 1.0), writes=[ones4f.b()])
        S.op("dve", lambda e: e.memset(identf_ones[:], 1.0), writes=[identf_ones.b()])
        S.op("dve", lambda e: e.memset(Cst[:], 0.0), writes=[Cst.b(h) for h in range(4)])
        S.op("dve", lambda e: e.memset(Brow[:], 0.0), writes=[Brow.b()])
        S.op("dve", lambda e: e.memset(Mrow[:], 0.0), writes=[Mrow.b()])
        S.op("dve", lambda e: e.tensor_scalar(nbf_sb[:], nbf_sb[:], -1.0, None, op0=ALU.mult),
             reads=[nbf_sb.b()], writes=[nbf_sb.b()])

        def load_w(dst, c0, ncols, dcol=0, src=w_in_v):
            for hh in range(2):
                cs = slice(hh * 8, hh * 8 + 8)
                S.dma("pool", lambda e, cs=cs: e.dma_start(out=dst.t[:, cs, dcol:dcol + ncols],
                                                             in_=src[:, cs, c0:c0 + ncols]),
                      writes=[dst.b()])

        def rmsnorm_T(xrows, xt, hbf, junk, ssq, hT_dst, hT_buf, nw_bc, slot):
            S.dma("sp", lambda e: e.dma_start(out=xt[:, slot, :], in_=xrows), writes=[xt.b(slot)])
            S.op("act", lambda e: e.activation(junk[:], xt[:, slot, :], AF.Square, accum_out=ssq[:, 0:1]),
                 reads=[xt.b(slot)], writes=[junk.b(), ssq.b(0)])
            S.op("act", lambda e: e.activation(ssq[:, 1:2], ssq[:, 0:1], AF.Sqrt, scale=1.0 / D, bias=epsb[:, 0:1]),
                 reads=[ssq.b(0), epsb.b()], writes=[ssq.b(1)])
            S.op("dve", lambda e: e.reciprocal(ssq[:, 2:3], ssq[:, 1:2]), reads=[ssq.b(1)], writes=[ssq.b(2)])
            S.op("dve", lambda e: e.scalar_tensor_tensor(out=hbf[:], in0=xt[:, slot, :], scalar=ssq[:, 2:3],
                                                         in1=nw_bc[:], op0=ALU.mult, op1=ALU.mult),
                 reads=[xt.b(slot), ssq.b(2), nw_bc.b()], writes=[hbf.b()])
            for half in range(2):
                pb = PB[half]
                pv = pb[:].bitcast(BF16)
                for cc in range(8):
                    c = half * 8 + cc
                    S.op("pe", lambda e, c=c, cc=cc, pv=pv: e.transpose(pv[:, cc * 128:(cc + 1) * 128],
                                                                        hbf[:, c * 128:(c + 1) * 128], ident[:]),
                         reads=[hbf.b(), ident.b()], writes=[pb.b()])
                eng = "act" if half == 0 else "dve"
                if eng == "act":
                    S.op("act", lambda e, half=half, pv=pv: e.copy(hT_dst(half), pv.rearrange("p (c t) -> p c t", c=8)),
                         reads=[pb.b()], writes=[hT_buf])
                else:
                    S.op("dve", lambda e, half=half, pv=pv: e.tensor_copy(hT_dst(half), pv.rearrange("p (c t) -> p c t", c=8)),
                         reads=[pb.b()], writes=[hT_buf])

        epsb = sb("epsb", [128, 1], F32)
        S.op("dve", lambda e: e.memset(epsb[:], EPS), writes=[epsb.b()])

        def gate_rows(gi_ps, gf_ps, gi_b, gf_b, vrow, nbrow, G):
            ig, lf, e1, B_, Gg, Mt, u, nml, nml2, aend, cl = (G[k] for k in
                                                               "ig lf e1 B G Mt u nml nml2 aend cl".split())
            S.op("act", lambda e: e.activation(ig[:], gi_ps, AF.Identity, bias=bi_sb[:, 0:1]),
                 reads=[gi_b, bi_sb.b()], writes=[ig.b()])
            S.op("act", lambda e: e.activation(e1[:], gf_ps, AF.Exp, scale=-1.0, bias=nbf_sb[:, 0:1]),
                 reads=[gf_b, nbf_sb.b()], writes=[e1.b()])
            S.op("act", lambda e: e.activation(e1[:], e1[:], AF.Ln, bias=one4[:, 0:1]),
                 reads=[e1.b(), one4.b()], writes=[e1.b()])
            if vrow is not None:
                S.op("dve", lambda e: e.tensor_tensor(ig[:], ig[:], vrow, op=ALU.mult), reads=[ig.b(), G["vb"]], writes=[ig.b()])
                S.op("dve", lambda e: e.tensor_tensor(ig[:], ig[:], nbrow, op=ALU.add), reads=[ig.b(), G["nb"]], writes=[ig.b()])
                S.op("dve", lambda e: e.scalar_tensor_tensor(out=lf[:], in0=e1[:], scalar=-1.0, in1=vrow,
                                                             op0=ALU.mult, op1=ALU.mult),
                     reads=[e1.b(), G["vb"]], writes=[lf.b()])
            else:
                S.op("dve", lambda e: e.tensor_scalar(lf[:], e1[:], -1.0, None, op0=ALU.mult),
                     reads=[e1.b()], writes=[lf.b()])
            S.op("dve", lambda e: e.tensor_tensor_scan(B_[:], ones4f[:], lf[:], Brow[:, 0:1], op0=ALU.mult, op1=ALU.add),
                 reads=[ones4f.b(), lf.b(), Brow.b()], writes=[B_.b()])
            S.op("dve", lambda e: e.tensor_copy(Brow[:], B_[:, 127:128]), reads=[B_.b()], writes=[Brow.b()])
            S.op("dve", lambda e: e.tensor_tensor(Gg[:], ig[:], B_[:], op=ALU.subtract), reads=[ig.b(), B_.b()], writes=[Gg.b()])
            S.op("dve", lambda e: e.tensor_tensor_scan(Mt[:], Gg[:], Gg[:], Mrow[:, 0:1], op0=ALU.max, op1=ALU.max),
                 reads=[Gg.b(), Mrow.b()], writes=[Mt.b()])
            S.op("dve", lambda e: e.tensor_scalar(nml[:], Mt[:, 127:128], -1.0, None, op0=ALU.mult), reads=[Mt.b()], writes=[nml.b()])
            S.op("dve", lambda e: e.tensor_scalar(nml2[:], Mt[:, 127:128], -1.0, LN_DK, op0=ALU.mult, op1=ALU.add),
                 reads=[Mt.b()], writes=[nml2.b()])
            S.op("act", lambda e: e.activation(aend[:], Mrow[:], AF.Exp, bias=nml[:, 0:1]),
                 reads=[Mrow.b(), nml.b()], writes=[aend.b()])
            S.op("act", lambda e: e.activation(u[:], Gg[:], AF.Exp, bias=nml2[:, 0:1]), reads=[Gg.b(), nml2.b()], writes=[u.b()])
            S.op("act", lambda e: e.activation(cl[:], B_[:], AF.Exp, scale=-1.0, bias=nml[:, 0:1]),
                 reads=[B_.b(), nml.b()], writes=[cl.b()])
            S.op("dve", lambda e: e.tensor_copy(Mrow[:], Mt[:, 127:128]), reads=[Mt.b(), aend.b()], writes=[Mrow.b()])

        one4 = sb("one4", [4, 1], F32)
        S.op("dve", lambda e: e.memset(one4[:], 1.0), writes=[one4.b()])

        def make_gate_tiles(stack, pfx):
            G = {}
            for k in "ig lf e1 B G Mt cl".split():
                G[k] = sb(pfx + k, [4, 128], F32, stack)
            G["u"] = sb(pfx + "u", [4, 128], BF16, stack)
            G["clb"] = sb(pfx + "clb", [4, 128], BF16, stack)
            for k in "nml nml2 aend".split():
                G[k] = sb(pfx + k, [4, 1], F32, stack)
            G["dg"] = sb(pfx + "dg", [4, 4], F32, stack)
            G["uT"] = sb(pfx + "uT", [128, 4], F32, stack)
            G["clT"] = sb(pfx + "clT", [128, 4], F32, stack)
            G["abc"] = sb(pfx + "abc", [128, 4], F32, stack)
            return G

        def gate_cols(G, misc, want_cl, c0=0, stage=None):
            mv = misc[:].bitcast(BF16)
            b0 = 2 * c0
            if stage in (None, 0):
                S.op("dve", lambda e: e.tensor_scalar(G["dg"][:], identf[0:4, 0:4], G["aend"][:, 0:1], None, op0=ALU.mult),
                     reads=[identf.b(), G["aend"].b()], writes=[G["dg"].b()])
                if want_cl:
                    S.op("act", lambda e: e.copy(G["clb"][:], G["cl"][:]), reads=[G["cl"].b()], writes=[G["clb"].b()])
            if stage in (None, 1):
                S.op("pe", lambda e: e.transpose(mv[:, b0:b0 + 4], G["u"][:], ident[0:4, 0:4]),
                     reads=[G["u"].b(), ident.b()], writes=[misc.b()])
                S.op("pe", lambda e: e.matmul(misc[:, c0 + 16:c0 + 20], lhsT=ones4f[:], rhs=G["dg"][:], start=True, stop=True),
                     reads=[ones4f.b(), G["dg"].b()], writes=[misc.b()])
                if want_cl:
                    S.op("pe", lambda e: e.transpose(mv[:, b0 + 8:b0 + 12], G["clb"][:], ident[0:4, 0:4]),
                         reads=[G["clb"].b(), ident.b()], writes=[misc.b()])
                S.op("dve", lambda e: e.tensor_copy(G["uT"][:], mv[:, b0:b0 + 4]), reads=[misc.b()], writes=[G["uT"].b()])
                S.op("dve", lambda e: e.tensor_copy(G["abc"][:], misc[:, c0 + 16:c0 + 20]), reads=[misc.b()], writes=[G["abc"].b()])
                if want_cl:
                    S.op("dve", lambda e: e.tensor_copy(G["clT"][:], mv[:, b0 + 8:b0 + 12]), reads=[misc.b()], writes=[G["clT"].b()])

        kvst = contextlib.ExitStack()
        with kvst:
            KT = sb("KT", [128, 2, NT1 * 128], BF16, kvst)
            Vsb = sb("Vsb", [128, NT1, 256], BF16, kvst)
            KIT = sb("KIT", [128, NT1 * 128], BF16, kvst)

            p1 = contextlib.ExitStack()
            with p1:
                W_kv = sb("W_kv", [128, NCH, 512], BF16, p1)
                W_ki = sb("W_ki", [128, NCH, 128], BF16, p1)
                W_mk = sb("W_mk", [128, NCH, 512], BF16, p1)
                W_mv = sb("W_mv", [128, NCH, 1024], BF16, p1)
                W_g = sb("W_g", [128, NCH, 8], BF16, p1)
                nw_bc = sb("nw_bc", [128, D], F32, p1)
                xt = sb("xt", [128, 2, D], F32, p1)
                hbf = sb("hbf", [128, 2, D], BF16, p1)
                junk = sb("junk", [128, D], BF16, p1)
                ssq = sb("ssq", [128, 2, 4], F32, p1)
                hT = sb("hT", [128, 2, NCH, 128], BF16, p1)
                kvo = sb("kvo", [128, 2, 512], F32, p1)
                kio = sb("kio", [128, 2, 64], F32, p1)
                kbf = sb("kbf", [128, 256], BF16, p1)
                Kp = sb("Kp", [128, 4, 128], BF16, p1)
                Vaug = sb("Vaug", [128, 4, 257], BF16, p1)
                vrow = sb("vrow", [4, 2, 128], F32, p1)
                nbrow = sb("nbrow", [4, 2, 128], F32, p1)
                G1 = make_gate_tiles(p1, "g1_")

                S.dma("sp", lambda e: e.dma_start(out=nw_bc[:], in_=norm_w.partition_broadcast(128)), writes=[nw_bc.b()])
                load_w(W_kv, C_AK, 512)
                load_w(W_ki, C_IK, 64, 0)
                load_w(W_ki, C_IK, 64, 64)
                load_w(W_g, C_MI, 8)
                load_w(W_mk, C_MK, 512)
                load_w(W_mv, C_MV, 512, 0)
                load_w(W_mv, C_MV + 512, 512, 512)
                S.op("dve", lambda e: e.memset(Vaug[:], 1.0), writes=[Vaug.b(h) for h in range(4)])

                def pre(T):
                    sl_ = T % 2
                    xrows = xk[T * 128:(T + 1) * 128, :]
                    S.dma("sp", lambda e: e.dma_start(out=xt[:, sl_, :], in_=xrows), writes=[xt.b(sl_)])
                    S.op("act", lambda e: e.activation(junk[:], xt[:, sl_, :], AF.Square, accum_out=ssq[:, sl_, 0:1]),
                         reads=[xt.b(sl_)], writes=[junk.b(), ssq.b(sl_)])
                    S.op("act", lambda e: e.activation(ssq[:, sl_, 1:2], ssq[:, sl_, 0:1], AF.Sqrt, scale=1.0 / D, bias=epsb[:, 0:1]),
                         reads=[ssq.b(sl_), epsb.b()], writes=[ssq.b(sl_)])
                    S.op("dve", lambda e: e.reciprocal(ssq[:, sl_, 2:3], ssq[:, sl_, 1:2]), reads=[ssq.b(sl_)], writes=[ssq.b(sl_)])
                    S.op("dve", lambda e: e.scalar_tensor_tensor(out=hbf[:, sl_, :], in0=xt[:, sl_, :], scalar=ssq[:, sl_, 2:3],
                                                                 in1=nw_bc[:], op0=ALU.mult, op1=ALU.mult),
                         reads=[xt.b(sl_), ssq.b(sl_), nw_bc.b()], writes=[hbf.b(sl_)])

                def mid(T):
                    sl_ = T % 2
                    for half in range(2):
                        pb = PB[half]
                        pv = pb[:].bitcast(BF16)
                        for cc in range(8):
                            c = half * 8 + cc
                            S.op("pe", lambda e, c=c, cc=cc, pv=pv: e.transpose(pv[:, cc * 128:(cc + 1) * 128],
                                                                                hbf[:, sl_, c * 128:(c + 1) * 128], ident[:]),
                                 reads=[hbf.b(sl_), ident.b()], writes=[pb.b()])
                        if half == 0:
                            S.op("act", lambda e, half=half, pv=pv: e.copy(hT[:, sl_, half * 8:(half + 1) * 8, :], pv.rearrange("p (c t) -> p c t", c=8)),
                                 reads=[pb.b()], writes=[hT.b(sl_)])
                        else:
                            S.op("dve", lambda e, half=half, pv=pv: e.tensor_copy(hT[:, sl_, half * 8:(half + 1) * 8, :], pv.rearrange("p (c t) -> p c t", c=8)),
                                 reads=[pb.b()], writes=[hT.b(sl_)])

                pre(0)
                mid(0)
                misc = PB[5]
                mvw = misc[:].bitcast(BF16)
                pmk = PB[3]
                pmv = PB[4]

                def tail_a():
                    gate_cols(G1, misc, False, c0=256, stage=1)
                    for h in range(4):
                        S.op("act", lambda e, h=h: e.activation(Kp[:, h, :], pmk[:, h * 128:(h + 1) * 128], AF.Copy,
                                                                scale=G1["uT"][:, h:h + 1]),
                             reads=[pmk.b(), G1["uT"].b()], writes=[Kp.b(h)])

                def tail_b():
                    for h in range(4):
                        pc = PB[6 + (h % 2)]
                        S.op("pe", lambda e, h=h, pc=pc: e.matmul(pc[:, 0:257], lhsT=Kp[:, h, :], rhs=Vaug[:, h, :],
                                                                  start=True, stop=True),
                             reads=[Kp.b(h), Vaug.b(h)], writes=[pc.b()])
                        S.op("dve", lambda e, h=h, pc=pc: e.scalar_tensor_tensor(out=Cst[:, h, :], in0=Cst[:, h, :],
                                                                                 scalar=G1["abc"][:, h:h + 1], in1=pc[:, 0:257],
                                                                                 op0=ALU.mult, op1=ALU.add),
                             reads=[Cst.b(h), G1["abc"].b(), pc.b()], writes=[Cst.b(h)])

                pend = False
                for T in range(NT1):
                    own = T >= NPRE
                    slot = T % 2
                    if pend:
                        tail_a()
                    if T + 1 < NT1:
                        pre(T + 1)
                    pkv = PB[2]
                    for c in range(NCH):
                        S.op("pe", lambda e, c=c, slot=slot: e.matmul(pkv[:], lhsT=hT[:, slot, c, :], rhs=W_kv[:, c, :],
                                                            start=(c == 0), stop=(c == NCH - 1)),
                             reads=[hT.b(slot), W_kv.b()], writes=[pkv.b()])
                    S.op("dve", lambda e: e.tensor_copy(kbf[:], pkv[:, 0:256]), reads=[pkv.b()], writes=[kbf.b()])
                    S.op("act", lambda e, T=T: e.copy(Vsb[:, T, :], pkv[:, 256:512]), reads=[pkv.b()], writes=[Vsb.b(T)])
                    if own:
                        o = T - NPRE
                        S.op("act", lambda e, slot=slot: e.copy(kvo[:, slot, :], pkv[:]), reads=[pkv.b()], writes=[kvo.b(slot)])
                        S.dma("sp", lambda e, o=o, slot=slot: e.dma_start(out=k_out[o * 128:(o + 1) * 128, :], in_=kvo[:, slot, 0:256]),
                              reads=[kvo.b(slot)])
                        S.dma("sp", lambda e, o=o, slot=slot: e.dma_start(out=v_out[o * 128:(o + 1) * 128, :], in_=kvo[:, slot, 256:512]),
                              reads=[kvo.b(slot)])
                    pki = PB[3] if own else PB[7]
                    for c in range(NCH):
                        S.op("pe", lambda e, c=c, slot=slot, pki=pki: e.matmul(pki[:, 0:128], lhsT=W_ki[:, c, :], rhs=hT[:, slot, c, :],
                                                                    start=(c == 0), stop=(c == NCH - 1)),
                             reads=[hT.b(slot), W_ki.b()], writes=[pki.b()])
                    S.op("dve", lambda e, T=T, pki=pki: e.tensor_copy(KIT[:, T * 128:(T + 1) * 128], pki[:, 0:128]),
                         reads=[pki.b()], writes=[KIT.b(T)])
                    for kvh in range(2):
                        S.op("pe", lambda e, kvh=kvh: e.transpose(mvw[:, 256 + kvh * 128:256 + (kvh + 1) * 128],
                                                                  kbf[:, kvh * 128:(kvh + 1) * 128], ident[:]),
                             reads=[kbf.b(), ident.b()], writes=[misc.b()])
                    S.op("act", lambda e, T=T: e.copy(KT[:, :, T * 128:(T + 1) * 128],
                                                      mvw[:, 256:512].rearrange("p (k t) -> p k t", k=2)),
                         reads=[misc.b()], writes=[KT.b(T)])
                    if own:
                        o = T - NPRE
                        for c in range(NCH):
                            S.op("pe", lambda e, c=c, slot=slot, pki=pki: e.matmul(pki[:, 128:192], lhsT=hT[:, slot, c, :], rhs=W_ki[:, c, 0:64],
                                                                        start=(c == 0), stop=(c == NCH - 1)),
                                 reads=[hT.b(slot), W_ki.b()], writes=[pki.b()])
                        S.op("act", lambda e, slot=slot, pki=pki: e.copy(kio[:, slot, :], pki[:, 128:192]),
                             reads=[pki.b()], writes=[kio.b(slot)])
                        S.dma("sp", lambda e, o=o, slot=slot: e.dma_start(out=ki_out[o * 128:(o + 1) * 128, :], in_=kio[:, slot, :]),
                              reads=[kio.b(slot)])
                        if pend:
                            tail_b()
                            pend = False
                        if T + 1 < NT1:
                            mid(T + 1)
                        continue
                    S.dma("sp", lambda e, T=T, slot=slot: e.dma_start(out=vrow[:, slot, :],
                                                                       in_=valid[:, T * 128:(T + 1) * 128].partition_broadcast(4)),
                          writes=[vrow.b(slot)])
                    S.op("dve", lambda e, slot=slot: e.tensor_scalar(nbrow[:, slot, :], vrow[:, slot, :], -1.0, -NEG,
                                                                     op0=ALU.add, op1=ALU.mult),
                         reads=[vrow.b(slot)], writes=[nbrow.b(slot)])
                    for c in range(NCH):
                        S.op("pe", lambda e, c=c, slot=slot: e.matmul(misc[0:4, 0:128], lhsT=W_g[:, c, 0:4], rhs=hT[:, slot, c, :],
                                                            start=(c == 0), stop=(c == NCH - 1)),
                             reads=[hT.b(slot), W_g.b()], writes=[misc.b()])
                    for c in range(NCH):
                        S.op("pe", lambda e, c=c, slot=slot: e.matmul(misc[0:4, 128:256], lhsT=W_g[:, c, 4:8], rhs=hT[:, slot, c, :],
                                                            start=(c == 0), stop=(c == NCH - 1)),
                             reads=[hT.b(slot), W_g.b()], writes=[misc.b()])
                    if pend:
                        tail_b()
                    if T + 1 < NT1:
                        mid(T + 1)
                    G1["vb"] = vrow.b(slot)
                    G1["nb"] = nbrow.b(slot)
                    gate_rows(misc[0:4, 0:128], misc[0:4, 128:256], misc.b(), misc.b(),
                              vrow[:, slot, :], nbrow[:, slot, :], G1)
                    gate_cols(G1, misc, False, c0=256, stage=0)

                    def mv_half(half):
                        for c in range(NCH):
                            S.op("pe", lambda e, c=c, slot=slot, half=half: e.matmul(pmv[:], lhsT=hT[:, slot, c, :],
                                                                          rhs=W_mv[:, c, half * 512:(half + 1) * 512],
                                                                          start=(c == 0), stop=(c == NCH - 1)),
                                 reads=[hT.b(slot), W_mv.b()], writes=[pmv.b()])
                        S.op("dve", lambda e, half=half: e.tensor_copy(Vaug[:, half * 2:half * 2 + 2, 0:256],
                                                                       pmv[:].rearrange("p (h v) -> p h v", h=2)),
                             reads=[pmv.b()], writes=[Vaug.b(half * 2), Vaug.b(half * 2 + 1)])

                    mv_half(0)
                    for c in range(NCH):
                        S.op("pe", lambda e, c=c, slot=slot: e.matmul(pmk[:], lhsT=hT[:, slot, c, :], rhs=W_mk[:, c, :],
                                                            start=(c == 0), stop=(c == NCH - 1)),
                             reads=[hT.b(slot), W_mk.b()], writes=[pmk.b()])
                    mv_half(1)
                    pend = True
                assert not pend

            S.barrier()
            def xrows2(t):
                return xk[(NPRE + t) * 128:(NPRE + t + 1) * 128, :] if t < NOWN else xs

            def build_hT2(stack, hT2):
                bst = contextlib.ExitStack()
                with bst:
                    nw_bc2 = sb("nw_bc2", [128, D], F32, bst)
                    xt2 = sb("xt2", [128, 2, D], F32, bst)
                    hbf2 = sb("hbf2", [128, 2, D], BF16, bst)
                    ssq2 = sb("ssq2", [128, 2, 4], F32, bst)
                    S.dma("sp", lambda e: e.dma_start(out=nw_bc2[:], in_=norm_w.partition_broadcast(128)), writes=[nw_bc2.b()])

                    def pre2(t):
                        sl_ = t % 2
                        S.dma("sp", lambda e: e.dma_start(out=xt2[:, sl_, :], in_=xrows2(t)), writes=[xt2.b(sl_)])
                        S.op("act", lambda e: e.activation(hbf2[:, sl_, :], xt2[:, sl_, :], AF.Square, accum_out=ssq2[:, sl_, 0:1]),
                             reads=[xt2.b(sl_)], writes=[hbf2.b(sl_), ssq2.b(sl_)])
                        S.op("act", lambda e: e.activation(ssq2[:, sl_, 1:2], ssq2[:, sl_, 0:1], AF.Sqrt, scale=1.0 / D, bias=epsb[:, 0:1]),
                             reads=[ssq2.b(sl_), epsb.b()], writes=[ssq2.b(sl_)])
                        S.op("dve", lambda e: e.reciprocal(ssq2[:, sl_, 2:3], ssq2[:, sl_, 1:2]), reads=[ssq2.b(sl_)], writes=[ssq2.b(sl_)])
                        S.op("dve", lambda e: e.scalar_tensor_tensor(out=hbf2[:, sl_, :], in0=xt2[:, sl_, :], scalar=ssq2[:, sl_, 2:3],
                                                                     in1=nw_bc2[:], op0=ALU.mult, op1=ALU.mult),
                             reads=[xt2.b(sl_), ssq2.b(sl_), nw_bc2.b()], writes=[hbf2.b(sl_)])

                    def mid2(t):
                        sl_ = t % 2
                        for half in range(2):
                            pb = PB[half]
                            pv = pb[:].bitcast(BF16)
                            for cc in range(8):
                                c = half * 8 + cc
                                S.op("pe", lambda e, c=c, cc=cc, pv=pv: e.transpose(pv[:, cc * 128:(cc + 1) * 128],
                                                                                    hbf2[:, sl_, c * 128:(c + 1) * 128], ident[:]),
                                     reads=[hbf2.b(sl_), ident.b()], writes=[pb.b()])
                            dst = hT2[:, half * 8:(half + 1) * 8, t * 128:(t + 1) * 128]
                            if half == 0:
                                S.op("act", lambda e, pv=pv, dst=dst: e.copy(dst, pv.rearrange("p (c t) -> p c t", c=8)),
                                     reads=[pb.b()], writes=[hT2.b(t)])
                            else:
                                S.op("dve", lambda e, pv=pv, dst=dst: e.tensor_copy(dst, pv.rearrange("p (c t) -> p c t", c=8)),
                                     reads=[pb.b()], writes=[hT2.b(t)])

                    pre2(0)
                    for t in range(NT2):
                        if t + 1 < NT2:
                            pre2(t + 1)
                        mid2(t)

            class Proj:
                def __init__(self, hT2, Wb, banks, src=None):
                    self.hT2, self.Wb, self.banks, self.k, self.slot = hT2, Wb, banks, 0, 0
                    self.src = src if src is not None else w_in_v

                def _bank(self):
                    pb = self.banks[self.k % len(self.banks)]
                    self.k += 1
                    return pb

                def load(self, c0, ncols):
                    self.slot ^= 1
                    sl = self.slot
                    for hh in range(2):
                        cs = slice(hh * 8, hh * 8 + 8)
                        S.dma("pool", lambda e, cs=cs, sl=sl: e.dma_start(out=self.Wb.t[:, sl, cs, 0:ncols],
                                                                           in_=self.src[:, cs, c0:c0 + ncols]),
                              writes=[self.Wb.b(sl)])
                    return sl

                def feat(self, c0, ncols, consume, psub=128, tiles=None):
                    sl = self.load(c0, ncols)
                    hT2, Wb = self.hT2, self.Wb
                    for j in range((ncols + psub - 1) // psub):
                        w = min(psub, ncols - j * psub)
                        for t0 in range(0, TOK2, 512):
                            n = min(512, TOK2 - t0)
                            pb = self._bank()
                            for c in range(NCH):
                                S.op("pe", lambda e, c=c, j=j, w=w, t0=t0, n=n, pb=pb, sl=sl: e.matmul(
                                    pb[0:w, 0:n], lhsT=Wb[:, sl, c, j * psub:j * psub + w], rhs=hT2[:, c, t0:t0 + n],
                                    start=(c == 0), stop=(c == NCH - 1)),
                                    reads=[Wb.b(sl)] + [hT2.b(tt) for tt in range(t0 // 128, (t0 + n) // 128)], writes=[pb.b()])
                            consume(j, t0, n, pb)

                def tok(self, c0, ncols, consume, tiles=None):
                    sl = self.load(c0, ncols)
                    hT2, Wb = self.hT2, self.Wb
                    for t in (tiles if tiles is not None else range(NT2)):
                        pb = self._bank()
                        for c in range(NCH):
                            S.op("pe", lambda e, c=c, t=t, pb=pb, sl=sl: e.matmul(
                                pb[:, 0:ncols], lhsT=hT2[:, c, t * 128:(t + 1) * 128], rhs=Wb[:, sl, c, 0:ncols],
                                start=(c == 0), stop=(c == NCH - 1)),
                                reads=[Wb.b(sl), hT2.b(t)], writes=[pb.b()])
                        consume(t, pb)

            _alt = [0]

            def evac(out_ap, in_ap, reads, writes, func=None):
                if func is not None:
                    S.op("act", lambda e: e.activation(out_ap, in_ap, func), reads=reads, writes=writes)
                    return
                _alt[0] ^= 1
                if _alt[0]:
                    S.op("act", lambda e: e.copy(out_ap, in_ap), reads=reads, writes=writes)
                else:
                    S.op("dve", lambda e: e.tensor_copy(out_ap, in_ap), reads=reads, writes=writes)

            def tview(ap, n):
                return ap.rearrange("p (t q) -> p t q", q=128)

            p2a = contextlib.ExitStack()
            with p2a:
                QT = sb("QT", [128, NT2, 8, 128], BF16, p2a)
                QIT = sb("QIT", [128, NT2, 8, 128], BF16, p2a)
                AZT = sb("AZT", [128, NT2, 8, 128], BF16, p2a)
                WI = sb("WI", [128, NT2, 16], F32, p2a)
                QIS = sb("QIS", [64, 4, 16], BF16, p2a)
                KIS = sb("KIS", [64, 4], BF16, p2a)
                KSD, VSD = Buf("ks_dram"), Buf("vs_dram")
                pj = contextlib.ExitStack()
                with pj:
                    hT2 = sb("hT2a", [128, NCH, TOK2], BF16, pj)
                    build_hT2(pj, hT2)
                    S.barrier()
                    Wb = sb("Wba", [128, 2, NCH, 512], BF16, pj)
                    P = Proj(hT2, Wb, [PB[2], PB[3], PB[4], PB[5]])

                    def to_tiles(dst, off):
                        def c(j, t0, n, pb):
                            evac(dst[:, t0 // 128:(t0 + n) // 128, off + j, :], tview(pb[:, 0:n], n),
                                 [pb.b()], [dst.b(tt) for tt in range(t0 // 128, (t0 + n) // 128)])
                        return c

                    def to_tiles_silu(dst, off):
                        def c(j, t0, n, pb):
                            evac(dst[:, t0 // 128:(t0 + n) // 128, off + j, :], tview(pb[:, 0:n], n),
                                 [pb.b()], [dst.b(tt) for tt in range(t0 // 128, (t0 + n) // 128)], func=AF.Silu)
                        return c

                    P.feat(C_AQ, 512, to_tiles(QT, 0))
                    P.feat(C_AQ + 512, 512, to_tiles(QT, 4))
                    P.feat(C_IQ, 512, to_tiles(QIT, 0))
                    P.feat(C_IQ + 512, 512, to_tiles(QIT, 4))
                    P.feat(C_AZ, 512, to_tiles_silu(AZT, 0))
                    P.feat(C_AZ + 512, 512, to_tiles_silu(AZT, 4))
                    P.tok(C_IW, 16, lambda t, pb: evac(WI[:, t, :], pb[:, 0:16], [pb.b()], [WI.b(t)]))
                    if NT2 > NOWN:
                        T8 = NOWN
                        srow = sb("srow", [4, 576], F32, pj)

                        def c_kv(t, pb):
                            evac(srow[:, 0:512], pb[0:4, 0:512], [pb.b()], [srow.b()])
                            S.dma("sp", lambda e: e.dma_start(out=ks_out, in_=srow[:, 0:256]), reads=[srow.b()], writes=[KSD])
                            S.dma("sp", lambda e: e.dma_start(out=vs_out, in_=srow[:, 256:512]), reads=[srow.b()], writes=[VSD])
                        P.tok(C_AK, 512, c_kv, tiles=[T8])

                        def c_ki(t, pb):
                            evac(srow[:, 512:576], pb[0:4, 0:64], [pb.b()], [srow.b()])
                            S.dma("sp", lambda e: e.dma_start(out=kis_out, in_=srow[:, 512:576]), reads=[srow.b()])
                        P.tok(C_IK, 64, c_ki, tiles=[T8])
                        sl = P.slot
                        pbq = PB[6]
                        for c in range(NCH):
                            S.op("pe", lambda e, c=c, sl=sl, pbq=pbq, Wb=Wb, hT2=hT2: e.matmul(pbq[0:64, 0:4], lhsT=Wb[:, sl, c, 0:64], rhs=hT2[:, c, T8 * 128:T8 * 128 + 4],
                                                                       start=(c == 0), stop=(c == NCH - 1)),
                                 reads=[Wb.b(sl), hT2.b(T8)], writes=[pbq.b()])
                        evac(KIS[:], pbq[0:64, 0:4], [pbq.b()], [KIS.b()])
                        for g in range(2):
                            sl = P.load(C_IQ + g * 512, 512)
                            pbq = PB[6 + g]
                            for hh in range(8):
                                for c in range(NCH):
                                    S.op("pe", lambda e, c=c, sl=sl, hh=hh, pbq=pbq, Wb=Wb, hT2=hT2: e.matmul(
                                        pbq[0:64, hh * 4:hh * 4 + 4], lhsT=Wb[:, sl, c, hh * 64:(hh + 1) * 64],
                                        rhs=hT2[:, c, T8 * 128:T8 * 128 + 4], start=(c == 0), stop=(c == NCH - 1)),
                                        reads=[Wb.b(sl), hT2.b(T8)], writes=[pbq.b()])
                            evac(QIS[:, :, g * 8:(g + 1) * 8], pbq[0:64, 0:32].rearrange("p (h s) -> p s h", s=4), [pbq.b()], [QIS.b()])

                S.barrier()
                at = contextlib.ExitStack()
                with at:
                    acc = sb("acc", [128, 2, NT1 * 128], F32, at)
                    junkb = sb("junkb", [128, NT1 * 128], BF16, at)
                    vbias = sb("vbias", [128, NT1 * 128], BF16, at)
                    maskT = sb("maskT", [128, 2, NT1, 128], BF16, at)
                    Rb = sb("Rb", [128, 4, 512], BF16, at)
                    Dg = sb("Dg", [128, 16, 128], BF16, at)
                    Eb = sb("Eb", [128, 4, 512], BF16, at)
                    PT = sb("PT", [128, 4, 512], BF16, at)
                    ftmp = sb("ftmp", [128, 2, 512], F32, at)
                    sm = sb("sm", [128, 8], F32, at)
                    wk = sb("wk", [128, NITER], F32, at)

                    S.dma("sp", lambda e: e.dma_start(out=acc[:, 0, :], in_=valid.partition_broadcast(128)), writes=[acc.b(0)])
                    S.op("dve", lambda e: e.tensor_scalar(vbias[:], acc[:, 0, :], -1.0, -NEG, op0=ALU.add, op1=ALU.mult),
                         reads=[acc.b(0)], writes=[vbias.b()])

                    def stage_A(i):
                        ab = i % 2
                        nkb = NPRE + 1 + i
                        NK = nkb * 128
                        for h in range(16):
                            S.op("dve", lambda e, h=h, i=i: e.tensor_scalar(Dg[:, h, :], ident[:], WI[:, i, h:h + 1], None, op0=ALU.mult),
                                 reads=[ident.b(), WI.b(i)], writes=[Dg.b()])
                        yield
                        kk = 0
                        for kc in range((NK + 511) // 512):
                            n = min(512, NK - kc * 512)
                            ksl = slice(kc * 512, kc * 512 + n)
                            pacc = PB[2]
                            pend = None
                            for h in range(16):
                                j, half = h // 2, h % 2
                                pb = PB[kk % 2]
                                rs = kk % 4
                                kk += 1
                                psl = slice(half * 64, half * 64 + 64)
                                S.op("pe", lambda e, pb=pb, j=j, psl=psl, ksl=ksl, n=n, i=i: e.matmul(
                                    pb[:, 0:n], lhsT=QIT[psl, i, j, :], rhs=KIT[psl, ksl], start=True, stop=True),
                                    reads=[QIT.b(i)] + [KIT.b(tt_) for tt_ in range(kc * 4, kc * 4 + n // 128)], writes=[pb.b()])
                                S.op("act", lambda e, pb=pb, rs=rs, n=n: e.activation(Rb[:, rs, 0:n], pb[:, 0:n], AF.Relu),
                                     reads=[pb.b()], writes=[Rb.b(rs)])
                                if pend is not None:
                                    ph, prs = pend
                                    S.op("pe", lambda e, ph=ph, prs=prs, n=n: e.matmul(pacc[:, 0:n], lhsT=Dg[:, ph, :], rhs=Rb[:, prs, 0:n],
                                                                                       start=(ph == 0), stop=False),
                                         reads=[Dg.b(), Rb.b(prs)], writes=[pacc.b()])
                                pend = (h, rs)
                            ph, prs = pend
                            S.op("pe", lambda e, ph=ph, prs=prs, n=n: e.matmul(pacc[:, 0:n], lhsT=Dg[:, ph, :], rhs=Rb[:, prs, 0:n],
                                                                               start=False, stop=True),
                                 reads=[Dg.b(), Rb.b(prs)], writes=[pacc.b()])
                            S.op("act", lambda e, ab=ab, ksl=ksl, n=n: e.copy(acc[:, ab, ksl], pacc[:, 0:n]), reads=[pacc.b()], writes=[acc.b(ab)])
                            yield

                    def stage_B(i):
                        ab = i % 2
                        nkb = NPRE + 1 + i
                        NK = nkb * 128
                        A = acc[:, ab, 0:NK]
                        ac = acc.b(ab)
                        S.op("dve", lambda e: e.tensor_reduce(sm[:, 0:1], A, AX.X, ALU.min), reads=[ac], writes=[sm.b()])
                        S.op("dve", lambda e: e.tensor_tensor(A, A, vbias[:, 0:NK], op=ALU.add), reads=[ac, vbias.b()], writes=[ac])
                        dsl = slice((nkb - 1) * 128, nkb * 128)
                        S.op("dve", lambda e: e.tensor_tensor(acc[:, ab, dsl], acc[:, ab, dsl], tribias[:], op=ALU.add),
                             reads=[ac, tribias.b()], writes=[ac])
                        S.op("dve", lambda e: e.tensor_reduce(sm[:, 1:2], A, AX.X, ALU.max), reads=[ac], writes=[sm.b()])
                        S.op("dve", lambda e: e.tensor_tensor(sm[:, 2:3], sm[:, 1:2], sm[:, 0:1], op=ALU.subtract), reads=[sm.b()], writes=[sm.b()])
                        S.op("dve", lambda e: e.tensor_scalar(sm[:, 2:3], sm[:, 2:3], 1.001, 1e-6, op0=ALU.mult, op1=ALU.add),
                             reads=[sm.b()], writes=[sm.b()])
                        S.op("dve", lambda e: e.tensor_scalar(wk[:], powr[:], sm[:, 2:3], None, op0=ALU.mult),
                             reads=[sm.b(), powr.b()], writes=[wk.b()])
                        S.op("dve", lambda e: e.tensor_tensor(sm[:, 3:4], sm[:, 0:1], wk[:, 0:1], op=ALU.add), reads=[sm.b(), wk.b()], writes=[sm.b()])
                        for k in range(NITER):
                            S.op("dve", lambda e: e.tensor_scalar(junkb[:, 0:NK], A, sm[:, 3:4], 0.0, op0=ALU.is_ge, op1=ALU.add,
                                                                  accum_out=sm[:, 4:5]),
                                 reads=[ac, sm.b()], writes=[junkb.b(), sm.b()])
                            S.op("dve", lambda e: e.tensor_scalar(sm[:, 5:6], sm[:, 4:5], 255.5, -0.5, op0=ALU.is_ge, op1=ALU.add),
                                 reads=[sm.b()], writes=[sm.b()])
                            S.op("dve", lambda e, k=k: e.scalar_tensor_tensor(out=sm[:, 3:4], in0=sm[:, 5:6], scalar=wk[:, k:k + 1],
                                                                              in1=sm[:, 3:4], op0=ALU.mult, op1=ALU.add),
                                 reads=[sm.b(), wk.b()], writes=[sm.b()])
                        S.op("dve", lambda e: e.scalar_tensor_tensor(out=sm[:, 6:7], in0=wk[:, NITER - 1:NITER], scalar=-0.5, in1=sm[:, 3:4],
                                                                     op0=ALU.mult, op1=ALU.add),
                             reads=[sm.b(), wk.b()], writes=[sm.b()])
                        S.op("dve", lambda e: e.tensor_scalar(junkb[:, 0:NK], A, sm[:, 6:7], None, op0=ALU.is_ge),
                             reads=[ac, sm.b()], writes=[junkb.b()])
                        yield
                        pm = PB[3]
                        pmv_ = pm[:].bitcast(BF16)
                        for g0 in range(0, nkb, 8):
                            g1 = min(nkb, g0 + 8)
                            for kb in range(g0, g1):
                                S.op("pe", lambda e, kb=kb, g0=g0: e.transpose(pmv_[:, (kb - g0) * 128:(kb - g0 + 1) * 128],
                                                                               junkb[:, kb * 128:(kb + 1) * 128], ident[:]),
                                     reads=[junkb.b(), ident.b()], writes=[pm.b()])
                            S.op("dve", lambda e, g0=g0, g1=g1: e.tensor_copy(maskT[:, ab, g0:g1, :],
                                                                              pmv_[:, 0:(g1 - g0) * 128].rearrange("p (k q) -> p k q", q=128)),
                                 reads=[pm.b()], writes=[maskT.b(ab)])

                    def stage_C(i):
                        ab = i % 2
                        nkb = NPRE + 1 + i
                        kk = 0
                        for kvh in range(2):
                            pO, pD = PB[6], PB[7]

                            def front(kb, kk):
                                pS = PB[4 + kk % 2]
                                es = kk % 4
                                S.op("pe", lambda e, pS=pS, kb=kb, kvh=kvh: e.matmul(
                                    pS[:, 0:512], lhsT=KT[:, kvh, kb * 128:(kb + 1) * 128],
                                    rhs=QT[:, i, kvh * 4:kvh * 4 + 4, :], start=True, stop=True),
                                    reads=[KT.b(kb), QT.b(i)], writes=[pS.b()])
                                S.op("act", lambda e, pS=pS, es=es: e.activation(Eb[:, es, :], pS[:, 0:512], AF.Exp, scale=float(128.0 ** -0.5)),
                                     reads=[pS.b()], writes=[Eb.b(es)])
                                S.op("pool", lambda e, es=es, kb=kb: e.tensor_tensor(
                                    PT[:, es, :].rearrange("p (h q) -> p h q", h=4), Eb[:, es, :].rearrange("p (h q) -> p h q", h=4),
                                    maskT[:, ab, kb:kb + 1, :].to_broadcast([128, 4, 128]), op=ALU.mult),
                                    reads=[Eb.b(es), maskT.b(ab)], writes=[PT.b(es)])

                            def back(kb, kk):
                                es = kk % 4
                                S.op("pe", lambda e, es=es, kb=kb, kvh=kvh: e.matmul(
                                    pO[:, 0:512], lhsT=Vsb[:, kb, kvh * 128:(kvh + 1) * 128], rhs=PT[:, es, :],
                                    start=(kb == 0), stop=(kb == nkb - 1)),
                                    reads=[Vsb.b(kb), PT.b(es)], writes=[pO.b()])
                                S.op("pe", lambda e, es=es, kb=kb: e.matmul(
                                    pD[:, 0:512], lhsT=ones_bf[:], rhs=PT[:, es, :],
                                    start=(kb == 0), stop=(kb == nkb - 1)),
                                    reads=[ones_bf.b(), PT.b(es)], writes=[pD.b()])

                            DEPTH = 2
                            for kb in range(min(DEPTH, nkb)):
                                front(kb, kk + kb)
                            for kb in range(nkb):
                                if kb + DEPTH < nkb:
                                    front(kb + DEPTH, kk + kb + DEPTH)
                                back(kb, kk + kb)
                                if kb % 2 == 1:
                                    yield
                            kk += nkb
                            S.op("act", lambda e: e.activation(ftmp[:, 0, :], pD[:, 0:512], AF.Ln), reads=[pD.b()], writes=[ftmp.b(0)])
                            S.op("act", lambda e: e.activation(ftmp[:, 0, :], ftmp[:, 0, :], AF.Exp, scale=-1.0), reads=[ftmp.b(0)], writes=[ftmp.b(0)])
                            S.op("act", lambda e: e.copy(ftmp[:, 1, :], pO[:, 0:512]), reads=[pO.b()], writes=[ftmp.b(1)])
                            S.op("pool", lambda e: e.tensor_tensor(ftmp[:, 1, :], ftmp[:, 1, :], ftmp[:, 0, :], op=ALU.mult),
                                 reads=[ftmp.b(0), ftmp.b(1)], writes=[ftmp.b(1)])
                            S.op("pool", lambda e, kvh=kvh: e.tensor_tensor(
                                ATT_T[:, i, kvh * 4:kvh * 4 + 4, :], ftmp[:, 1, :].rearrange("p (h q) -> p h q", h=4),
                                AZT[:, i, kvh * 4:kvh * 4 + 4, :], op=ALU.mult),
                                reads=[ftmp.b(1), AZT.b(i)], writes=[ATT_T.b(i)])
                            yield

                    def run_all(g):
                        for _ in g:
                            pass

                    def interleave(*gens):
                        gens = list(gens)
                        while gens:
                            for g in list(gens):
                                try:
                                    next(g)
                                except StopIteration:
                                    gens.remove(g)

                    run_all(stage_A(0))
                    if NOWN > 1:
                        run_all(stage_A(1))
                    run_all(stage_B(0))
                    for i in range(NOWN):
                        gA = stage_A(i + 2) if i + 2 < NOWN else iter(())
                        gB = stage_B(i + 1) if i + 1 < NOWN else iter(())
                        next(gA, None)
                        next(gB, None)
                        interleave(gA, stage_C(i))
                        run_all(gB)

                if NT2 > NOWN:
                    S.barrier()
                    sa = contextlib.ExitStack()
                    with sa:
                        T8 = NOWN
                        SC_ATT = float(128.0 ** -0.5)
                        ikp = sb("ikp", [128, 2, 2048], F32, sa)
                        kT = sb("kT", [64, 64, 128], BF16, sa)
                        Self = sb("Self", [4, 4, 128], F32, sa)
                        Wb16 = sb("Wb16", [128, 4, 16], F32, sa)
                        stmp = sb("stmp", [128, 512], F32, sa)
                        Isc = sb("Isc", [128, 4, 129], F32, sa)
                        cm = sb("cm", [128, 4, 129], F32, sa)
                        Mk = sb("Mk", [128, 4, 129], F32, sa)
                        pt_i = sb("pt_i", [128, 4], I32, sa)
                        pt_t = sb("pt_t", [128, 2, 4], I32, sa)
                        pt2 = sb("pt2", [128, 4, 4], I32, sa)
                        pt_f = sb("pt_f", [128, 2, 4], F32, sa)
                        iot = sb("iot", [128, 17 + 16 + 128], F32, sa)
                        sut = sb("sut", [128, 128], F32, sa)
                        mm = sb("mm", [128, 8], F32, sa)
                        g4 = sb("g4", [4, 4], F32, sa)
                        R8 = sb("R8", [4, 8], F32, sa)
                        LW = sb("LW", [128, 8], F32, sa)
                        tt = sb("tt", [128, 4], F32, sa)
                        cn = sb("cn", [128, 4], F32, sa)
                        sw = sb("sw", [128, 4], F32, sa)
                        incl = sb("incl", [128, 128], F32, sa)
                        rank = sb("rank", [128, 128], F32, sa)
                        hi_ = sb("hi_", [128, 128], F32, sa)
                        lo_ = sb("lo_", [128, 128], F32, sa)
                        GE = sb("GE", [128, 128, 17], BF16, sa)
                        Aoh = sb("Aoh", [128, 128, 16], BF16, sa)
                        Boh = sb("Boh", [128, 128, 16], BF16, sa)
                        Bv3 = sb("Bv3", [128, 128, 3, 16], BF16, sa)
                        dg_sb = sb("dg_sb", [16, 48], F32, sa)
                        r16 = sb("r16", [16, 16], F32, sa)
                        i16 = sb("i16", [16, 16], I32, sa)
                        i128 = sb("i128", [128, 4, 2], I32, sa)
                        kg = sb("kg", [128, 3, 256], BF16, sa)
                        vg = sb("vg", [128, 3, 256], BF16, sa)
                        KgT = sb("KgT", [128, 2, 384], BF16, sa)
                        BIAS = sb("BIAS", [4, 384], F32, sa)
                        lg = sb("lg", [4, 2, 384], F32, sa)
                        pbf = sb("pbf", [4, 2, 384], BF16, sa)
                        s4 = sb("s4", [4, 8], F32, sa)
                        PTs = sb("PTs", [128, 2, 3, 4], BF16, sa)
                        SCRD = Buf("scr_dram")

                        S.dma("sp", lambda e: e.dma_start(out=pt_i[:], in_=ptabT), writes=[pt_i.b()])
                        S.dma("sp", lambda e: e.dma_start(out=iot[:], in_=c_iota.partition_broadcast(128)), writes=[iot.b()])
                        S.dma("sp", lambda e: e.dma_start(out=sut[:], in_=c_sut), writes=[sut.b()])
                        S.op("dve", lambda e: e.tensor_scalar(pt_t[:, 0, :], pt_i[:], 127, None, op0=ALU.bitwise_and), reads=[pt_i.b()], writes=[pt_t.b()])
                        S.op("dve", lambda e: e.tensor_scalar(pt_t[:, 1, :], pt_i[:], 7, None, op0=ALU.arith_shift_right), reads=[pt_i.b()], writes=[pt_t.b()])
                        S.op("dve", lambda e: e.tensor_copy(pt_f[:], pt_t[:]), reads=[pt_t.b()], writes=[pt_f.b()])
                        for rq in range(4):
                            S.op("dve", lambda e, rq=rq: e.tensor_scalar(pt2[:, rq, :], pt_i[:], 4.0, float(rq), op0=ALU.mult, op1=ALU.add),
                                 reads=[pt_i.b()], writes=[pt2.b()])
                        S.op("dve", lambda e: e.memset(Isc[:, :, 128:129], NEG), writes=[Isc.b()])
                        S.op("dve", lambda e: e.memset(BIAS[:], 0.0), writes=[BIAS.b()])
                        S.op("dve", lambda e: e.memset(BIAS[:, 257:384], NEG), reads=[BIAS.b()], writes=[BIAS.b()])
                        S.op("pool", lambda e: e.memset(kg[:, 2, :], 0.0), writes=[kg.b()])
                        S.op("pool", lambda e: e.memset(vg[:, 2, :], 0.0), writes=[vg.b()])
                        for s_ in range(4):
                            S.op("dve", lambda e, s_=s_: e.tensor_copy(Self[:, s_, :], identf[0:4, s_:s_ + 1].to_broadcast([4, 128])),
                                 reads=[identf.b()], writes=[Self.b()])
                        for s_ in range(4):
                            pbw = PB[s_ % 2]
                            S.op("pe", lambda e, s_=s_, pbw=pbw: e.matmul(pbw[:, 0:16], lhsT=Self[:, s_, :], rhs=WI[0:4, T8, :], start=True, stop=True),
                                 reads=[Self.b(), WI.b(T8)], writes=[pbw.b()])
                            evac(Wb16[:, s_, :], pbw[:, 0:16], [pbw.b()], [Wb16.b()])

                        kk = 0
                        for s_ in range(4):
                            for rq in range(4):
                                hb = rq % 2
                                S.dma("pool", lambda e, s_=s_, rq=rq, hb=hb: e.indirect_dma_start(
                                    out=ikp[:, hb, :], out_offset=None, in_=cidx.rearrange("p (h f) -> (p h) f", h=4),
                                    in_offset=bass.IndirectOffsetOnAxis(ap=pt2[:, rq, s_:s_ + 1], axis=0)),
                                    reads=[pt2.b()], writes=[ikp.b(hb)])
                                for r0 in range(0, 32, 4):
                                    pbT = PB[kk % 2]
                                    kk += 1
                                    for r in range(r0, r0 + 4):
                                        S.op("pe", lambda e, r=r, r0=r0, pbT=pbT, hb=hb: e.transpose(pbT[0:64, (r - r0) * 128:(r - r0 + 1) * 128],
                                                                                                    ikp[:, hb, r * 64:(r + 1) * 64], identf[:]),
                                             reads=[ikp.b(hb), identf.b()], writes=[pbT.b()])
                                    evac(kT[:, hb * 32 + r0:hb * 32 + r0 + 4, :], pbT[0:64, 0:512].rearrange("p (r q) -> p r q", q=128),
                                         [pbT.b()], [kT.b(hb)])
                                pbS = PB[2 + (kk % 2)]
                                kk += 1
                                for r in range(32):
                                    S.op("pe", lambda e, r=r, pbS=pbS, s_=s_, hb=hb: e.matmul(pbS[:, r * 16:(r + 1) * 16], lhsT=kT[:, hb * 32 + r, :],
                                                                                             rhs=QIS[:, s_, :], start=True, stop=True),
                                         reads=[kT.b(hb), QIS.b()], writes=[pbS.b()])
                                S.op("dve", lambda e, pbS=pbS, s_=s_: e.scalar_tensor_tensor(
                                    out=stmp[:].rearrange("p (r h) -> p r h", h=16), in0=pbS[:, 0:512].rearrange("p (r h) -> p r h", h=16),
                                    scalar=0.0, in1=Wb16[:, s_:s_ + 1, :].to_broadcast([128, 32, 16]), op0=ALU.max, op1=ALU.mult),
                                    reads=[pbS.b(), Wb16.b()], writes=[stmp.b()])
                                c0 = rq * 32
                                S.op("dve", lambda e, s_=s_, c0=c0: e.tensor_reduce(Isc[:, s_, c0:c0 + 32], stmp[:].rearrange("p (r h) -> p r h", h=16),
                                                                                    AX.X, ALU.add),
                                     reads=[stmp.b()], writes=[Isc.b()])
                            pbn = PB[4]
                            S.op("pe", lambda e, s_=s_: e.matmul(pbn[0:1, 0:16], lhsT=KIS[:, s_:s_ + 1], rhs=QIS[:, s_, :], start=True, stop=True),
                                 reads=[KIS.b(), QIS.b()], writes=[pbn.b()])
                            S.op("dve", lambda e, s_=s_: e.scalar_tensor_tensor(out=stmp[0:1, 0:16], in0=pbn[0:1, 0:16], scalar=0.0, in1=Wb16[0:1, s_, :],
                                                                                op0=ALU.max, op1=ALU.mult),
                                 reads=[pbn.b(), Wb16.b(), stmp.b()], writes=[stmp.b()])
                            S.op("dve", lambda e, s_=s_: e.tensor_reduce(Isc[0:1, s_, 128:129], stmp[0:1, 0:16], AX.X, ALU.add),
                                 reads=[stmp.b(), Isc.b()], writes=[Isc.b()])

                        S.op("dve", lambda e: e.tensor_reduce(mm[:, 0:4], Isc[:, :, 0:128], AX.X, ALU.min), reads=[Isc.b()], writes=[mm.b()])
                        S.op("dve", lambda e: e.tensor_reduce(mm[:, 4:8], Isc[:], AX.X, ALU.max), reads=[Isc.b(), mm.b()], writes=[mm.b()])
                        for w_, op_ in ((0, ALU.min), (1, ALU.max)):
                            pbm = PB[w_]
                            S.op("pe", lambda e, w_=w_, pbm=pbm: e.matmul(pbm[0:4, 0:128], lhsT=mm[:, w_ * 4:(w_ + 1) * 4], rhs=identf[:], start=True, stop=True),
                                 reads=[mm.b(), identf.b()], writes=[pbm.b()])
                            S.op("dve", lambda e, w_=w_, pbm=pbm, op_=op_: e.tensor_reduce(g4[:, w_:w_ + 1], pbm[0:4, 0:128], AX.X, op_),
                                 reads=[pbm.b(), g4.b()], writes=[g4.b()])
                        S.op("dve", lambda e: e.tensor_tensor(g4[:, 2:3], g4[:, 1:2], g4[:, 0:1], op=ALU.subtract), reads=[g4.b()], writes=[g4.b()])
                        S.op("dve", lambda e: e.tensor_scalar(g4[:, 2:3], g4[:, 2:3], 1.001, 1e-6, op0=ALU.mult, op1=ALU.add), reads=[g4.b()], writes=[g4.b()])
                        S.op("dve", lambda e: e.tensor_scalar(R8[:, 0:4], identf[0:4, 0:4], g4[:, 0:1], None, op0=ALU.mult), reads=[g4.b(), identf.b()], writes=[R8.b()])
                        S.op("dve", lambda e: e.tensor_scalar(R8[:, 4:8], identf[0:4, 0:4], g4[:, 2:3], None, op0=ALU.mult), reads=[g4.b(), identf.b(), R8.b()], writes=[R8.b()])
                        pbm = PB[2]
                        S.op("pe", lambda e: e.matmul(pbm[:, 0:8], lhsT=ones4f[:], rhs=R8[:], start=True, stop=True), reads=[ones4f.b(), R8.b()], writes=[pbm.b()])
                        S.op("dve", lambda e: e.tensor_copy(LW[:], pbm[:, 0:8]), reads=[pbm.b()], writes=[LW.b()])
                        S.op("dve", lambda e: e.scalar_tensor_tensor(out=tt[:], in0=LW[:, 4:8], scalar=0.5, in1=LW[:, 0:4], op0=ALU.mult, op1=ALU.add),
                             reads=[LW.b()], writes=[tt.b()])
                        for k in range(NITER):
                            S.op("dve", lambda e: e.tensor_tensor(cm[:], Isc[:], tt[:].unsqueeze(2).to_broadcast([128, 4, 129]), op=ALU.is_ge),
                                 reads=[Isc.b(), tt.b()], writes=[cm.b()])
                            S.op("dve", lambda e: e.tensor_reduce(cn[:], cm[:], AX.X, ALU.add), reads=[cm.b()], writes=[cn.b()])
                            pbc = PB[3 + (k % 2)]
                            S.op("pe", lambda e, pbc=pbc: e.matmul(pbc[:, 0:4], lhsT=identf_ones[:], rhs=cn[:], start=True, stop=True),
                                 reads=[identf_ones.b(), cn.b()], writes=[pbc.b()])
                            S.op("dve", lambda e, pbc=pbc: e.tensor_scalar(sw[:], pbc[:, 0:4], 255.5, -0.5, op0=ALU.is_ge, op1=ALU.add),
                                 reads=[pbc.b()], writes=[sw.b()])
                            S.op("dve", lambda e: e.tensor_tensor(sw[:], sw[:], LW[:, 4:8], op=ALU.mult), reads=[sw.b(), LW.b()], writes=[sw.b()])
                            S.op("dve", lambda e, k=k: e.scalar_tensor_tensor(out=tt[:], in0=sw[:], scalar=float(0.5 ** (k + 1)), in1=tt[:],
                                                                              op0=ALU.mult, op1=ALU.add),
                                 reads=[sw.b(), tt.b()], writes=[tt.b()])
                        S.op("dve", lambda e: e.scalar_tensor_tensor(out=tt[:], in0=LW[:, 4:8], scalar=float(-(0.5 ** (NITER + 1))), in1=tt[:],
                                                                     op0=ALU.mult, op1=ALU.add),
                             reads=[LW.b(), tt.b()], writes=[tt.b()])
                        S.op("dve", lambda e: e.tensor_tensor(Mk[:], Isc[:], tt[:].unsqueeze(2).to_broadcast([128, 4, 129]), op=ALU.is_ge),
                             reads=[Isc.b(), tt.b()], writes=[Mk.b()])

                        i17 = iot[:, 0:17]
                        i16v = iot[:, 17:33]
                        icol = iot[:, 33:161]
                        def sel(s_):
                            Ms = Mk[:, s_, 0:128]
                            S.op("dve", lambda e, Ms=Ms: e.tensor_tensor_scan(incl[:], identf_ones[:], Ms, 0.0, op0=ALU.mult, op1=ALU.add),
                                 reads=[identf_ones.b(), Mk.b()], writes=[incl.b()])
                            pbo_ = PB[0]
                            S.op("pe", lambda e: e.matmul(pbo_[:, 0:1], lhsT=sut[:], rhs=incl[:, 127:128], start=True, stop=True),
                                 reads=[sut.b(), incl.b()], writes=[pbo_.b()])
                            S.op("dve", lambda e, Ms=Ms: e.tensor_tensor(rank[:], incl[:], Ms, op=ALU.subtract), reads=[incl.b(), Mk.b()], writes=[rank.b()])
                            S.op("dve", lambda e: e.tensor_copy(cn[:, 0:1], pbo_[:, 0:1]), reads=[pbo_.b(), cn.b()], writes=[cn.b()])
                            S.op("dve", lambda e: e.tensor_scalar(rank[:], rank[:], cn[:, 0:1], 1.0, op0=ALU.add, op1=ALU.add) if False else
                                 e.tensor_scalar(rank[:], rank[:], cn[:, 0:1], None, op0=ALU.add),
                                 reads=[rank.b(), cn.b()], writes=[rank.b()])
                            S.op("dve", lambda e, Ms=Ms: e.scalar_tensor_tensor(out=rank[:], in0=rank[:], scalar=1.0, in1=Ms, op0=ALU.add, op1=ALU.mult),
                                 reads=[rank.b(), Mk.b()], writes=[rank.b()])
                            S.op("dve", lambda e: e.tensor_scalar(rank[:], rank[:], -1.0, None, op0=ALU.add), reads=[rank.b()], writes=[rank.b()])
                            S.op("dve", lambda e: e.tensor_tensor(GE[:], rank[:].unsqueeze(2).to_broadcast([128, 128, 17]),
                                                                  i17.unsqueeze(1).to_broadcast([128, 128, 17]), op=ALU.is_ge),
                                 reads=[rank.b(), iot.b()], writes=[GE.b()])
                            S.op("pool", lambda e: e.tensor_tensor(Aoh[:], GE[:, :, 0:16], GE[:, :, 1:17], op=ALU.subtract), reads=[GE.b()], writes=[Aoh.b()])
                            S.op("dve", lambda e: e.tensor_reduce(hi_[:], GE[:, :, 1:17], AX.X, ALU.add), reads=[GE.b()], writes=[hi_.b()])
                            S.op("dve", lambda e: e.scalar_tensor_tensor(out=lo_[:], in0=hi_[:], scalar=-16.0, in1=rank[:], op0=ALU.mult, op1=ALU.add),
                                 reads=[hi_.b(), rank.b()], writes=[lo_.b()])
                            S.op("dve", lambda e: e.tensor_tensor(Boh[:], lo_[:].unsqueeze(2).to_broadcast([128, 128, 16]),
                                                                  i16v.unsqueeze(1).to_broadcast([128, 128, 16]), op=ALU.is_equal),
                                 reads=[lo_.b(), iot.b()], writes=[Boh.b()])
                            S.op("pool", lambda e: e.tensor_tensor(Bv3[:, :, 0, :], Boh[:], icol.unsqueeze(2).to_broadcast([128, 128, 16]), op=ALU.mult),
                                 reads=[Boh.b(), iot.b()], writes=[Bv3.b()])
                            S.op("dve", lambda e, s_=s_: e.tensor_scalar(Bv3[:, :, 1, :], Boh[:], pt_f[:, 0, s_:s_ + 1], None, op0=ALU.mult),
                                 reads=[Boh.b(), pt_f.b(), Bv3.b()], writes=[Bv3.b()])
                            S.op("dve", lambda e, s_=s_: e.tensor_scalar(Bv3[:, :, 2, :], Boh[:], pt_f[:, 1, s_:s_ + 1], None, op0=ALU.mult),
                                 reads=[Boh.b(), pt_f.b(), Bv3.b()], writes=[Bv3.b()])
                            pbi = PB[1]
                            for col in range(128):
                                S.op("pe", lambda e, col=col: e.matmul(pbi[0:16, 0:48], lhsT=Aoh[:, col, :], rhs=Bv3[:, col, :, :].rearrange("p k s -> p (k s)"),
                                                                       start=(col == 0), stop=(col == 127)),
                                     reads=[Aoh.b(), Bv3.b()], writes=[pbi.b()])
                            S.op("dve", lambda e: e.tensor_copy(dg_sb[:], pbi[0:16, 0:48]), reads=[pbi.b()], writes=[dg_sb.b()])
                            S.op("dve", lambda e: e.scalar_tensor_tensor(out=r16[:], in0=dg_sb[:, 32:48], scalar=128.0, in1=dg_sb[:, 16:32], op0=ALU.mult, op1=ALU.add),
                                 reads=[dg_sb.b()], writes=[r16.b()])
                            S.op("dve", lambda e: e.scalar_tensor_tensor(out=r16[:], in0=r16[:], scalar=128.0, in1=dg_sb[:, 0:16], op0=ALU.mult, op1=ALU.add),
                                 reads=[dg_sb.b(), r16.b()], writes=[r16.b()])
                            S.op("dve", lambda e: e.tensor_copy(i16[:], r16[:]), reads=[r16.b()], writes=[i16.b()])
                            S.dma("sp", lambda e, s_=s_: e.dma_start(out=scr[s_:s_ + 1, :].rearrange("o (a b) -> (o a) b", a=16), in_=i16[:]),
                                  reads=[i16.b()], writes=[SCRD])
                            S.dma("sp", lambda e, s_=s_: e.dma_start(out=i128[:, s_, :], in_=scr[s_:s_ + 1, :].rearrange("o (h p) -> p (o h)", p=128),
                                                                     allow_slow_non_contiguous=True),
                                  reads=[SCRD], writes=[i128.b(s_)])
                        def gath(s_):
                            for hf in range(2):
                                S.dma("pool", lambda e, s_=s_, hf=hf: e.indirect_dma_start(
                                    out=kg[:, hf, :], out_offset=None, in_=ck,
                                    in_offset=bass.IndirectOffsetOnAxis(ap=i128[:, s_, hf:hf + 1], axis=0)),
                                    reads=[i128.b(s_)], writes=[kg.b()])
                                S.dma("pool", lambda e, s_=s_, hf=hf: e.indirect_dma_start(
                                    out=vg[:, hf, :], out_offset=None, in_=cv,
                                    in_offset=bass.IndirectOffsetOnAxis(ap=i128[:, s_, hf:hf + 1], axis=0)),
                                    reads=[i128.b(s_)], writes=[vg.b()])
                            S.dma("pool", lambda e, s_=s_: e.dma_start(out=kg[0:1, 2, :], in_=ks_out[s_:s_ + 1, :]), reads=[KSD], writes=[kg.b()])
                            S.dma("pool", lambda e, s_=s_: e.dma_start(out=vg[0:1, 2, :], in_=vs_out[s_:s_ + 1, :]), reads=[VSD], writes=[vg.b()])

                        def att(s_):
                            pbk = PB[2]
                            pbkv = pbk[:].bitcast(BF16)
                            for kvh in range(2):
                                for tl in range(3):
                                    S.op("pe", lambda e, kvh=kvh, tl=tl: e.transpose(pbkv[:, (kvh * 3 + tl) * 128:(kvh * 3 + tl + 1) * 128],
                                                                                     kg[:, tl, kvh * 128:(kvh + 1) * 128], ident[:]),
                                         reads=[kg.b(), ident.b()], writes=[pbk.b()])
                            evac(KgT[:].rearrange("p k n -> p (k n)"), pbkv[:, 0:768], [pbk.b()], [KgT.b()])
                            pb4 = PB[3]
                            S.op("pe", lambda e, s_=s_: e.matmul(pb4[0:4, 0:1], lhsT=identf_ones[:, 0:4], rhs=Mk[:, s_, 128:129], start=True, stop=True),
                                 reads=[identf_ones.b(), Mk.b()], writes=[pb4.b()])
                            S.op("dve", lambda e: e.tensor_scalar(BIAS[:, 255:256], pb4[0:4, 0:1], NEG, None, op0=ALU.mult), reads=[pb4.b(), BIAS.b()], writes=[BIAS.b()])
                            S.op("dve", lambda e: e.tensor_scalar(BIAS[:, 256:257], pb4[0:4, 0:1], -1.0, -NEG, op0=ALU.add, op1=ALU.mult),
                                 reads=[pb4.b(), BIAS.b()], writes=[BIAS.b()])
                            for kvh in range(2):
                                pbl = PB[4 + kvh]
                                S.op("pe", lambda e, kvh=kvh, s_=s_, pbl=pbl: e.matmul(pbl[0:4, 0:384], lhsT=QT[:, T8, kvh * 4:kvh * 4 + 4, s_], rhs=KgT[:, kvh, :],
                                                                                      start=True, stop=True),
                                     reads=[QT.b(T8), KgT.b()], writes=[pbl.b()])
                                S.op("dve", lambda e, kvh=kvh, pbl=pbl: e.scalar_tensor_tensor(out=lg[:, kvh, :], in0=pbl[0:4, 0:384], scalar=SC_ATT, in1=BIAS[:],
                                                                                               op0=ALU.mult, op1=ALU.add),
                                     reads=[pbl.b(), BIAS.b()], writes=[lg.b()])
                                S.op("dve", lambda e, kvh=kvh: e.tensor_reduce(s4[:, kvh:kvh + 1], lg[:, kvh, :], AX.X, ALU.max), reads=[lg.b()], writes=[s4.b()])
                                S.op("dve", lambda e, kvh=kvh: e.tensor_scalar(s4[:, 2 + kvh:3 + kvh], s4[:, kvh:kvh + 1], -1.0, None, op0=ALU.mult),
                                     reads=[s4.b()], writes=[s4.b()])
                                S.op("act", lambda e, kvh=kvh: e.activation(lg[:, kvh, :], lg[:, kvh, :], AF.Exp, bias=s4[:, 2 + kvh:3 + kvh],
                                                                            accum_out=s4[:, 4 + kvh:5 + kvh]),
                                     reads=[lg.b(), s4.b()], writes=[lg.b(), s4.b()])
                                S.op("dve", lambda e, kvh=kvh: e.reciprocal(s4[:, 6 + kvh:7 + kvh], s4[:, 4 + kvh:5 + kvh]), reads=[s4.b()], writes=[s4.b()])
                                S.op("dve", lambda e, kvh=kvh: e.tensor_scalar(pbf[:, kvh, :], lg[:, kvh, :], s4[:, 6 + kvh:7 + kvh], None, op0=ALU.mult),
                                     reads=[lg.b(), s4.b()], writes=[pbf.b()])
                                pbp = PB[6]
                                pbpv = pbp[:].bitcast(BF16)
                                for tl in range(3):
                                    S.op("pe", lambda e, kvh=kvh, tl=tl: e.transpose(pbpv[:, (kvh * 3 + tl) * 4:(kvh * 3 + tl) * 4 + 4],
                                                                                     pbf[:, kvh, tl * 128:(tl + 1) * 128], ident[0:4, 0:4]),
                                         reads=[pbf.b(), ident.b()], writes=[pbp.b()])
                                evac(PTs[:, kvh, :, :].rearrange("p t h -> p (t h)"), pbpv[:, kvh * 12:kvh * 12 + 12], [pbp.b()], [PTs.b()])
                                pbO = PB[7]
                                for tl in range(3):
                                    S.op("pe", lambda e, kvh=kvh, tl=tl: e.matmul(pbO[:, kvh * 4:kvh * 4 + 4], lhsT=vg[:, tl, kvh * 128:(kvh + 1) * 128],
                                                                                  rhs=PTs[:, kvh, tl, :], start=(tl == 0), stop=(tl == 2)),
                                         reads=[vg.b(), PTs.b()], writes=[pbO.b()])
                                S.op("dve", lambda e, kvh=kvh, s_=s_: e.tensor_tensor(ATT_T[:, T8, kvh * 4:kvh * 4 + 4, s_], pbO[:, kvh * 4:kvh * 4 + 4],
                                                                                      AZT[:, T8, kvh * 4:kvh * 4 + 4, s_], op=ALU.mult),
                                     reads=[pbO.b(), AZT.b(T8), ATT_T.b(T8)], writes=[ATT_T.b(T8)])
                        sel(0)
                        gath(0)
                        for s_ in range(4):
                            if s_ + 1 < 4:
                                sel(s_ + 1)
                            att(s_)
                            if s_ + 1 < 4:
                                gath(s_ + 1)
        S.barrier()
        ML_T = sb("ML_T", [128, NT2, 8, 128], BF16)
        if NT2 > NOWN:
            S.op("pool", lambda e: e.memset(ML_T[:, NOWN, :, :], 0.0), writes=[ML_T.b(NOWN)])
        p2b = contextlib.ExitStack()
        with p2b:
            MQT = sb("MQT", [128, NT2, 4, 128], BF16, p2b)
            MKT = sb("MKT", [128, NT2, 4, 128], BF16, p2b)
            MK = sb("MK", [128, NT2, 512], BF16, p2b)
            MV = sb("MV", [128, NT2, 1024], BF16, p2b)
            MOZ = sb("MOZ", [128, NT2, 1024], BF16, p2b)
            GI = sb("GI", [4, TOK2], F32, p2b)
            GF = sb("GF", [4, TOK2], F32, p2b)
            mlnw_bc = sb("mlnw_bc", [128, 1024], F32, p2b)
            MQS = sb("MQS", [4, 512], BF16, p2b)
            GS = sb("GS", [4, 8], F32, p2b)
            S.dma("sp", lambda e: e.dma_start(out=mlnw_bc[:], in_=mlnorm_w.partition_broadcast(128)), writes=[mlnw_bc.b()])
            pj = contextlib.ExitStack()
            with pj:
                hT2 = sb("hT2b", [128, NCH, TOK2], BF16, pj)
                build_hT2(pj, hT2)
                S.barrier()
                Wb = sb("Wbb", [128, 2, NCH, 512], BF16, pj)
                ztmp = sb("ztmp", [128, 2, 512], BF16, pj)
                P = Proj(hT2, Wb, [PB[2], PB[3], PB[4], PB[5]])

                def to_tiles4(dst):
                    def c(j, t0, n, pb):
                        evac(dst[:, t0 // 128:(t0 + n) // 128, j, :], tview(pb[:, 0:n], n),
                             [pb.b()], [dst.b(tt) for tt in range(t0 // 128, (t0 + n) // 128)])
                    return c

                P.feat(C_MQ, 512, to_tiles4(MQT))
                P.feat(C_MK, 512, to_tiles4(MKT))
                P.tok(C_MK, 512, lambda t, pb: evac(MK[:, t, :], pb[:, 0:512], [pb.b()], [MK.b(t)]))
                for hf in range(2):
                    P.tok(C_MV + hf * 512, 512, lambda t, pb, hf=hf: evac(MV[:, t, hf * 512:(hf + 1) * 512], pb[:, 0:512],
                                                                           [pb.b()], [MV.b(t)]))
                for hf in range(2):
                    P.tok(C_MO + hf * 512, 512, lambda t, pb, hf=hf: evac(MOZ[:, t, hf * 512:(hf + 1) * 512], pb[:, 0:512],
                                                                           [pb.b()], [MOZ.b(t)], func=AF.Sigmoid))
                for hf in range(2):
                    def cz(t, pb, hf=hf):
                        zs = t % 2
                        S.op("act", lambda e: e.activation(ztmp[:, zs, :], pb[:, 0:512], AF.Silu), reads=[pb.b()], writes=[ztmp.b(zs)])
                        S.op("pool", lambda e: e.tensor_tensor(MOZ[:, t, hf * 512:(hf + 1) * 512], MOZ[:, t, hf * 512:(hf + 1) * 512],
                                                               ztmp[:, zs, :], op=ALU.mult),
                             reads=[ztmp.b(zs), MOZ.b(t)], writes=[MOZ.b(t)])
                    P.tok(C_MZ + hf * 512, 512, cz)

                def cg(j, t0, n, pb):
                    dst = GI if j == 0 else GF
                    evac(dst[:, t0:t0 + n], pb[0:4, 0:n], [pb.b()], [dst.b()])
                P.feat(C_MI, 8, cg, psub=4)
                if NT2 > NOWN:
                    P.tok(C_MQ, 512, lambda t, pb: evac(MQS[:], pb[0:4, 0:512], [pb.b()], [MQS.b()]), tiles=[NOWN])
                    P.tok(C_MI, 8, lambda t, pb: evac(GS[:], pb[0:4, 0:8], [pb.b()], [GS.b()]), tiles=[NOWN])

            S.barrier()
            ms = contextlib.ExitStack()
            with ms:
                G2s = [make_gate_tiles(ms, "g2a_"), make_gate_tiles(ms, "g2b_")]
                Kp2 = sb("Kp2", [128, 4, 128], BF16, ms)
                Vaug2 = sb("Vaug2", [128, 2, 4, 257], BF16, ms)
                Cbf = sb("Cbf", [128, 4, 257], BF16, ms)
                Sm = sb("Sm", [128, 4, 128], BF16, ms)
                hh = sb("hh", [128, 4, 256], F32, ms)
                mixml = sb("mixml", [128, 4, 256], BF16, ms)
                sq = sb("sq", [128, 4, 8], F32, ms)
                junkh = sb("junkh", [128, 4, 256], BF16, ms)
                S.op("dve", lambda e: e.memset(Vaug2[:], 1.0), writes=[Vaug2.b((g, h)) for g in range(2) for h in range(4)])

                def gates(t):
                    tsl = slice(t * 128, (t + 1) * 128)
                    G2 = G2s[t % 2]
                    gate_rows(GI[:, tsl], GF[:, tsl], GI.b(), GF.b(), None, None, G2)
                    gate_cols(G2, PB[6], True, c0=448)
                    S.op("pool", lambda e, t=t: e.tensor_copy(Vaug2[:, t % 2, :, 0:256], MV[:, t, :].rearrange("p (h v) -> p h v", h=4)),
                         reads=[MV.b(t)], writes=[Vaug2.b((t % 2, h)) for h in range(4)])

                def head(t, h):
                    G2 = G2s[t % 2]
                    g = t % 2
                    bA, bB = PB[2 * h], PB[2 * h + 1]
                    pS = bA[:, 0:128]
                    pC = bA[:, 128:385]
                    pH = bB[:, 0:257]
                    pTv = bB[:].bitcast(BF16)[:, 768:1024]
                    sqh = sq[:, h, :]
                    S.op("dve", lambda e: e.tensor_scalar(Cst[:, h, :], Cst[:, h, :], G2["abc"][:, h:h + 1], None, op0=ALU.mult),
                         reads=[Cst.b(h), G2["abc"].b()], writes=[Cst.b(h)])
                    S.op("act", lambda e: e.copy(Cbf[:, h, :], Cst[:, h, :]), reads=[Cst.b(h)], writes=[Cbf.b(h)])
                    S.op("pe", lambda e: e.matmul(pS, lhsT=MKT[:, t, h, :], rhs=MQT[:, t, h, :], start=True, stop=True),
                         reads=[MKT.b(t), MQT.b(t)], writes=[bA.b()])
                    S.op("dve", lambda e: e.scalar_tensor_tensor(out=Sm[:, h, :], in0=pS, scalar=G2["uT"][:, h:h + 1], in1=tri[:],
                                                                 op0=ALU.mult, op1=ALU.mult),
                         reads=[bA.b(), G2["uT"].b(), tri.b()], writes=[Sm.b(h)])
                    S.op("pe", lambda e: e.matmul(pH, lhsT=Sm[:, h, :], rhs=Vaug2[:, g, h, :], start=True, stop=False),
                         reads=[Sm.b(h), Vaug2.b((g, h))], writes=[bB.b()])
                    S.op("pe", lambda e: e.matmul(pH, lhsT=MQT[:, t, h, :], rhs=Cbf[:, h, :], start=False, stop=True),
                         reads=[MQT.b(t), Cbf.b(h)], writes=[bB.b()])
                    S.op("act", lambda e: e.activation(Kp2[:, h, :], MK[:, t, h * 128:(h + 1) * 128], AF.Copy,
                                                       scale=G2["uT"][:, h:h + 1]),
                         reads=[MK.b(t), G2["uT"].b()], writes=[Kp2.b(h)])
                    S.op("pe", lambda e: e.matmul(pC, lhsT=Kp2[:, h, :], rhs=Vaug2[:, g, h, :], start=True, stop=True),
                         reads=[Kp2.b(h), Vaug2.b((g, h))], writes=[bA.b()])
                    S.op("dve", lambda e: e.tensor_tensor(Cst[:, h, :], Cst[:, h, :], pC, op=ALU.add),
                         reads=[Cst.b(h), bA.b()], writes=[Cst.b(h)])
                    S.op("act", lambda e: e.activation(sqh[:, 0:1], pH[:, 256:257], AF.Abs),
                         reads=[bB.b()], writes=[sq.b(h)])
                    S.op("dve", lambda e: e.tensor_tensor(sqh[:, 0:1], sqh[:, 0:1], G2["clT"][:, h:h + 1], op=ALU.max),
                         reads=[sq.b(h), G2["clT"].b()], writes=[sq.b(h)])
                    S.op("dve", lambda e: e.reciprocal(sqh[:, 1:2], sqh[:, 0:1]), reads=[sq.b(h)], writes=[sq.b(h)])
                    S.op("dve", lambda e: e.tensor_scalar(hh[:, h, :], pH[:, 0:256], sqh[:, 1:2], None, op0=ALU.mult),
                         reads=[bB.b(), sq.b(h)], writes=[hh.b(h)])
                    S.op("act", lambda e: e.activation(junkh[:, h, :], hh[:, h, :], AF.Square, accum_out=sqh[:, 2:3]),
                         reads=[hh.b(h)], writes=[junkh.b(h), sq.b(h)])
                    S.op("act", lambda e: e.activation(sqh[:, 3:4], sqh[:, 2:3], AF.Sqrt, scale=1.0 / 256, bias=epsb[:, 0:1]),
                         reads=[sq.b(h), epsb.b()], writes=[sq.b(h)])
                    S.op("dve", lambda e: e.reciprocal(sqh[:, 4:5], sqh[:, 3:4]), reads=[sq.b(h)], writes=[sq.b(h)])
                    S.op("dve", lambda e: e.scalar_tensor_tensor(out=hh[:, h, :], in0=hh[:, h, :], scalar=sqh[:, 4:5],
                                                                 in1=mlnw_bc[:, h * 256:(h + 1) * 256],
                                                                 op0=ALU.mult, op1=ALU.mult),
                         reads=[hh.b(h), sq.b(h), mlnw_bc.b()], writes=[hh.b(h)])
                    S.op("pool", lambda e: e.tensor_tensor(mixml[:, h, :], hh[:, h, :], MOZ[:, t, h * 256:(h + 1) * 256], op=ALU.mult),
                         reads=[hh.b(h), MOZ.b(t)], writes=[mixml.b(h)])
                    for half in range(2):
                        S.op("pe", lambda e, half=half: e.transpose(pTv[:, half * 128:(half + 1) * 128],
                                                                    mixml[:, h, half * 128:(half + 1) * 128], ident[:]),
                             reads=[mixml.b(h), ident.b()], writes=[bB.b()])
                    evac(ML_T[:, t, h * 2:h * 2 + 2, :], pTv.rearrange("p (k q) -> p k q", q=128), [bB.b()], [ML_T.b(t)])

                gates(0)
                for t in range(NOWN):
                    lists = [S.capture(lambda h=h: head(t, h)) for h in range(4)]
                    if t + 1 < NOWN:
                        lists.append(S.capture(lambda: gates(t + 1)))
                    S.replay_rr(lists, [1, 1, 1, 1, 2][:len(lists)])

            if NT2 > NOWN:
                S.barrier()
                ss = contextlib.ExitStack()
                with ss:
                    T8 = NOWN
                    CS = sb("CS", [128, 32, 128], F32, ss)
                    NS = sb("NS", [4, 512], F32, ss)
                    MS = sb("MS", [4, 4], F32, ss)
                    BI4 = sb("BI4", [4, 4], F32, ss)
                    BF4 = sb("BF4", [4, 4], F32, ss)
                    sc = sb("sc", [4, 16, 4], F32, ss)
                    SC = sb("SC", [4, 5, 4], F32, ss)
                    Rr = sb("Rr", [4, 5, 4, 4], F32, ss)
                    SCB = sb("SCB", [128, 5, 16], F32, ss)
                    prod = sb("prod", [4, 512], F32, ss)
                    Sel = sb("Sel", [4, 4, 128], BF16, ss)
                    QB = sb("QB", [128, 4, 512], F32, ss)
                    KB = sb("KB", [128, 4, 512], F32, ss)
                    CQ = sb("CQ", [128, 32], F32, ss)
                    ctmp = sb("ctmp", [128, 8, 128], F32, ss)
                    VT = sb("VT", [128, 8, 4], F32, ss)
                    MZT = sb("MZT", [128, 8, 4], F32, ss)
                    mlnwT = sb("mlnwT", [128, 8], F32, ss)
                    HS = sb("HS", [128, 4, 4, 2], F32, ss)
                    H2 = sb("H2", [128, 4, 4, 2], F32, ss)
                    DV = sb("DV", [128, 4, 4, 2], F32, ss)
                    tot = sb("tot", [128, 16], F32, ss)
                    NSn = sb("NSn", [4, 512], F32, ss)

                    S.dma("sp", lambda e: e.dma_start(out=CS[:], in_=st_C.rearrange("s h (a p) d -> p (s h a) d", p=128)), writes=[CS.b()])
                    S.dma("sp", lambda e: e.dma_start(out=NS[:], in_=st_n), writes=[NS.b()])
                    S.dma("sp", lambda e: e.dma_start(out=MS[:], in_=st_m), writes=[MS.b()])
                    S.dma("sp", lambda e: e.dma_start(out=BI4[:], in_=b_ig.rearrange("h o -> o h").partition_broadcast(4)), writes=[BI4.b()])
                    S.dma("sp", lambda e: e.dma_start(out=BF4[:], in_=b_fg.rearrange("h o -> o h").partition_broadcast(4)), writes=[BF4.b()])

                    def so(fn, r, w):
                        S.op("dve", fn, reads=r, writes=w)
                    scb = sc.b()
                    so(lambda e: e.tensor_tensor(sc[:, 0, :], GS[:, 0:4], BI4[:], op=ALU.add), [GS.b(), BI4.b()], [scb])
                    so(lambda e: e.tensor_tensor(sc[:, 1, :], GS[:, 4:8], BF4[:], op=ALU.add), [GS.b(), BF4.b()], [scb])
                    S.op("act", lambda e: e.activation(sc[:, 1, :], sc[:, 1, :], AF.Exp, scale=-1.0), reads=[scb], writes=[scb])
                    S.op("act", lambda e: e.activation(sc[:, 1, :], sc[:, 1, :], AF.Ln, bias=one4[:, 0:1]), reads=[scb, one4.b()], writes=[scb])
                    so(lambda e: e.tensor_tensor(sc[:, 2, :], MS[:], sc[:, 1, :], op=ALU.subtract), [scb, MS.b()], [scb])
                    so(lambda e: e.tensor_tensor(sc[:, 3, :], sc[:, 2, :], sc[:, 0, :], op=ALU.max), [scb], [scb])
                    so(lambda e: e.tensor_tensor(sc[:, 4, :], sc[:, 2, :], sc[:, 3, :], op=ALU.subtract), [scb], [scb])
                    so(lambda e: e.tensor_tensor(sc[:, 5, :], sc[:, 0, :], sc[:, 3, :], op=ALU.subtract), [scb], [scb])
                    S.op("act", lambda e: e.activation(sc[:, 4:6, :], sc[:, 4:6, :], AF.Exp), reads=[scb], writes=[scb])
                    S.op("act", lambda e: e.activation(sc[:, 11, :], sc[:, 3, :], AF.Exp, scale=-1.0), reads=[scb], writes=[scb])
                    so(lambda e: e.tensor_tensor(prod[:], MQS[:], MK[0:4, T8, :], op=ALU.mult), [MQS.b(), MK.b(T8)], [prod.b()])
                    so(lambda e: e.tensor_reduce(sc[:, 6, :], prod[:].rearrange("s (h d) -> s h d", h=4), AX.X, ALU.add), [prod.b()], [scb])
                    so(lambda e: e.tensor_tensor(prod[:], MQS[:], NS[:], op=ALU.mult), [MQS.b(), NS.b(), scb], [prod.b()])
                    so(lambda e: e.tensor_reduce(sc[:, 7, :], prod[:].rearrange("s (h d) -> s h d", h=4), AX.X, ALU.add), [prod.b()], [scb])
                    so(lambda e: e.scalar_tensor_tensor(out=sc[:, 8, :], in0=sc[:, 6, :], scalar=float(128.0 ** -0.5), in1=sc[:, 5, :],
                                                        op0=ALU.mult, op1=ALU.mult), [scb], [scb])
                    so(lambda e: e.tensor_tensor(sc[:, 9, :], sc[:, 4, :], sc[:, 7, :], op=ALU.mult), [scb], [scb])
                    so(lambda e: e.tensor_tensor(sc[:, 9, :], sc[:, 9, :], sc[:, 8, :], op=ALU.add), [scb], [scb])
                    so(lambda e: e.tensor_scalar(sc[:, 10, :], sc[:, 9, :], -1.0, None, op0=ALU.mult), [scb], [scb])
                    so(lambda e: e.tensor_tensor(sc[:, 10, :], sc[:, 10, :], sc[:, 9, :], op=ALU.max), [scb], [scb])
                    so(lambda e: e.tensor_tensor(sc[:, 10, :], sc[:, 10, :], sc[:, 11, :], op=ALU.max), [scb], [scb])
                    so(lambda e: e.reciprocal(sc[:, 12, :], sc[:, 10, :]), [scb], [scb])
                    so(lambda e: e.tensor_copy(SC[:, 0, :], sc[:, 4, :]), [scb], [SC.b()])
                    so(lambda e: e.tensor_scalar(SC[:, 1, :], sc[:, 5, :], float(128.0 ** -0.5), None, op0=ALU.mult), [scb], [SC.b()])
                    so(lambda e: e.tensor_tensor(SC[:, 2, :], sc[:, 4, :], sc[:, 12, :], op=ALU.mult), [scb], [SC.b()])
                    so(lambda e: e.tensor_tensor(SC[:, 3, :], sc[:, 8, :], sc[:, 12, :], op=ALU.mult), [scb], [SC.b()])
                    so(lambda e: e.tensor_copy(SC[:, 4, :], sc[:, 3, :]), [scb], [SC.b()])
                    S.dma("sp", lambda e: e.dma_start(out=ms_out, in_=SC[:, 4, :]), reads=[SC.b()])
                    so(lambda e: e.tensor_tensor(NSn[:].rearrange("s (h d) -> s h d", h=4), NS[:].rearrange("s (h d) -> s h d", h=4),
                                                 SC[:, 0, :].unsqueeze(2).to_broadcast([4, 4, 128]), op=ALU.mult), [NS.b(), SC.b()], [NSn.b()])
                    so(lambda e: e.tensor_tensor(prod[:].rearrange("s (h d) -> s h d", h=4), MK[0:4, T8, :].rearrange("s (h d) -> s h d", h=4),
                                                 SC[:, 1, :].unsqueeze(2).to_broadcast([4, 4, 128]), op=ALU.mult), [MK.b(T8), SC.b(), scb], [prod.b()])
                    so(lambda e: e.tensor_tensor(NSn[:], NSn[:], prod[:], op=ALU.add), [prod.b(), NSn.b()], [NSn.b()])
                    S.dma("sp", lambda e: e.dma_start(out=ns_out, in_=NSn[:]), reads=[NSn.b()])
                    so(lambda e: e.tensor_tensor(Rr[:], SC[:].unsqueeze(2).to_broadcast([4, 5, 4, 4]),
                                                 identf[0:4, 0:4].unsqueeze(1).unsqueeze(3).to_broadcast([4, 5, 4, 4]), op=ALU.mult),
                       [SC.b(), identf.b()], [Rr.b()])
                    pbx = PB[0]
                    S.op("pe", lambda e: e.matmul(pbx[:, 0:80], lhsT=ones4f[:], rhs=Rr[:].rearrange("s k a h -> s (k a h)"), start=True, stop=True),
                         reads=[ones4f.b(), Rr.b()], writes=[pbx.b()])
                    evac(SCB[:].rearrange("p k m -> p (k m)"), pbx[:, 0:80], [pbx.b()], [SCB.b()])
                    for s_ in range(4):
                        so(lambda e, s_=s_: e.tensor_copy(Sel[:, s_, :], ident[0:4, s_:s_ + 1].to_broadcast([4, 128])), [ident.b()], [Sel.b()])
                    for s_ in range(4):
                        for which, src_ap, dst in ((0, MQS[:], QB), (1, MK[0:4, T8, :], KB)):
                            pbq = PB[1 + (s_ * 2 + which) % 3]
                            S.op("pe", lambda e, s_=s_, src_ap=src_ap, pbq=pbq: e.matmul(pbq[:, 0:512], lhsT=Sel[:, s_, :], rhs=src_ap, start=True, stop=True),
                                 reads=[Sel.b(), MQS.b(), MK.b(T8)], writes=[pbq.b()])
                            evac(dst[:, s_, :], pbq[:, 0:512], [pbq.b()], [dst.b(s_)])
                    for s_ in range(4):
                        so(lambda e, s_=s_: e.tensor_tensor(ctmp[:].rearrange("p (h a) d -> p h a d", h=4),
                                                            CS[:, s_ * 8:(s_ + 1) * 8, :].rearrange("p (h a) d -> p h a d", h=4),
                                                            QB[:, s_, :].rearrange("p (h d) -> p h d", h=4).unsqueeze(2).to_broadcast([128, 4, 2, 128]),
                                                            op=ALU.mult), [CS.b(), QB.b(s_)], [ctmp.b()])
                        so(lambda e, s_=s_: e.tensor_reduce(CQ[:, s_ * 8:(s_ + 1) * 8], ctmp[:], AX.X, ALU.add), [ctmp.b()], [CQ.b()])
                    pbt = PB[4]
                    pbtv = pbt[:].bitcast(BF16)
                    for c in range(8):
                        S.op("pe", lambda e, c=c: e.transpose(pbtv[:, c * 4:c * 4 + 4], MV[0:4, T8, c * 128:(c + 1) * 128], ident[0:4, 0:4]),
                             reads=[MV.b(T8), ident.b()], writes=[pbt.b()])
                    evac(VT[:].rearrange("p c s -> p (c s)"), pbtv[:, 0:32], [pbt.b()], [VT.b()])
                    for c in range(8):
                        S.op("pe", lambda e, c=c: e.transpose(pbtv[:, c * 4:c * 4 + 4], MOZ[0:4, T8, c * 128:(c + 1) * 128], ident[0:4, 0:4]),
                             reads=[MOZ.b(T8), ident.b()], writes=[pbt.b()])
                    evac(MZT[:].rearrange("p c s -> p (c s)"), pbtv[:, 0:32], [pbt.b()], [MZT.b()])
                    pbw = PB[5]
                    for c in range(8):
                        S.op("pe", lambda e, c=c: e.matmul(pbw[:, c * 4:c * 4 + 4], lhsT=mlnw_bc[0:4, c * 128:(c + 1) * 128], rhs=identf[0:4, 0:4],
                                                            start=True, stop=True),
                             reads=[mlnw_bc.b(), identf.b()], writes=[pbw.b()])
                    evac(mlnwT[:], pbw[:, 0:32].rearrange("p (c k) -> p c k", k=4)[:, :, 0], [pbw.b()], [mlnwT.b()])
                    VTv = VT[:].rearrange("p (h a) s -> p s h a", h=4)
                    MZv = MZT[:].rearrange("p (h a) s -> p s h a", h=4)

                    def bc(k):
                        return SCB[:, k, :].rearrange("p (s h) -> p s h", s=4).unsqueeze(3).to_broadcast([128, 4, 4, 2])
                    so(lambda e: e.tensor_tensor(HS[:], CQ[:].rearrange("p (s h a) -> p s h a", s=4, h=4), bc(2), op=ALU.mult), [CQ.b(), SCB.b()], [HS.b()])
                    so(lambda e: e.tensor_tensor(H2[:], VTv, bc(3), op=ALU.mult), [VT.b(), SCB.b()], [H2.b()])
                    so(lambda e: e.tensor_tensor(HS[:], HS[:], H2[:], op=ALU.add), [HS.b(), H2.b()], [HS.b()])
                    so(lambda e: e.tensor_tensor(DV[:], VTv, bc(1), op=ALU.mult), [VT.b(), SCB.b()], [DV.b()])
                    so(lambda e: e.tensor_tensor(H2[:], HS[:], HS[:], op=ALU.mult), [HS.b()], [H2.b()])
                    pbn = PB[6]
                    S.op("pe", lambda e: e.matmul(pbn[:, 0:32], lhsT=identf_ones[:], rhs=H2[:].rearrange("p s h a -> p (s h a)"), start=True, stop=True),
                         reads=[identf_ones.b(), H2.b()], writes=[pbn.b()])
                    so(lambda e: e.tensor_reduce(tot[:], pbn[:, 0:32].rearrange("p (m a) -> p m a", a=2), AX.X, ALU.add), [pbn.b()], [tot.b()])
                    S.op("act", lambda e: e.activation(tot[:], tot[:], AF.Sqrt, scale=1.0 / 256, bias=epsb[:, 0:1]), reads=[tot.b(), epsb.b()], writes=[tot.b()])
                    so(lambda e: e.reciprocal(tot[:], tot[:]), [tot.b()], [tot.b()])
                    so(lambda e: e.tensor_tensor(HS[:], HS[:], tot[:].rearrange("p (s h) -> p s h", s=4).unsqueeze(3).to_broadcast([128, 4, 4, 2]),
                                                 op=ALU.mult), [HS.b(), tot.b()], [HS.b()])
                    so(lambda e: e.tensor_tensor(HS[:], HS[:], mlnwT[:].rearrange("p (h a) -> p h a", h=4).unsqueeze(1).to_broadcast([128, 4, 4, 2]),
                                                 op=ALU.mult), [HS.b(), mlnwT.b()], [HS.b()])
                    so(lambda e: e.tensor_tensor(ML_T[:, T8, :, 0:4].rearrange("p (h a) s -> p s h a", h=4), HS[:], MZv, op=ALU.mult),
                       [HS.b(), MZT.b()], [ML_T.b(T8)])
                    for s_ in range(4):
                        for hh_ in range(4):
                            for a_ in range(2):
                                idx = s_ * 8 + hh_ * 2 + a_
                                so(lambda e, s_=s_, hh_=hh_, a_=a_: e.tensor_scalar(ctmp[:, 0, :], KB[:, s_, hh_ * 128:(hh_ + 1) * 128],
                                                                                    DV[:, s_, hh_, a_:a_ + 1], None, op0=ALU.mult),
                                   [KB.b(s_), DV.b(), ctmp.b()], [ctmp.b()])
                                so(lambda e, s_=s_, hh_=hh_, idx=idx: e.scalar_tensor_tensor(out=CS[:, idx, :], in0=CS[:, idx, :],
                                                                                             scalar=SCB[:, 0, s_ * 4 + hh_:s_ * 4 + hh_ + 1], in1=ctmp[:, 0, :],
                                                                                             op0=ALU.mult, op1=ALU.add),
                                   [CS.b(), SCB.b(), ctmp.b(), CQ.b()], [CS.b()])
                    S.dma("sp", lambda e: e.dma_start(out=Cs_out.rearrange("s h (a p) d -> p (s h a) d", p=128), in_=CS[:]), reads=[CS.b()])
        S.barrier()
        ost = contextlib.ExitStack()
        with ost:
            Cn = sb("Cn", [128, 4, 2, 128], F32, ost)
            nrow = sb("nrow", [1, 4, 128], F32, ost)
            mrow = sb("mrow", [4, 1], F32, ost)
            for h in (range(4) if not _os.environ.get('KNOSTATE') else []):
                for half in range(2):
                    pc = PB[6 + (half % 2)]
                    S.op("pe", lambda e, h=h, half=half, pc=pc: e.matmul(pc[:, 0:128], lhsT=Cst[:, h, half * 128:(half + 1) * 128],
                                                                         rhs=identf[:], start=True, stop=True),
                         reads=[Cst.b(h), identf.b()], writes=[pc.b()])
                    S.op("act", lambda e, h=h, half=half, pc=pc: e.copy(Cn[:, h, half, :], pc[:, 0:128]),
                         reads=[pc.b()], writes=[Cn.b()])
                pc = PB[5]
                S.op("pe", lambda e, h=h, pc=pc: e.matmul(pc[0:1, 0:128], lhsT=Cst[:, h, 256:257], rhs=identf[:], start=True, stop=True),
                     reads=[Cst.b(h), identf.b()], writes=[pc.b()])
                S.op("act", lambda e, h=h, pc=pc: e.copy(nrow[:, h, :], pc[0:1, 0:128]), reads=[pc.b()], writes=[nrow.b()])
            S.op("dve", lambda e: e.tensor_tensor(mrow[:], Brow[:], Mrow[:], op=ALU.add), reads=[Brow.b(), Mrow.b()], writes=[mrow.b()])
            S.dma("sp", lambda e: e.dma_start(out=C_out.rearrange("h (a p) d -> p h a d", p=128), in_=Cn[:]), reads=[Cn.b()])
            S.dma("sp", lambda e: e.dma_start(out=n_out.rearrange("(o h) d -> o h d", o=1), in_=nrow[:]), reads=[nrow.b()])
            S.dma("sp", lambda e: e.dma_start(out=m_out, in_=mrow[:]), reads=[mrow.b()])
        S.barrier()
        p2c = contextlib.ExitStack()
        with p2c:
            ypre = sb("ypre", [128, NT2, D], F32, p2c)
            fnw_bc = sb("fnw_bc", [128, D], F32, p2c)
            Wbo = sb("Wbo", [128, 2, NCH, 512], BF16, p2c)
            junk3 = sb("junk3", [128, D], BF16, p2c)
            sq3 = sb("sq3", [128, 4], F32, p2c)
            S.dma("sp", lambda e: e.dma_start(out=fnw_bc[:], in_=fnorm_w.partition_broadcast(128)), writes=[fnw_bc.b()])
            for t in range(NT2):
                S.dma("sp", lambda e, t=t: e.dma_start(out=ypre[:, t, :], in_=xrows2(t)), writes=[ypre.b(t)])
            Po = Proj(None, Wbo, [PB[0], PB[1], PB[2], PB[3]], src=w_out_v)
            for g in range(4):
                sl = Po.load(g * 512, 512)
                gsl = slice(g * 512, (g + 1) * 512)
                for t in range(NT2):
                    pb = Po._bank()
                    for fc in range(16):
                        S.op("pe", lambda e, t=t, fc=fc, pb=pb, sl=sl: e.matmul(
                            pb[:, 0:512], lhsT=(ATT_T[:, t, fc, :] if fc < 8 else ML_T[:, t, fc - 8, :]), rhs=Wbo[:, sl, fc, :],
                            start=(fc == 0), stop=(fc == 15)),
                            reads=[ATT_T.b(t), ML_T.b(t), Wbo.b(sl)], writes=[pb.b()])
                    S.op("dve", lambda e, t=t, gsl=gsl, pb=pb: e.tensor_tensor(ypre[:, t, gsl], ypre[:, t, gsl], pb[:, 0:512], op=ALU.add),
                         reads=[pb.b(), ypre.b(t)], writes=[ypre.b(t)])
            for t in range(NT2):
                S.op("act", lambda e, t=t: e.activation(junk3[:], ypre[:, t, :], AF.Square, accum_out=sq3[:, 0:1]),
                     reads=[ypre.b(t)], writes=[junk3.b(), sq3.b()])
                S.op("act", lambda e: e.activation(sq3[:, 1:2], sq3[:, 0:1], AF.Sqrt, scale=1.0 / D, bias=epsb[:, 0:1]),
                     reads=[sq3.b(), epsb.b()], writes=[sq3.b()])
                S.op("dve", lambda e: e.reciprocal(sq3[:, 2:3], sq3[:, 1:2]), reads=[sq3.b()], writes=[sq3.b()])
                S.op("dve", lambda e, t=t: e.scalar_tensor_tensor(out=ypre[:, t, :], in0=ypre[:, t, :], scalar=sq3[:, 2:3], in1=fnw_bc[:],
                                                                  op0=ALU.mult, op1=ALU.mult),
                     reads=[ypre.b(t), sq3.b(), fnw_bc.b()], writes=[ypre.b(t)])
                dst = y_out[t * 128:(t + 1) * 128, :] if t < NOWN else ys_out
                S.dma("sp", lambda e, t=t, dst=dst: e.dma_start(out=dst, in_=ypre[:, t, :]), reads=[ypre.b(t)])
        S.finish("sp")
        S.emit()
    return nc


def host_consts():
    idn = np.eye(128, dtype=np.float32)
    s = np.arange(128)
    tri = (s[:, None] <= s[None, :]).astype(np.float32)
    tribias = np.where(s[None, :] <= s[:, None], 0.0, NEG).astype(np.float32)
    pw = (0.5 ** np.arange(1, NITER + 1)).astype(np.float32)[None, :]
    sut = (s[:, None] < s[None, :]).astype(np.float32)
    iota = np.concatenate([16.0 * np.arange(17), np.arange(16), np.arange(128)]).astype(np.float32)[None, :]
    return {"c_ident": idn, "c_tri": tri, "c_tribias": tribias, "c_pow": pw, "c_sut": sut, "c_iota": iota}


def make_in_maps(inputs, cores):
    xp = np.asarray(inputs["x_prompt"], np.float32)
    consts = host_consts()
    cidx_h = np.asarray(inputs["cache_idx_k"], np.float32).reshape(5120, 8192)
    ck_h = np.asarray(inputs["cache_k"], np.float32).reshape(655360, 256)
    cv_h = np.asarray(inputs["cache_v"], np.float32).reshape(655360, 256)
    maps = []
    for c in cores:
        b, j = c // 4, c % 4
        xkc = np.zeros((NT1 * 128, D), np.float32)
        nreal = 1024 * (j + 1)
        xkc[NT1 * 128 - nreal:] = xp[b, :nreal]
        vld = np.zeros((1, NT1 * 128), np.float32)
        vld[0, NT1 * 128 - nreal:] = 1.0
        xsc = np.zeros((128, D), np.float32)
        xsc[0:4] = np.asarray(inputs["x_sample"], np.float32)[4 * c:4 * c + 4, 0]
        m = {
            "xk": xkc, "valid": vld, "xs": xsc,
            "ptabT": np.ascontiguousarray(np.asarray(inputs["page_table"], np.int32)[4 * c:4 * c + 4].T),
            "cidx": cidx_h, "ck": ck_h, "cv": cv_h,
            "st_C": np.ascontiguousarray(np.asarray(inputs["state_C"], np.float32)[0, 4 * c:4 * c + 4]),
            "st_n": np.ascontiguousarray(np.asarray(inputs["state_n"], np.float32)[0, 4 * c:4 * c + 4]).reshape(4, 512),
            "st_m": np.ascontiguousarray(np.asarray(inputs["state_m"], np.float32)[0, 4 * c:4 * c + 4]),
            "w_in": np.asarray(inputs["w_in"], np.float32)[0],
            "w_out": np.asarray(inputs["w_out"], np.float32)[0],
            "norm_w": np.asarray(inputs["norm_w"], np.float32).reshape(1, D),
            "fnorm_w": np.asarray(inputs["final_norm_w"], np.float32).reshape(1, D),
            "mlnorm_w": np.asarray(inputs["ml_norm_w"], np.float32).reshape(1, 1024),
            "b_ig": np.asarray(inputs["b_igate"], np.float32).reshape(4, 1),
            "b_fg": np.asarray(inputs["b_fgate"], np.float32).reshape(4, 1),
        }
        m.update(consts)
        maps.append(m)
    return maps


_NC_CACHE = {}


def kernel(**inputs):
    cores = list(range(8))
    if "nc" not in _NC_CACHE:
        _NC_CACHE["nc"] = build_program()
    nc = _NC_CACHE["nc"]
    maps = make_in_maps(inputs, cores)
    res = run_bass_kernel_spmd(nc, maps, core_ids=cores)
    R = res.results
    f32 = np.float32
    y_prompt = np.zeros((2, 4096, D), f32)
    k_prompt = np.zeros((1, 2, 4096, 2, 128), f32)
    v_prompt = np.zeros((1, 2, 4096, 2, 128), f32)
    ik_prompt = np.zeros((1, 2, 4096, 64), f32)
    C_prompt = np.zeros((1, 2, 4, 256, 128), f32)
    n_prompt = np.zeros((1, 2, 4, 128), f32)
    m_prompt = np.zeros((1, 2, 4), f32)
    for c in cores:
        b, j = c // 4, c % 4
        sl = slice(1024 * j, 1024 * (j + 1))
        y_prompt[b, sl] = R[c]["y_out"]
        k_prompt[0, b, sl] = R[c]["k_out"].reshape(1024, 2, 128)
        v_prompt[0, b, sl] = R[c]["v_out"].reshape(1024, 2, 128)
        ik_prompt[0, b, sl] = R[c]["ki_out"]
        if j == 3:
            C_prompt[0, b] = R[c]["C_out"]
            n_prompt[0, b] = R[c]["n_out"]
            m_prompt[0, b] = R[c]["m_out"].reshape(4)
    y_sample = np.zeros((32, 1, D), f32)
    k_sample = np.zeros((1, 32, 1, 2, 128), f32)
    v_sample = np.zeros((1, 32, 1, 2, 128), f32)
    ik_sample = np.zeros((1, 32, 1, 64), f32)
    C_sample = np.zeros((1, 32, 4, 256, 128), f32)
    n_sample = np.zeros((1, 32, 4, 128), f32)
    m_sample = np.zeros((1, 32, 4), f32)
    for c in cores:
        ss = slice(4 * c, 4 * c + 4)
        y_sample[ss, 0] = R[c]["ys_out"][0:4]
        k_sample[0, ss, 0] = R[c]["ks_out"].reshape(4, 2, 128)
        v_sample[0, ss, 0] = R[c]["vs_out"].reshape(4, 2, 128)
        ik_sample[0, ss, 0] = R[c]["kis_out"]
        C_sample[0, ss] = R[c]["Cs_out"]
        n_sample[0, ss] = R[c]["ns_out"].reshape(4, 4, 128)
        m_sample[0, ss] = R[c]["ms_out"]
    return (y_prompt, y_sample, k_prompt, v_prompt, ik_prompt, C_prompt, n_prompt, m_prompt,
            k_sample, v_sample, ik_sample, C_sample, n_sample, m_sample)
```

```python
import contextlib
import numpy as np
import ml_dtypes
import concourse.bass as bass
import concourse.mybir as mybir
from concourse.bass_utils import run_bass_kernel_spmd

F32 = mybir.dt.float32
BF16 = mybir.dt.bfloat16
I32 = mybir.dt.int32
ALU = mybir.AluOpType
AF = mybir.ActivationFunctionType
AX = mybir.AxisListType

ENGS = ("pe", "act", "dve", "pool", "sp")

D = 2048
NCH = 16
INW = 7768
C_AQ, C_AK, C_AV, C_IQ, C_IK, C_IW, C_AZ = 0, 1024, 1280, 1536, 2560, 2624, 2640
C_MQ, C_MK, C_MV, C_MI, C_MF, C_MO, C_MZ = 3664, 4176, 4688, 5712, 5716, 5720, 6744
EPS = 1e-6
NPRE = 24
NOWN = 8
NT1 = NPRE + NOWN
NEG = -1.0e30
LN_DK = float(np.log(128.0 ** -0.5))
NITER = 18
import os as _os
KMAXOPS = int(_os.environ.get('KMAXOPS', '100000000'))
KLOG = bool(_os.environ.get('KLOG'))


class Buf:
    __slots__ = ("name", "writer", "readers", "excl")

    def __init__(self, name=""):
        self.name = name
        self.excl = False
        self.writer = None
        self.readers = {}


class Sched:
    NDMA = 6

    def __init__(self, nc):
        self.nc = nc
        self.ops = {e: [] for e in ENGS}
        self.cnt = {e: 0 for e in ENGS}
        self.waited = {e: {} for e in ENGS}
        self.dma_rr = {e: 0 for e in ENGS}
        self.dma_cnt = {}
        self.cap = None

    def capture(self, f):
        prev, self.cap = self.cap, []
        f()
        out, self.cap = self.cap, prev
        return out

    def replay_rr(self, lists, weights=None):
        pos = [0] * len(lists)
        weights = weights or [1] * len(lists)
        live = True
        while live:
            live = False
            for k, l in enumerate(lists):
                for _ in range(weights[k]):
                    if pos[k] < len(l):
                        kind, eng, fn, reads, writes = l[pos[k]]
                        pos[k] += 1
                        live = True
                        (self.op if kind == "op" else self.dma)(eng, fn, reads, writes)

    def _need(self, eng, dep, waits):
        if dep is None:
            return
        key, val, deng = dep
        if deng == eng and eng == "pe":
            return
        if self.waited[eng].get(key, 0) >= val:
            return
        waits[key] = max(waits.get(key, 0), val)

    def _deps(self, eng, reads, writes, is_dma):
        waits = {}
        for b in reads:
            self._need(eng, b.writer, waits)
        for b in writes:
            self._need(eng, b.writer, waits)
            for rk, (rv, re_) in b.readers.items():
                self._need(eng, (rk, rv, re_), waits)
        for k, v in waits.items():
            self.waited[eng][k] = v
        return waits

    def _mark(self, reads, writes, tok):
        for b in reads:
            if b.readers.get(tok[0], (0,))[0] < tok[1]:
                b.readers[tok[0]] = (tok[1], tok[2])
        for b in writes:
            b.writer = tok
            b.readers = {}

    def op(self, eng, fn, reads=(), writes=()):
        if self.cap is not None:
            self.cap.append(("op", eng, fn, list(reads), list(writes)))
            return
        self.total = getattr(self, 'total', 0) + 1
        if self.total > KMAXOPS:
            return
        if KLOG:
            print('OP', self.total, eng, [b.name for b in reads], '->', [b.name for b in writes])
        if any(b.excl for b in reads):
            writes = list(writes) + [b for b in reads if b.excl]
            reads = [b for b in reads if not b.excl]
        waits = self._deps(eng, reads, writes, False)
        self.cnt[eng] += 1
        tok = ("c_" + eng, self.cnt[eng], eng)
        self.ops[eng].append((list(waits.items()), fn, ("c_" + eng, 1)))
        self._mark(reads, writes, tok)

    def dma(self, eng, fn, reads=(), writes=()):
        if self.cap is not None:
            self.cap.append(("dma", eng, fn, list(reads), list(writes)))
            return
        self.total = getattr(self, 'total', 0) + 1
        if self.total > KMAXOPS:
            return
        if KLOG:
            print('DMA', self.total, eng, [b.name for b in reads], '->', [b.name for b in writes])
        waits = self._deps(eng, reads, writes, True)
        i = self.dma_rr[eng]
        self.dma_rr[eng] = (i + 1) % self.NDMA
        key = "d_%s_%d" % (eng, i)
        prev = self.dma_cnt.get(key, 0)
        if prev > 0 and self.waited[eng].get(key, 0) < prev:
            waits[key] = max(waits.get(key, 0), prev)
            self.waited[eng][key] = prev
        self.dma_cnt[key] = prev + 16
        tok = (key, prev + 16, "dma_" + eng)
        self.ops[eng].append((list(waits.items()), fn, (key, 16)))
        self._mark(reads, writes, tok)

    def barrier(self):
        if _os.environ.get('KBAR'):
            print('BARRIER at op', getattr(self, 'total', 0), {e: self.cnt[e] for e in ENGS})
        tgt = {"c_" + e: self.cnt[e] for e in ENGS if self.cnt[e] > 0}
        tgt.update(self.dma_cnt)
        for eng in ENGS:
            waits = []
            for key, val in tgt.items():
                if self.waited[eng].get(key, 0) < val:
                    waits.append((key, val))
                    self.waited[eng][key] = val
            self.ops[eng].append((waits, None, None))

    def finish(self, eng="sp"):
        waits = []
        for key, val in self.dma_cnt.items():
            if self.waited[eng].get(key, 0) < val:
                waits.append((key, val))
                self.waited[eng][key] = val
        self.ops[eng].append((waits, None, None))

    def emit(self):
        nc = self.nc
        keys = ["c_" + e for e in ENGS if self.cnt[e] > 0] + list(self.dma_cnt.keys())
        with contextlib.ExitStack() as st:
            sems = {k: st.enter_context(nc.semaphore(k)) for k in keys}
            block = st.enter_context(nc.Block())

            def run(eng_name):
                def body(eng):
                    for waits, fn, inc in self.ops[eng_name]:
                        for k, v in waits:
                            eng.wait_ge(sems[k], v)
                        if fn is not None:
                            fn(eng).then_inc(sems[inc[0]], inc[1])
                return body

            block.sync(run("sp"))
            block.tensor(run("pe"))
            block.scalar(run("act"))
            block.vector(run("dve"))
            block.gpsimd(run("pool"))


class TB:
    def __init__(self, t, name):
        self.t = t
        self.name = name
        self.bufs = {}

    def b(self, key=0):
        if key not in self.bufs:
            self.bufs[key] = Buf("%s/%s" % (self.name, key))
        return self.bufs[key]

    def __getitem__(self, k):
        return self.t[k]


class TView:
    def __init__(self, tb, lo, hi):
        self.ap = tb.t[:, lo:hi]
        self.buf = tb.b()

    def __getitem__(self, k):
        if isinstance(k, slice) and k == slice(None):
            return self.ap
        return self.ap[k]

    def b(self, key=0):
        return self.buf


def build_program(ntile2=NOWN + 1, debug=False):
    nc = bass.Bass("TRN2", target_bir_lowering=False)
    NT2 = ntile2
    TOK2 = NT2 * 128

    def din(name, shape, dt=F32):
        return nc.dram_tensor(name, shape, dt, kind="ExternalInput").ap()

    def dout(name, shape, dt=F32):
        return nc.dram_tensor(name, shape, dt, kind="ExternalOutput").ap()

    xk = din("xk", [NT1 * 128, D])
    xs = din("xs", [128, D])
    st_C = din("st_C", [4, 4, 256, 128])
    ptabT = din("ptabT", [128, 4], I32)
    cidx = din("cidx", [5120, 8192])
    ck = din("ck", [655360, 256])
    cv = din("cv", [655360, 256])
    c_sut = din("c_sut", [128, 128])
    c_iota = din("c_iota", [1, 17 + 16 + 128])
    st_n = din("st_n", [4, 512])
    st_m = din("st_m", [4, 4])
    valid = din("valid", [1, NT1 * 128])
    w_in = din("w_in", [D, INW])
    w_out = din("w_out", [D, D])
    norm_w = din("norm_w", [1, D])
    fnorm_w = din("fnorm_w", [1, D])
    mlnorm_w = din("mlnorm_w", [1, 1024])
    b_ig = din("b_ig", [4, 1])
    b_fg = din("b_fg", [4, 1])
    c_ident = din("c_ident", [128, 128])
    c_tri = din("c_tri", [128, 128])
    c_tribias = din("c_tribias", [128, 128])
    c_pow = din("c_pow", [1, NITER])

    y_out = dout("y_out", [NOWN * 128, D])
    ys_out = dout("ys_out", [128, D])
    Cs_out = dout("Cs_out", [4, 4, 256, 128])
    ks_out = dout("ks_out", [4, 256])
    vs_out = dout("vs_out", [4, 256])
    kis_out = dout("kis_out", [4, 64])
    scr = nc.dram_tensor("scr_idx", [4, 256], I32, kind="Internal").ap()
    ns_out = dout("ns_out", [4, 512])
    ms_out = dout("ms_out", [4, 4])
    k_out = dout("k_out", [NOWN * 128, 256])
    v_out = dout("v_out", [NOWN * 128, 256])
    ki_out = dout("ki_out", [NOWN * 128, 64])
    C_out = dout("C_out", [4, 256, 128])
    n_out = dout("n_out", [4, 128])
    m_out = dout("m_out", [4, 1])

    w_in_v = w_in.rearrange("(c p) n -> p c n", p=128)
    w_out_v = w_out.rearrange("(c p) n -> p c n", p=128)

    S = Sched(nc)
    st = contextlib.ExitStack()

    _uid = [0]

    def sb(name, shape, dt, stack=None):
        _uid[0] += 1
        name = "%s_%d" % (name, _uid[0])
        return TB((stack or st).enter_context(nc.sbuf_tensor(name, shape, dt)), name)

    def ps(name, shape, dt, stack=None):
        return TB((stack or st).enter_context(nc.psum_tensor(name, shape, dt)), name)

    with st:
        PB = [ps("pb%d" % i, [128, 512], F32) for i in range(8)]
        for pb_ in PB:
            pb_.b().excl = True

        ident = sb("ident", [128, 128], BF16)
        identf = sb("identf", [128, 128], F32)
        tri = sb("tri", [128, 128], BF16)
        tribias = sb("tribias", [128, 128], F32)
        ones_bf = sb("ones_bf", [128, 128], BF16)
        ones4f = sb("ones4f", [4, 128], F32)
        identf_ones = sb("onesf", [128, 128], F32)
        big = sb("big", [4, 1], F32)
        bi_sb = sb("bi_sb", [4, 1], F32)
        nbf_sb = sb("nbf_sb", [4, 1], F32)
        powr = sb("powr", [128, NITER], F32)
        Cst = sb("Cst", [128, 4, 257], F32)
        Brow = sb("Brow", [4, 1], F32)
        Mrow = sb("Mrow", [4, 1], F32)
        ATT_T = sb("ATT_T", [128, NT2, 8, 128], BF16)
        if NT2 > NOWN:
            S.op("pool", lambda e: e.memset(ATT_T[:, NOWN, :, :], 0.0), writes=[ATT_T.b(NOWN)])

        S.dma("pool", lambda e: e.dma_start(out=ident[:], in_=c_ident), writes=[ident.b()])
        S.dma("sp", lambda e: e.dma_start(out=identf[:], in_=c_ident), writes=[identf.b()])
        S.dma("pool", lambda e: e.dma_start(out=tri[:], in_=c_tri), writes=[tri.b()])
        S.dma("sp", lambda e: e.dma_start(out=tribias[:], in_=c_tribias), writes=[tribias.b()])
        S.dma("sp", lambda e: e.dma_start(out=bi_sb[:], in_=b_ig), writes=[bi_sb.b()])
        S.dma("sp", lambda e: e.dma_start(out=nbf_sb[:], in_=b_fg), writes=[nbf_sb.b()])
        S.dma("sp", lambda e: e.dma_start(out=powr[:], in_=c_pow.partition_broadcast(128)), writes=[powr.b()])
        S.op("dve", lambda e: e.memset(ones_bf[:], 1.0), writes=[ones_bf.b()])
        S.op("dve", lambda e: e.memset(ones4f[:], 1.0), writes=[ones4f.b()])
        S.op("dve", lambda e: e.memset(identf_ones[:], 1.0), writes=[identf_ones.b()])
        S.op("dve", lambda e: e.memset(Cst[:], 0.0), writes=[Cst.b(h) for h in range(4)])
        S.op("dve", lambda e: e.memset(Brow[:], 0.0), writes=[Brow.b()])
        S.op("dve", lambda e: e.memset(Mrow[:], 0.0), writes=[Mrow.b()])
        S.op("dve", lambda e: e.tensor_scalar(nbf_sb[:], nbf_sb[:], -1.0, None, op0=ALU.mult),
             reads=[nbf_sb.b()], writes=[nbf_sb.b()])

        def load_w(dst, c0, ncols, dcol=0, src=w_in_v):
            for hh in range(2):
                cs = slice(hh * 8, hh * 8 + 8)
                S.dma("pool", lambda e, cs=cs: e.dma_start(out=dst.t[:, cs, dcol:dcol + ncols],
                                                             in_=src[:, cs, c0:c0 + ncols]),
                      writes=[dst.b()])

        def rmsnorm_T(xrows, xt, hbf, junk, ssq, hT_dst, hT_buf, nw_bc, slot):
            S.dma("sp", lambda e: e.dma_start(out=xt[:, slot, :], in_=xrows), writes=[xt.b(slot)])
            S.op("act", lambda e: e.activation(junk[:], xt[:, slot, :], AF.Square, accum_out=ssq[:, 0:1]),
                 reads=[xt.b(slot)], writes=[junk.b(), ssq.b(0)])
            S.op("act", lambda e: e.activation(ssq[:, 1:2], ssq[:, 0:1], AF.Sqrt, scale=1.0 / D, bias=epsb[:, 0:1]),
                 reads=[ssq.b(0), epsb.b()], writes=[ssq.b(1)])
            S.op("dve", lambda e: e.reciprocal(ssq[:, 2:3], ssq[:, 1:2]), reads=[ssq.b(1)], writes=[ssq.b(2)])
            S.op("dve", lambda e: e.scalar_tensor_tensor(out=hbf[:], in0=xt[:, slot, :], scalar=ssq[:, 2:3],
                                                         in1=nw_bc[:], op0=ALU.mult, op1=ALU.mult),
                 reads=[xt.b(slot), ssq.b(2), nw_bc.b()], writes=[hbf.b()])
            for half in range(2):
                pb = PB[half]
                pv = pb[:].bitcast(BF16)
                for cc in range(8):
                    c = half * 8 + cc
                    S.op("pe", lambda e, c=c, cc=cc, pv=pv: e.transpose(pv[:, cc * 128:(cc + 1) * 128],
                                                                        hbf[:, c * 128:(c + 1) * 128], ident[:]),
                         reads=[hbf.b(), ident.b()], writes=[pb.b()])
                eng = "act" if half == 0 else "dve"
                if eng == "act":
                    S.op("act", lambda e, half=half, pv=pv: e.copy(hT_dst(half), pv.rearrange("p (c t) -> p c t", c=8)),
                         reads=[pb.b()], writes=[hT_buf])
                else:
                    S.op("dve", lambda e, half=half, pv=pv: e.tensor_copy(hT_dst(half), pv.rearrange("p (c t) -> p c t", c=8)),
                         reads=[pb.b()], writes=[hT_buf])

        epsb = sb("epsb", [128, 1], F32)
        S.op("dve", lambda e: e.memset(epsb[:], EPS), writes=[epsb.b()])

        def gate_rows(gi_ps, gf_ps, gi_b, gf_b, vrow, nbrow, G):
            ig, lf, e1, B_, Gg, Mt, u, nml, nml2, aend, cl = (G[k] for k in
                                                               "ig lf e1 B G Mt u nml nml2 aend cl".split())
            S.op("act", lambda e: e.activation(ig[:], gi_ps, AF.Identity, bias=bi_sb[:, 0:1]),
                 reads=[gi_b, bi_sb.b()], writes=[ig.b()])
            S.op("act", lambda e: e.activation(e1[:], gf_ps, AF.Exp, scale=-1.0, bias=nbf_sb[:, 0:1]),
                 reads=[gf_b, nbf_sb.b()], writes=[e1.b()])
            S.op("act", lambda e: e.activation(e1[:], e1[:], AF.Ln, bias=one4[:, 0:1]),
                 reads=[e1.b(), one4.b()], writes=[e1.b()])
            if vrow is not None:
                S.op("dve", lambda e: e.tensor_tensor(ig[:], ig[:], vrow, op=ALU.mult), reads=[ig.b(), G["vb"]], writes=[ig.b()])
                S.op("dve", lambda e: e.tensor_tensor(ig[:], ig[:], nbrow, op=ALU.add), reads=[ig.b(), G["nb"]], writes=[ig.b()])
                S.op("dve", lambda e: e.scalar_tensor_tensor(out=lf[:], in0=e1[:], scalar=-1.0, in1=vrow,
                                                             op0=ALU.mult, op1=ALU.mult),
                     reads=[e1.b(), G["vb"]], writes=[lf.b()])
            else:
                S.op("dve", lambda e: e.tensor_scalar(lf[:], e1[:], -1.0, None, op0=ALU.mult),
                     reads=[e1.b()], writes=[lf.b()])
            S.op("dve", lambda e: e.tensor_tensor_scan(B_[:], ones4f[:], lf[:], Brow[:, 0:1], op0=ALU.mult, op1=ALU.add),
                 reads=[ones4f.b(), lf.b(), Brow.b()], writes=[B_.b()])
            S.op("dve", lambda e: e.tensor_copy(Brow[:], B_[:, 127:128]), reads=[B_.b()], writes=[Brow.b()])
            S.op("dve", lambda e: e.tensor_tensor(Gg[:], ig[:], B_[:], op=ALU.subtract), reads=[ig.b(), B_.b()], writes=[Gg.b()])
            S.op("dve", lambda e: e.tensor_tensor_scan(Mt[:], Gg[:], Gg[:], Mrow[:, 0:1], op0=ALU.max, op1=ALU.max),
                 reads=[Gg.b(), Mrow.b()], writes=[Mt.b()])
            S.op("dve", lambda e: e.tensor_scalar(nml[:], Mt[:, 127:128], -1.0, None, op0=ALU.mult), reads=[Mt.b()], writes=[nml.b()])
            S.op("dve", lambda e: e.tensor_scalar(nml2[:], Mt[:, 127:128], -1.0, LN_DK, op0=ALU.mult, op1=ALU.add),
                 reads=[Mt.b()], writes=[nml2.b()])
            S.op("act", lambda e: e.activation(aend[:], Mrow[:], AF.Exp, bias=nml[:, 0:1]),
                 reads=[Mrow.b(), nml.b()], writes=[aend.b()])
            S.op("act", lambda e: e.activation(u[:], Gg[:], AF.Exp, bias=nml2[:, 0:1]), reads=[Gg.b(), nml2.b()], writes=[u.b()])
            S.op("act", lambda e: e.activation(cl[:], B_[:], AF.Exp, scale=-1.0, bias=nml[:, 0:1]),
                 reads=[B_.b(), nml.b()], writes=[cl.b()])
            S.op("dve", lambda e: e.tensor_copy(Mrow[:], Mt[:, 127:128]), reads=[Mt.b(), aend.b()], writes=[Mrow.b()])

        one4 = sb("one4", [4, 1], F32)
        S.op("dve", lambda e: e.memset(one4[:], 1.0), writes=[one4.b()])

        def make_gate_tiles(stack, pfx):
            G = {}
            for k in "ig lf e1 B G Mt cl".split():
                G[k] = sb(pfx + k, [4, 128], F32, stack)
            G["u"] = sb(pfx + "u", [4, 128], BF16, stack)
            G["clb"] = sb(pfx + "clb", [4, 128], BF16, stack)
            for k in "nml nml2 aend".split():
                G[k] = sb(pfx + k, [4, 1], F32, stack)
            G["dg"] = sb(pfx + "dg", [4, 4], F32, stack)
            G["uT"] = sb(pfx + "uT", [128, 4], F32, stack)
            G["clT"] = sb(pfx + "clT", [128, 4], F32, stack)
            G["abc"] = sb(pfx + "abc", [128, 4], F32, stack)
            return G

        def gate_cols(G, misc, want_cl, c0=0, stage=None):
            mv = misc[:].bitcast(BF16)
            b0 = 2 * c0
            if stage in (None, 0):
                S.op("dve", lambda e: e.tensor_scalar(G["dg"][:], identf[0:4, 0:4], G["aend"][:, 0:1], None, op0=ALU.mult),
                     reads=[identf.b(), G["aend"].b()], writes=[G["dg"].b()])
                if want_cl:
                    S.op("act", lambda e: e.copy(G["clb"][:], G["cl"][:]), reads=[G["cl"].b()], writes=[G["clb"].b()])
            if stage in (None, 1):
                S.op("pe", lambda e: e.transpose(mv[:, b0:b0 + 4], G["u"][:], ident[0:4, 0:4]),
                     reads=[G["u"].b(), ident.b()], writes=[misc.b()])
                S.op("pe", lambda e: e.matmul(misc[:, c0 + 16:c0 + 20], lhsT=ones4f[:], rhs=G["dg"][:], start=True, stop=True),
                     reads=[ones4f.b(), G["dg"].b()], writes=[misc.b()])
                if want_cl:
                    S.op("pe", lambda e: e.transpose(mv[:, b0 + 8:b0 + 12], G["clb"][:], ident[0:4, 0:4]),
                         reads=[G["clb"].b(), ident.b()], writes=[misc.b()])
                S.op("dve", lambda e: e.tensor_copy(G["uT"][:], mv[:, b0:b0 + 4]), reads=[misc.b()], writes=[G["uT"].b()])
                S.op("dve", lambda e: e.tensor_copy(G["abc"][:], misc[:, c0 + 16:c0 + 20]), reads=[misc.b()], writes=[G["abc"].b()])
                if want_cl:
                    S.op("dve", lambda e: e.tensor_copy(G["clT"][:], mv[:, b0 + 8:b0 + 12]), reads=[misc.b()], writes=[G["clT"].b()])

        kvst = contextlib.ExitStack()
        with kvst:
            KT = sb("KT", [128, 2, NT1 * 128], BF16, kvst)
            Vsb = sb("Vsb", [128, NT1, 256], BF16, kvst)
            KIT = sb("KIT", [128, NT1 * 128], BF16, kvst)

            p1 = contextlib.ExitStack()
            with p1:
                W_kv = sb("W_kv", [128, NCH, 512], BF16, p1)
                W_ki = sb("W_ki", [128, NCH, 128], BF16, p1)
                W_mk = sb("W_mk", [128, NCH, 512], BF16, p1)
                W_mv = sb("W_mv", [128, NCH, 1024], BF16, p1)
                W_g = sb("W_g", [128, NCH, 8], BF16, p1)
                nw_bc = sb("nw_bc", [128, D], F32, p1)
                xt = sb("xt", [128, 2, D], F32, p1)
                hbf = sb("hbf", [128, 2, D], BF16, p1)
                junk = sb("junk", [128, D], BF16, p1)
                ssq = sb("ssq", [128, 2, 4], F32, p1)
                hT = sb("hT", [128, 2, NCH, 128], BF16, p1)
                kvo = sb("kvo", [128, 2, 512], F32, p1)
                kio = sb("kio", [128, 2, 64], F32, p1)
                kbf = sb("kbf", [128, 256], BF16, p1)
                Kp = sb("Kp", [128, 4, 128], BF16, p1)
                Vaug = sb("Vaug", [128, 4, 257], BF16, p1)
                vrow = sb("vrow", [4, 2, 128], F32, p1)
                nbrow = sb("nbrow", [4, 2, 128], F32, p1)
                G1 = make_gate_tiles(p1, "g1_")

                S.dma("sp", lambda e: e.dma_start(out=nw_bc[:], in_=norm_w.partition_broadcast(128)), writes=[nw_bc.b()])
                load_w(W_kv, C_AK, 512)
                load_w(W_ki, C_IK, 64, 0)
                load_w(W_ki, C_IK, 64, 64)
                load_w(W_g, C_MI, 8)
                load_w(W_mk, C_MK, 512)
                load_w(W_mv, C_MV, 512, 0)
                load_w(W_mv, C_MV + 512, 512, 512)
                S.op("dve", lambda e: e.memset(Vaug[:], 1.0), writes=[Vaug.b(h) for h in range(4)])

                def pre_a(T):
                    sl_ = T % 2
                    xrows = xk[T * 128:(T + 1) * 128, :]
                    S.dma("sp", lambda e: e.dma_start(out=xt[:, sl_, :], in_=xrows), writes=[xt.b(sl_)])
                    S.op("act", lambda e: e.activation(junk[:], xt[:, sl_, :], AF.Square, accum_out=ssq[:, sl_, 0:1]),
                         reads=[xt.b(sl_)], writes=[junk.b(), ssq.b(sl_)])
                    S.op("act", lambda e: e.activation(ssq[:, sl_, 1:2], ssq[:, sl_, 0:1], AF.Sqrt, scale=1.0 / D, bias=epsb[:, 0:1]),
                         reads=[ssq.b(sl_), epsb.b()], writes=[ssq.b(sl_)])

                def pre_b(T):
                    sl_ = T % 2
                    S.op("dve", lambda e: e.reciprocal(ssq[:, sl_, 2:3], ssq[:, sl_, 1:2]), reads=[ssq.b(sl_)], writes=[ssq.b(sl_)])
                    S.op("dve", lambda e: e.scalar_tensor_tensor(out=hbf[:, sl_, :], in0=xt[:, sl_, :], scalar=ssq[:, sl_, 2:3],
                                                                 in1=nw_bc[:], op0=ALU.mult, op1=ALU.mult),
                         reads=[xt.b(sl_), ssq.b(sl_), nw_bc.b()], writes=[hbf.b(sl_)])

                def pre(T):
                    pre_a(T)
                    pre_b(T)

                def mid(T):
                    sl_ = T % 2
                    for half in range(2):
                        pb = PB[half]
                        pv = pb[:].bitcast(BF16)
                        for cc in range(8):
                            c = half * 8 + cc
                            S.op("pe", lambda e, c=c, cc=cc, pv=pv: e.transpose(pv[:, cc * 128:(cc + 1) * 128],
                                                                                hbf[:, sl_, c * 128:(c + 1) * 128], ident[:]),
                                 reads=[hbf.b(sl_), ident.b()], writes=[pb.b()])
                        if half == 0:
                            S.op("act", lambda e, half=half, pv=pv: e.copy(hT[:, sl_, half * 8:(half + 1) * 8, :], pv.rearrange("p (c t) -> p c t", c=8)),
                                 reads=[pb.b()], writes=[hT.b(sl_)])
                        else:
                            S.op("dve", lambda e, half=half, pv=pv: e.tensor_copy(hT[:, sl_, half * 8:(half + 1) * 8, :], pv.rearrange("p (c t) -> p c t", c=8)),
                                 reads=[pb.b()], writes=[hT.b(sl_)])

                pre(0)
                mid(0)
                misc = PB[5]
                mvw = misc[:].bitcast(BF16)
                pmk = PB[3]
                pmv = PB[4]

                def tail_a():
                    gate_cols(G1, misc, False, c0=256, stage=1)
                    for h in range(4):
                        S.op("act", lambda e, h=h: e.activation(Kp[:, h, :], pmk[:, h * 128:(h + 1) * 128], AF.Copy,
                                                                scale=G1["uT"][:, h:h + 1]),
                             reads=[pmk.b(), G1["uT"].b()], writes=[Kp.b(h)])

                def tail_b():
                    for h in range(4):
                        pc = PB[6 + (h % 2)]
                        S.op("pe", lambda e, h=h, pc=pc: e.matmul(pc[:, 0:257], lhsT=Kp[:, h, :], rhs=Vaug[:, h, :],
                                                                  start=True, stop=True),
                             reads=[Kp.b(h), Vaug.b(h)], writes=[pc.b()])
                        S.op("dve", lambda e, h=h, pc=pc: e.scalar_tensor_tensor(out=Cst[:, h, :], in0=Cst[:, h, :],
                                                                                 scalar=G1["abc"][:, h:h + 1], in1=pc[:, 0:257],
                                                                                 op0=ALU.mult, op1=ALU.add),
                             reads=[Cst.b(h), G1["abc"].b(), pc.b()], writes=[Cst.b(h)])

                pend = False
                for T in range(NT1):
                    own = T >= NPRE
                    slot = T % 2
                    if pend:
                        tail_a()
                    if T + 1 < NT1:
                        pre_a(T + 1)
                    pkv = PB[2]
                    for c in range(NCH):
                        S.op("pe", lambda e, c=c, slot=slot: e.matmul(pkv[:], lhsT=hT[:, slot, c, :], rhs=W_kv[:, c, :],
                                                            start=(c == 0), stop=(c == NCH - 1)),
                             reads=[hT.b(slot), W_kv.b()], writes=[pkv.b()])
                    S.op("dve", lambda e: e.tensor_copy(kbf[:], pkv[:, 0:256]), reads=[pkv.b()], writes=[kbf.b()])
                    S.op("act", lambda e, T=T: e.copy(Vsb[:, T, :], pkv[:, 256:512]), reads=[pkv.b()], writes=[Vsb.b(T)])
                    if own:
                        o = T - NPRE
                        S.op("act", lambda e, slot=slot: e.copy(kvo[:, slot, :], pkv[:]), reads=[pkv.b()], writes=[kvo.b(slot)])
                        S.dma("sp", lambda e, o=o, slot=slot: e.dma_start(out=k_out[o * 128:(o + 1) * 128, :], in_=kvo[:, slot, 0:256]),
                              reads=[kvo.b(slot)])
                        S.dma("sp", lambda e, o=o, slot=slot: e.dma_start(out=v_out[o * 128:(o + 1) * 128, :], in_=kvo[:, slot, 256:512]),
                              reads=[kvo.b(slot)])
                    pki = PB[3] if own else PB[7]
                    for c in range(NCH):
                        S.op("pe", lambda e, c=c, slot=slot, pki=pki: e.matmul(pki[:, 0:128], lhsT=W_ki[:, c, :], rhs=hT[:, slot, c, :],
                                                                    start=(c == 0), stop=(c == NCH - 1)),
                             reads=[hT.b(slot), W_ki.b()], writes=[pki.b()])
                    S.op("dve", lambda e, T=T, pki=pki: e.tensor_copy(KIT[:, T * 128:(T + 1) * 128], pki[:, 0:128]),
                         reads=[pki.b()], writes=[KIT.b(T)])
                    if T + 1 < NT1:
                        pre_b(T + 1)
                    for kvh in range(2):
                        S.op("pe", lambda e, kvh=kvh: e.transpose(mvw[:, 256 + kvh * 128:256 + (kvh + 1) * 128],
                                                                  kbf[:, kvh * 128:(kvh + 1) * 128], ident[:]),
                             reads=[kbf.b(), ident.b()], writes=[misc.b()])
                    S.op("act", lambda e, T=T: e.copy(KT[:, :, T * 128:(T + 1) * 128],
                                                      mvw[:, 256:512].rearrange("p (k t) -> p k t", k=2)),
                         reads=[misc.b()], writes=[KT.b(T)])
                    if own:
                        o = T - NPRE
                        for c in range(NCH):
                            S.op("pe", lambda e, c=c, slot=slot, pki=pki: e.matmul(pki[:, 128:192], lhsT=hT[:, slot, c, :], rhs=W_ki[:, c, 0:64],
                                                                        start=(c == 0), stop=(c == NCH - 1)),
                                 reads=[hT.b(slot), W_ki.b()], writes=[pki.b()])
                        S.op("act", lambda e, slot=slot, pki=pki: e.copy(kio[:, slot, :], pki[:, 128:192]),
                             reads=[pki.b()], writes=[kio.b(slot)])
                        S.dma("sp", lambda e, o=o, slot=slot: e.dma_start(out=ki_out[o * 128:(o + 1) * 128, :], in_=kio[:, slot, :]),
                              reads=[kio.b(slot)])
                        if pend:
                            tail_b()
                            pend = False
                        if T + 1 < NT1:
                            mid(T + 1)
                        continue
                    S.dma("sp", lambda e, T=T, slot=slot: e.dma_start(out=vrow[:, slot, :],
                                                                       in_=valid[:, T * 128:(T + 1) * 128].partition_broadcast(4)),
                          writes=[vrow.b(slot)])
                    S.op("dve", lambda e, slot=slot: e.tensor_scalar(nbrow[:, slot, :], vrow[:, slot, :], -1.0, -NEG,
                                                                     op0=ALU.add, op1=ALU.mult),
                         reads=[vrow.b(slot)], writes=[nbrow.b(slot)])
                    for c in range(NCH):
                        S.op("pe", lambda e, c=c, slot=slot: e.matmul(misc[0:4, 0:128], lhsT=W_g[:, c, 0:4], rhs=hT[:, slot, c, :],
                                                            start=(c == 0), stop=(c == NCH - 1)),
                             reads=[hT.b(slot), W_g.b()], writes=[misc.b()])
                    for c in range(NCH):
                        S.op("pe", lambda e, c=c, slot=slot: e.matmul(misc[0:4, 128:256], lhsT=W_g[:, c, 4:8], rhs=hT[:, slot, c, :],
                                                            start=(c == 0), stop=(c == NCH - 1)),
                             reads=[hT.b(slot), W_g.b()], writes=[misc.b()])
                    if pend:
                        tail_b()
                    if T + 1 < NT1:
                        mid(T + 1)
                    G1["vb"] = vrow.b(slot)
                    G1["nb"] = nbrow.b(slot)
                    gate_rows(misc[0:4, 0:128], misc[0:4, 128:256], misc.b(), misc.b(),
                              vrow[:, slot, :], nbrow[:, slot, :], G1)
                    gate_cols(G1, misc, False, c0=256, stage=0)

                    def mv_half(half):
                        for c in range(NCH):
                            S.op("pe", lambda e, c=c, slot=slot, half=half: e.matmul(pmv[:], lhsT=hT[:, slot, c, :],
                                                                          rhs=W_mv[:, c, half * 512:(half + 1) * 512],
                                                                          start=(c == 0), stop=(c == NCH - 1)),
                                 reads=[hT.b(slot), W_mv.b()], writes=[pmv.b()])
                        S.op("dve", lambda e, half=half: e.tensor_copy(Vaug[:, half * 2:half * 2 + 2, 0:256],
                                                                       pmv[:].rearrange("p (h v) -> p h v", h=2)),
                             reads=[pmv.b()], writes=[Vaug.b(half * 2), Vaug.b(half * 2 + 1)])

                    mv_half(0)
                    for c in range(NCH):
                        S.op("pe", lambda e, c=c, slot=slot: e.matmul(pmk[:], lhsT=hT[:, slot, c, :], rhs=W_mk[:, c, :],
                                                            start=(c == 0), stop=(c == NCH - 1)),
                             reads=[hT.b(slot), W_mk.b()], writes=[pmk.b()])
                    mv_half(1)
                    pend = True
                assert not pend

            S.barrier()
            def xrows2(t):
                return xk[(NPRE + t) * 128:(NPRE + t + 1) * 128, :] if t < NOWN else xs

            def build_hT2(stack, hT2):
                bst = contextlib.ExitStack()
                with bst:
                    nw_bc2 = sb("nw_bc2", [128, D], F32, bst)
                    xt2 = sb("xt2", [128, 2, D], F32, bst)
                    hbf2 = sb("hbf2", [128, 2, D], BF16, bst)
                    ssq2 = sb("ssq2", [128, 2, 4], F32, bst)
                    S.dma("sp", lambda e: e.dma_start(out=nw_bc2[:], in_=norm_w.partition_broadcast(128)), writes=[nw_bc2.b()])

                    def pre2(t):
                        sl_ = t % 2
                        S.dma("sp", lambda e: e.dma_start(out=xt2[:, sl_, :], in_=xrows2(t)), writes=[xt2.b(sl_)])
                        S.op("act", lambda e: e.activation(hbf2[:, sl_, :], xt2[:, sl_, :], AF.Square, accum_out=ssq2[:, sl_, 0:1]),
                             reads=[xt2.b(sl_)], writes=[hbf2.b(sl_), ssq2.b(sl_)])
                        S.op("act", lambda e: e.activation(ssq2[:, sl_, 1:2], ssq2[:, sl_, 0:1], AF.Sqrt, scale=1.0 / D, bias=epsb[:, 0:1]),
                             reads=[ssq2.b(sl_), epsb.b()], writes=[ssq2.b(sl_)])
                        S.op("dve", lambda e: e.reciprocal(ssq2[:, sl_, 2:3], ssq2[:, sl_, 1:2]), reads=[ssq2.b(sl_)], writes=[ssq2.b(sl_)])
                        S.op("dve", lambda e: e.scalar_tensor_tensor(out=hbf2[:, sl_, :], in0=xt2[:, sl_, :], scalar=ssq2[:, sl_, 2:3],
                                                                     in1=nw_bc2[:], op0=ALU.mult, op1=ALU.mult),
                             reads=[xt2.b(sl_), ssq2.b(sl_), nw_bc2.b()], writes=[hbf2.b(sl_)])

                    def mid2(t):
                        sl_ = t % 2
                        for half in range(2):
                            pb = PB[half]
                            pv = pb[:].bitcast(BF16)
                            for cc in range(8):
                                c = half * 8 + cc
                                S.op("pe", lambda e, c=c, cc=cc, pv=pv: e.transpose(pv[:, cc * 128:(cc + 1) * 128],
                                                                                    hbf2[:, sl_, c * 128:(c + 1) * 128], ident[:]),
                                     reads=[hbf2.b(sl_), ident.b()], writes=[pb.b()])
                            dst = hT2[:, half * 8:(half + 1) * 8, t * 128:(t + 1) * 128]
                            if half == 0:
                                S.op("act", lambda e, pv=pv, dst=dst: e.copy(dst, pv.rearrange("p (c t) -> p c t", c=8)),
                                     reads=[pb.b()], writes=[hT2.b(t)])
                            else:
                                S.op("dve", lambda e, pv=pv, dst=dst: e.tensor_copy(dst, pv.rearrange("p (c t) -> p c t", c=8)),
                                     reads=[pb.b()], writes=[hT2.b(t)])

                    pre2(0)
                    for t in range(NT2):
                        if t + 1 < NT2:
                            pre2(t + 1)
                        mid2(t)

            class Proj:
                def __init__(self, hT2, Wb, banks, src=None):
                    self.hT2, self.Wb, self.banks, self.k, self.slot = hT2, Wb, banks, 0, 0
                    self.src = src if src is not None else w_in_v

                def _bank(self):
                    pb = self.banks[self.k % len(self.banks)]
                    self.k += 1
                    return pb

                def load(self, c0, ncols):
                    self.slot ^= 1
                    sl = self.slot
                    for hh in range(2):
                        cs = slice(hh * 8, hh * 8 + 8)
                        S.dma("pool", lambda e, cs=cs, sl=sl: e.dma_start(out=self.Wb.t[:, sl, cs, 0:ncols],
                                                                           in_=self.src[:, cs, c0:c0 + ncols]),
                              writes=[self.Wb.b(sl)])
                    return sl

                def feat(self, c0, ncols, consume, psub=128, tiles=None):
                    sl = self.load(c0, ncols)
                    hT2, Wb = self.hT2, self.Wb
                    for j in range((ncols + psub - 1) // psub):
                        w = min(psub, ncols - j * psub)
                        for t0 in range(0, TOK2, 512):
                            n = min(512, TOK2 - t0)
                            pb = self._bank()
                            for c in range(NCH):
                                S.op("pe", lambda e, c=c, j=j, w=w, t0=t0, n=n, pb=pb, sl=sl: e.matmul(
                                    pb[0:w, 0:n], lhsT=Wb[:, sl, c, j * psub:j * psub + w], rhs=hT2[:, c, t0:t0 + n],
                                    start=(c == 0), stop=(c == NCH - 1)),
                                    reads=[Wb.b(sl)] + [hT2.b(tt) for tt in range(t0 // 128, (t0 + n) // 128)], writes=[pb.b()])
                            consume(j, t0, n, pb)

                def tok(self, c0, ncols, consume, tiles=None):
                    sl = self.load(c0, ncols)
                    hT2, Wb = self.hT2, self.Wb
                    for t in (tiles if tiles is not None else range(NT2)):
                        pb = self._bank()
                        for c in range(NCH):
                            S.op("pe", lambda e, c=c, t=t, pb=pb, sl=sl: e.matmul(
                                pb[:, 0:ncols], lhsT=hT2[:, c, t * 128:(t + 1) * 128], rhs=Wb[:, sl, c, 0:ncols],
                                start=(c == 0), stop=(c == NCH - 1)),
                                reads=[Wb.b(sl), hT2.b(t)], writes=[pb.b()])
                        consume(t, pb)

            _alt = [0]

            def evac(out_ap, in_ap, reads, writes, func=None):
                if func is not None:
                    S.op("act", lambda e: e.activation(out_ap, in_ap, func), reads=reads, writes=writes)
                    return
                _alt[0] ^= 1
                if _alt[0]:
                    S.op("act", lambda e: e.copy(out_ap, in_ap), reads=reads, writes=writes)
                else:
                    S.op("dve", lambda e: e.tensor_copy(out_ap, in_ap), reads=reads, writes=writes)

            def tview(ap, n):
                return ap.rearrange("p (t q) -> p t q", q=128)

            p2a = contextlib.ExitStack()
            with p2a:
                QT = sb("QT", [128, NT2, 8, 128], BF16, p2a)
                QIT = sb("QIT", [128, NT2, 8, 128], BF16, p2a)
                AZT = sb("AZT", [128, NT2, 8, 128], BF16, p2a)
                WI = sb("WI", [128, NT2, 16], F32, p2a)
                QIS = sb("QIS", [64, 4, 16], BF16, p2a)
                KIS = sb("KIS", [64, 4], BF16, p2a)
                KSD, VSD = Buf("ks_dram"), Buf("vs_dram")
                pj = contextlib.ExitStack()
                with pj:
                    hT2 = sb("hT2a", [128, NCH, TOK2], BF16, pj)
                    build_hT2(pj, hT2)
                    S.barrier()
                    Wb = sb("Wba", [128, 2, NCH, 512], BF16, pj)
                    P = Proj(hT2, Wb, [PB[2], PB[3], PB[4], PB[5]])

                    def to_tiles(dst, off):
                        def c(j, t0, n, pb):
                            evac(dst[:, t0 // 128:(t0 + n) // 128, off + j, :], tview(pb[:, 0:n], n),
                                 [pb.b()], [dst.b(tt) for tt in range(t0 // 128, (t0 + n) // 128)])
                        return c

                    def to_tiles_silu(dst, off):
                        def c(j, t0, n, pb):
                            evac(dst[:, t0 // 128:(t0 + n) // 128, off + j, :], tview(pb[:, 0:n], n),
                                 [pb.b()], [dst.b(tt) for tt in range(t0 // 128, (t0 + n) // 128)], func=AF.Silu)
                        return c

                    P.feat(C_AQ, 512, to_tiles(QT, 0))
                    P.feat(C_AQ + 512, 512, to_tiles(QT, 4))
                    P.feat(C_IQ, 512, to_tiles(QIT, 0))
                    P.feat(C_IQ + 512, 512, to_tiles(QIT, 4))
                    P.feat(C_AZ, 512, to_tiles_silu(AZT, 0))
                    P.feat(C_AZ + 512, 512, to_tiles_silu(AZT, 4))
                    P.tok(C_IW, 16, lambda t, pb: evac(WI[:, t, :], pb[:, 0:16], [pb.b()], [WI.b(t)]))
                    if NT2 > NOWN:
                        T8 = NOWN
                        srow = sb("srow", [4, 576], F32, pj)

                        def c_kv(t, pb):
                            evac(srow[:, 0:512], pb[0:4, 0:512], [pb.b()], [srow.b()])
                            S.dma("sp", lambda e: e.dma_start(out=ks_out, in_=srow[:, 0:256]), reads=[srow.b()], writes=[KSD])
                            S.dma("sp", lambda e: e.dma_start(out=vs_out, in_=srow[:, 256:512]), reads=[srow.b()], writes=[VSD])
                        P.tok(C_AK, 512, c_kv, tiles=[T8])

                        def c_ki(t, pb):
                            evac(srow[:, 512:576], pb[0:4, 0:64], [pb.b()], [srow.b()])
                            S.dma("sp", lambda e: e.dma_start(out=kis_out, in_=srow[:, 512:576]), reads=[srow.b()])
                        P.tok(C_IK, 64, c_ki, tiles=[T8])
                        sl = P.slot
                        pbq = PB[6]
                        for c in range(NCH):
                            S.op("pe", lambda e, c=c, sl=sl, pbq=pbq, Wb=Wb, hT2=hT2: e.matmul(pbq[0:64, 0:4], lhsT=Wb[:, sl, c, 0:64], rhs=hT2[:, c, T8 * 128:T8 * 128 + 4],
                                                                       start=(c == 0), stop=(c == NCH - 1)),
                                 reads=[Wb.b(sl), hT2.b(T8)], writes=[pbq.b()])
                        evac(KIS[:], pbq[0:64, 0:4], [pbq.b()], [KIS.b()])
                        for g in range(2):
                            sl = P.load(C_IQ + g * 512, 512)
                            pbq = PB[6 + g]
                            for hh in range(8):
                                for c in range(NCH):
                                    S.op("pe", lambda e, c=c, sl=sl, hh=hh, pbq=pbq, Wb=Wb, hT2=hT2: e.matmul(
                                        pbq[0:64, hh * 4:hh * 4 + 4], lhsT=Wb[:, sl, c, hh * 64:(hh + 1) * 64],
                                        rhs=hT2[:, c, T8 * 128:T8 * 128 + 4], start=(c == 0), stop=(c == NCH - 1)),
                                        reads=[Wb.b(sl), hT2.b(T8)], writes=[pbq.b()])
                            evac(QIS[:, :, g * 8:(g + 1) * 8], pbq[0:64, 0:32].rearrange("p (h s) -> p s h", s=4), [pbq.b()], [QIS.b()])

                S.barrier()
                at = contextlib.ExitStack()
                with at:
                    acc = sb("acc", [128, 2, NT1 * 128], F32, at)
                    junkb = sb("junkb", [128, NT1 * 128], BF16, at)
                    vbias = sb("vbias", [128, NT1 * 128], BF16, at)
                    maskT = sb("maskT", [128, 2, NT1, 128], BF16, at)
                    Rb = sb("Rb", [128, 4, 512], BF16, at)
                    Dg = sb("Dg", [128, 16, 128], BF16, at)
                    Eb = sb("Eb", [128, 4, 512], BF16, at)
                    PT = sb("PT", [128, 4, 512], BF16, at)
                    ftmp = sb("ftmp", [128, 2, 512], F32, at)
                    sm = sb("sm", [128, 8], F32, at)
                    wk = sb("wk", [128, NITER], F32, at)

                    S.dma("sp", lambda e: e.dma_start(out=acc[:, 0, :], in_=valid.partition_broadcast(128)), writes=[acc.b(0)])
                    S.op("dve", lambda e: e.tensor_scalar(vbias[:], acc[:, 0, :], -1.0, -NEG, op0=ALU.add, op1=ALU.mult),
                         reads=[acc.b(0)], writes=[vbias.b()])

                    def stage_A(i):
                        ab = i % 2
                        nkb = NPRE + 1 + i
                        NK = nkb * 128
                        for h in range(16):
                            S.op("dve", lambda e, h=h, i=i: e.tensor_scalar(Dg[:, h, :], ident[:], WI[:, i, h:h + 1], None, op0=ALU.mult),
                                 reads=[ident.b(), WI.b(i)], writes=[Dg.b()])
                        yield
                        kk = 0
                        for kc in range((NK + 511) // 512):
                            n = min(512, NK - kc * 512)
                            ksl = slice(kc * 512, kc * 512 + n)
                            pacc = PB[2]
                            pend = None
                            for h in range(16):
                                j, half = h // 2, h % 2
                                pb = PB[kk % 2]
                                rs = kk % 4
                                kk += 1
                                psl = slice(half * 64, half * 64 + 64)
                                S.op("pe", lambda e, pb=pb, j=j, psl=psl, ksl=ksl, n=n, i=i: e.matmul(
                                    pb[:, 0:n], lhsT=QIT[psl, i, j, :], rhs=KIT[psl, ksl], start=True, stop=True),
                                    reads=[QIT.b(i)] + [KIT.b(tt_) for tt_ in range(kc * 4, kc * 4 + n // 128)], writes=[pb.b()])
                                S.op("act", lambda e, pb=pb, rs=rs, n=n: e.activation(Rb[:, rs, 0:n], pb[:, 0:n], AF.Relu),
                                     reads=[pb.b()], writes=[Rb.b(rs)])
                                if pend is not None:
                                    ph, prs = pend
                                    S.op("pe", lambda e, ph=ph, prs=prs, n=n: e.matmul(pacc[:, 0:n], lhsT=Dg[:, ph, :], rhs=Rb[:, prs, 0:n],
                                                                                       start=(ph == 0), stop=False),
                                         reads=[Dg.b(), Rb.b(prs)], writes=[pacc.b()])
                                pend = (h, rs)
                            ph, prs = pend
                            S.op("pe", lambda e, ph=ph, prs=prs, n=n: e.matmul(pacc[:, 0:n], lhsT=Dg[:, ph, :], rhs=Rb[:, prs, 0:n],
                                                                               start=False, stop=True),
                                 reads=[Dg.b(), Rb.b(prs)], writes=[pacc.b()])
                            S.op("act", lambda e, ab=ab, ksl=ksl, n=n: e.copy(acc[:, ab, ksl], pacc[:, 0:n]), reads=[pacc.b()], writes=[acc.b(ab)])
                            yield

                    def stage_B(i):
                        ab = i % 2
                        nkb = NPRE + 1 + i
                        NK = nkb * 128
                        A = acc[:, ab, 0:NK]
                        ac = acc.b(ab)
                        S.op("dve", lambda e: e.tensor_reduce(sm[:, 0:1], A, AX.X, ALU.min), reads=[ac], writes=[sm.b()])
                        S.op("dve", lambda e: e.tensor_tensor(A, A, vbias[:, 0:NK], op=ALU.add), reads=[ac, vbias.b()], writes=[ac])
                        dsl = slice((nkb - 1) * 128, nkb * 128)
                        S.op("dve", lambda e: e.tensor_tensor(acc[:, ab, dsl], acc[:, ab, dsl], tribias[:], op=ALU.add),
                             reads=[ac, tribias.b()], writes=[ac])
                        S.op("dve", lambda e: e.tensor_reduce(sm[:, 1:2], A, AX.X, ALU.max), reads=[ac], writes=[sm.b()])
                        S.op("dve", lambda e: e.tensor_tensor(sm[:, 2:3], sm[:, 1:2], sm[:, 0:1], op=ALU.subtract), reads=[sm.b()], writes=[sm.b()])
                        S.op("dve", lambda e: e.tensor_scalar(sm[:, 2:3], sm[:, 2:3], 1.001, 1e-6, op0=ALU.mult, op1=ALU.add),
                             reads=[sm.b()], writes=[sm.b()])
                        S.op("dve", lambda e: e.tensor_scalar(wk[:], powr[:], sm[:, 2:3], None, op0=ALU.mult),
                             reads=[sm.b(), powr.b()], writes=[wk.b()])
                        S.op("dve", lambda e: e.tensor_tensor(sm[:, 3:4], sm[:, 0:1], wk[:, 0:1], op=ALU.add), reads=[sm.b(), wk.b()], writes=[sm.b()])
                        for k in range(NITER):
                            S.op("dve", lambda e: e.tensor_scalar(junkb[:, 0:NK], A, sm[:, 3:4], 0.0, op0=ALU.is_ge, op1=ALU.add,
                                                                  accum_out=sm[:, 4:5]),
                                 reads=[ac, sm.b()], writes=[junkb.b(), sm.b()])
                            S.op("dve", lambda e: e.tensor_scalar(sm[:, 5:6], sm[:, 4:5], 255.5, -0.5, op0=ALU.is_ge, op1=ALU.add),
                                 reads=[sm.b()], writes=[sm.b()])
                            S.op("dve", lambda e, k=k: e.scalar_tensor_tensor(out=sm[:, 3:4], in0=sm[:, 5:6], scalar=wk[:, k:k + 1],
                                                                              in1=sm[:, 3:4], op0=ALU.mult, op1=ALU.add),
                                 reads=[sm.b(), wk.b()], writes=[sm.b()])
                        S.op("dve", lambda e: e.scalar_tensor_tensor(out=sm[:, 6:7], in0=wk[:, NITER - 1:NITER], scalar=-0.5, in1=sm[:, 3:4],
                                                                     op0=ALU.mult, op1=ALU.add),
                             reads=[sm.b(), wk.b()], writes=[sm.b()])
                        S.op("dve", lambda e: e.tensor_scalar(junkb[:, 0:NK], A, sm[:, 6:7], None, op0=ALU.is_ge),
                             reads=[ac, sm.b()], writes=[junkb.b()])
                        yield
                        pm = PB[3]
                        pmv_ = pm[:].bitcast(BF16)
                        for g0 in range(0, nkb, 8):
                            g1 = min(nkb, g0 + 8)
                            for kb in range(g0, g1):
                                S.op("pe", lambda e, kb=kb, g0=g0: e.transpose(pmv_[:, (kb - g0) * 128:(kb - g0 + 1) * 128],
                                                                               junkb[:, kb * 128:(kb + 1) * 128], ident[:]),
                                     reads=[junkb.b(), ident.b()], writes=[pm.b()])
                            S.op("dve", lambda e, g0=g0, g1=g1: e.tensor_copy(maskT[:, ab, g0:g1, :],
                                                                              pmv_[:, 0:(g1 - g0) * 128].rearrange("p (k q) -> p k q", q=128)),
                                 reads=[pm.b()], writes=[maskT.b(ab)])

                    def stage_C(i):
                        ab = i % 2
                        nkb = NPRE + 1 + i
                        kk = 0
                        for kvh in range(2):
                            pO, pD = PB[6], PB[7]

                            def front(kb, kk):
                                pS = PB[4 + kk % 2]
                                es = kk % 4
                                S.op("pe", lambda e, pS=pS, kb=kb, kvh=kvh: e.matmul(
                                    pS[:, 0:512], lhsT=KT[:, kvh, kb * 128:(kb + 1) * 128],
                                    rhs=QT[:, i, kvh * 4:kvh * 4 + 4, :], start=True, stop=True),
                                    reads=[KT.b(kb), QT.b(i)], writes=[pS.b()])
                                S.op("act", lambda e, pS=pS, es=es: e.activation(Eb[:, es, :], pS[:, 0:512], AF.Exp, scale=float(128.0 ** -0.5)),
                                     reads=[pS.b()], writes=[Eb.b(es)])
                                S.op("pool", lambda e, es=es, kb=kb: e.tensor_tensor(
                                    PT[:, es, :].rearrange("p (h q) -> p h q", h=4), Eb[:, es, :].rearrange("p (h q) -> p h q", h=4),
                                    maskT[:, ab, kb:kb + 1, :].to_broadcast([128, 4, 128]), op=ALU.mult),
                                    reads=[Eb.b(es), maskT.b(ab)], writes=[PT.b(es)])

                            def back(kb, kk):
                                es = kk % 4
                                S.op("pe", lambda e, es=es, kb=kb, kvh=kvh: e.matmul(
                                    pO[:, 0:512], lhsT=Vsb[:, kb, kvh * 128:(kvh + 1) * 128], rhs=PT[:, es, :],
                                    start=(kb == 0), stop=(kb == nkb - 1)),
                                    reads=[Vsb.b(kb), PT.b(es)], writes=[pO.b()])
                                S.op("pe", lambda e, es=es, kb=kb: e.matmul(
                                    pD[:, 0:512], lhsT=ones_bf[:], rhs=PT[:, es, :],
                                    start=(kb == 0), stop=(kb == nkb - 1)),
                                    reads=[ones_bf.b(), PT.b(es)], writes=[pD.b()])

                            DEPTH = 2
                            for kb in range(min(DEPTH, nkb)):
                                front(kb, kk + kb)
                            for kb in range(nkb):
                                if kb + DEPTH < nkb:
                                    front(kb + DEPTH, kk + kb + DEPTH)
                                back(kb, kk + kb)
                                if kb % 2 == 1:
                                    yield
                            kk += nkb
                            S.op("act", lambda e: e.activation(ftmp[:, 0, :], pD[:, 0:512], AF.Ln), reads=[pD.b()], writes=[ftmp.b(0)])
                            S.op("act", lambda e: e.activation(ftmp[:, 0, :], ftmp[:, 0, :], AF.Exp, scale=-1.0), reads=[ftmp.b(0)], writes=[ftmp.b(0)])
                            S.op("act", lambda e: e.copy(ftmp[:, 1, :], pO[:, 0:512]), reads=[pO.b()], writes=[ftmp.b(1)])
                            S.op("pool", lambda e: e.tensor_tensor(ftmp[:, 1, :], ftmp[:, 1, :], ftmp[:, 0, :], op=ALU.mult),
                                 reads=[ftmp.b(0), ftmp.b(1)], writes=[ftmp.b(1)])
                            S.op("pool", lambda e, kvh=kvh: e.tensor_tensor(
                                ATT_T[:, i, kvh * 4:kvh * 4 + 4, :], ftmp[:, 1, :].rearrange("p (h q) -> p h q", h=4),
                                AZT[:, i, kvh * 4:kvh * 4 + 4, :], op=ALU.mult),
                                reads=[ftmp.b(1), AZT.b(i)], writes=[ATT_T.b(i)])
                            yield

                    def run_all(g):
                        for _ in g:
                            pass

                    def interleave(*gens):
                        gens = list(gens)
                        while gens:
                            for g in list(gens):
                                try:
                                    next(g)
                                except StopIteration:
                                    gens.remove(g)

                    run_all(stage_A(0))
                    if NOWN > 1:
                        run_all(stage_A(1))
                    run_all(stage_B(0))
                    for i in range(NOWN):
                        gA = stage_A(i + 2) if i + 2 < NOWN else iter(())
                        gB = stage_B(i + 1) if i + 1 < NOWN else iter(())
                        next(gA, None)
                        next(gB, None)
                        interleave(gA, stage_C(i))
                        run_all(gB)

                if NT2 > NOWN:
                    S.barrier()
                    sa = contextlib.ExitStack()
                    with sa:
                        T8 = NOWN
                        SC_ATT = float(128.0 ** -0.5)
                        ikp = sb("ikp", [128, 2, 2048], F32, sa)
                        kT = sb("kT", [64, 64, 128], BF16, sa)
                        Self = sb("Self", [4, 4, 128], F32, sa)
                        Wb16 = sb("Wb16", [128, 4, 16], F32, sa)
                        stmp = sb("stmp", [128, 512], F32, sa)
                        Isc = sb("Isc", [128, 4, 129], F32, sa)
                        cm = sb("cm", [128, 4, 129], F32, sa)
                        Mk = sb("Mk", [128, 4, 129], F32, sa)
                        pt_i = sb("pt_i", [128, 4], I32, sa)
                        pt_t = sb("pt_t", [128, 2, 4], I32, sa)
                        pt2 = sb("pt2", [128, 4, 4], I32, sa)
                        pt_f = sb("pt_f", [128, 2, 4], F32, sa)
                        iot = sb("iot", [128, 17 + 16 + 128], F32, sa)
                        sut = sb("sut", [128, 128], F32, sa)
                        mm = sb("mm", [128, 8], F32, sa)
                        g4 = sb("g4", [4, 4], F32, sa)
                        R8 = sb("R8", [4, 8], F32, sa)
                        LW = sb("LW", [128, 8], F32, sa)
                        tt = sb("tt", [128, 4], F32, sa)
                        cn = sb("cn", [128, 4], F32, sa)
                        sw = sb("sw", [128, 4], F32, sa)
                        incl = sb("incl", [128, 128], F32, sa)
                        rank = sb("rank", [128, 128], F32, sa)
                        hi_ = sb("hi_", [128, 128], F32, sa)
                        lo_ = sb("lo_", [128, 128], F32, sa)
                        GE = sb("GE", [128, 128, 17], BF16, sa)
                        Aoh = sb("Aoh", [128, 128, 16], BF16, sa)
                        Boh = sb("Boh", [128, 128, 16], BF16, sa)
                        Bv3 = sb("Bv3", [128, 128, 3, 16], BF16, sa)
                        dg_sb = sb("dg_sb", [16, 48], F32, sa)
                        r16 = sb("r16", [16, 16], F32, sa)
                        i16 = sb("i16", [16, 16], I32, sa)
                        i128 = sb("i128", [128, 4, 2], I32, sa)
                        kg = sb("kg", [128, 3, 256], BF16, sa)
                        vg = sb("vg", [128, 3, 256], BF16, sa)
                        KgT = sb("KgT", [128, 2, 384], BF16, sa)
                        BIAS = sb("BIAS", [4, 384], F32, sa)
                        lg = sb("lg", [4, 2, 384], F32, sa)
                        pbf = sb("pbf", [4, 2, 384], BF16, sa)
                        s4 = sb("s4", [4, 8], F32, sa)
                        PTs = sb("PTs", [128, 2, 3, 4], BF16, sa)
                        SCRD = Buf("scr_dram")

                        S.dma("sp", lambda e: e.dma_start(out=pt_i[:], in_=ptabT), writes=[pt_i.b()])
                        S.dma("sp", lambda e: e.dma_start(out=iot[:], in_=c_iota.partition_broadcast(128)), writes=[iot.b()])
                        S.dma("sp", lambda e: e.dma_start(out=sut[:], in_=c_sut), writes=[sut.b()])
                        S.op("dve", lambda e: e.tensor_scalar(pt_t[:, 0, :], pt_i[:], 127, None, op0=ALU.bitwise_and), reads=[pt_i.b()], writes=[pt_t.b()])
                        S.op("dve", lambda e: e.tensor_scalar(pt_t[:, 1, :], pt_i[:], 7, None, op0=ALU.arith_shift_right), reads=[pt_i.b()], writes=[pt_t.b()])
                        S.op("dve", lambda e: e.tensor_copy(pt_f[:], pt_t[:]), reads=[pt_t.b()], writes=[pt_f.b()])
                        for rq in range(4):
                            S.op("dve", lambda e, rq=rq: e.tensor_scalar(pt2[:, rq, :], pt_i[:], 4.0, float(rq), op0=ALU.mult, op1=ALU.add),
                                 reads=[pt_i.b()], writes=[pt2.b()])
                        S.op("dve", lambda e: e.memset(Isc[:, :, 128:129], NEG), writes=[Isc.b()])
                        S.op("dve", lambda e: e.memset(BIAS[:], 0.0), writes=[BIAS.b()])
                        S.op("dve", lambda e: e.memset(BIAS[:, 257:384], NEG), reads=[BIAS.b()], writes=[BIAS.b()])
                        S.op("pool", lambda e: e.memset(kg[:, 2, :], 0.0), writes=[kg.b()])
                        S.op("pool", lambda e: e.memset(vg[:, 2, :], 0.0), writes=[vg.b()])
                        for s_ in range(4):
                            S.op("dve", lambda e, s_=s_: e.tensor_copy(Self[:, s_, :], identf[0:4, s_:s_ + 1].to_broadcast([4, 128])),
                                 reads=[identf.b()], writes=[Self.b()])
                        for s_ in range(4):
                            pbw = PB[s_ % 2]
                            S.op("pe", lambda e, s_=s_, pbw=pbw: e.matmul(pbw[:, 0:16], lhsT=Self[:, s_, :], rhs=WI[0:4, T8, :], start=True, stop=True),
                                 reads=[Self.b(), WI.b(T8)], writes=[pbw.b()])
                            evac(Wb16[:, s_, :], pbw[:, 0:16], [pbw.b()], [Wb16.b()])

                        kk = 0
                        for s_ in range(4):
                            for rq in range(4):
                                hb = rq % 2
                                S.dma("pool", lambda e, s_=s_, rq=rq, hb=hb: e.indirect_dma_start(
                                    out=ikp[:, hb, :], out_offset=None, in_=cidx.rearrange("p (h f) -> (p h) f", h=4),
                                    in_offset=bass.IndirectOffsetOnAxis(ap=pt2[:, rq, s_:s_ + 1], axis=0)),
                                    reads=[pt2.b()], writes=[ikp.b(hb)])
                                for r0 in range(0, 32, 4):
                                    pbT = PB[kk % 2]
                                    kk += 1
                                    for r in range(r0, r0 + 4):
                                        S.op("pe", lambda e, r=r, r0=r0, pbT=pbT, hb=hb: e.transpose(pbT[0:64, (r - r0) * 128:(r - r0 + 1) * 128],
                                                                                                    ikp[:, hb, r * 64:(r + 1) * 64], identf[:]),
                                             reads=[ikp.b(hb), identf.b()], writes=[pbT.b()])
                                    evac(kT[:, hb * 32 + r0:hb * 32 + r0 + 4, :], pbT[0:64, 0:512].rearrange("p (r q) -> p r q", q=128),
                                         [pbT.b()], [kT.b(hb)])
                                pbS = PB[2 + (kk % 2)]
                                kk += 1
                                for r in range(32):
                                    S.op("pe", lambda e, r=r, pbS=pbS, s_=s_, hb=hb: e.matmul(pbS[:, r * 16:(r + 1) * 16], lhsT=kT[:, hb * 32 + r, :],
                                                                                             rhs=QIS[:, s_, :], start=True, stop=True),
                                         reads=[kT.b(hb), QIS.b()], writes=[pbS.b()])
                                S.op("dve", lambda e, pbS=pbS, s_=s_: e.scalar_tensor_tensor(
                                    out=stmp[:].rearrange("p (r h) -> p r h", h=16), in0=pbS[:, 0:512].rearrange("p (r h) -> p r h", h=16),
                                    scalar=0.0, in1=Wb16[:, s_:s_ + 1, :].to_broadcast([128, 32, 16]), op0=ALU.max, op1=ALU.mult),
                                    reads=[pbS.b(), Wb16.b()], writes=[stmp.b()])
                                c0 = rq * 32
                                S.op("dve", lambda e, s_=s_, c0=c0: e.tensor_reduce(Isc[:, s_, c0:c0 + 32], stmp[:].rearrange("p (r h) -> p r h", h=16),
                                                                                    AX.X, ALU.add),
                                     reads=[stmp.b()], writes=[Isc.b()])
                            pbn = PB[4]
                            S.op("pe", lambda e, s_=s_: e.matmul(pbn[0:1, 0:16], lhsT=KIS[:, s_:s_ + 1], rhs=QIS[:, s_, :], start=True, stop=True),
                                 reads=[KIS.b(), QIS.b()], writes=[pbn.b()])
                            S.op("dve", lambda e, s_=s_: e.scalar_tensor_tensor(out=stmp[0:1, 0:16], in0=pbn[0:1, 0:16], scalar=0.0, in1=Wb16[0:1, s_, :],
                                                                                op0=ALU.max, op1=ALU.mult),
                                 reads=[pbn.b(), Wb16.b(), stmp.b()], writes=[stmp.b()])
                            S.op("dve", lambda e, s_=s_: e.tensor_reduce(Isc[0:1, s_, 128:129], stmp[0:1, 0:16], AX.X, ALU.add),
                                 reads=[stmp.b(), Isc.b()], writes=[Isc.b()])

                        S.op("dve", lambda e: e.tensor_reduce(mm[:, 0:4], Isc[:, :, 0:128], AX.X, ALU.min), reads=[Isc.b()], writes=[mm.b()])
                        S.op("dve", lambda e: e.tensor_reduce(mm[:, 4:8], Isc[:], AX.X, ALU.max), reads=[Isc.b(), mm.b()], writes=[mm.b()])
                        for w_, op_ in ((0, ALU.min), (1, ALU.max)):
                            pbm = PB[w_]
                            S.op("pe", lambda e, w_=w_, pbm=pbm: e.matmul(pbm[0:4, 0:128], lhsT=mm[:, w_ * 4:(w_ + 1) * 4], rhs=identf[:], start=True, stop=True),
                                 reads=[mm.b(), identf.b()], writes=[pbm.b()])
                            S.op("dve", lambda e, w_=w_, pbm=pbm, op_=op_: e.tensor_reduce(g4[:, w_:w_ + 1], pbm[0:4, 0:128], AX.X, op_),
                                 reads=[pbm.b(), g4.b()], writes=[g4.b()])
                        S.op("dve", lambda e: e.tensor_tensor(g4[:, 2:3], g4[:, 1:2], g4[:, 0:1], op=ALU.subtract), reads=[g4.b()], writes=[g4.b()])
                        S.op("dve", lambda e: e.tensor_scalar(g4[:, 2:3], g4[:, 2:3], 1.001, 1e-6, op0=ALU.mult, op1=ALU.add), reads=[g4.b()], writes=[g4.b()])
                        S.op("dve", lambda e: e.tensor_scalar(R8[:, 0:4], identf[0:4, 0:4], g4[:, 0:1], None, op0=ALU.mult), reads=[g4.b(), identf.b()], writes=[R8.b()])
                        S.op("dve", lambda e: e.tensor_scalar(R8[:, 4:8], identf[0:4, 0:4], g4[:, 2:3], None, op0=ALU.mult), reads=[g4.b(), identf.b(), R8.b()], writes=[R8.b()])
                        pbm = PB[2]
                        S.op("pe", lambda e: e.matmul(pbm[:, 0:8], lhsT=ones4f[:], rhs=R8[:], start=True, stop=True), reads=[ones4f.b(), R8.b()], writes=[pbm.b()])
                        S.op("dve", lambda e: e.tensor_copy(LW[:], pbm[:, 0:8]), reads=[pbm.b()], writes=[LW.b()])
                        S.op("dve", lambda e: e.scalar_tensor_tensor(out=tt[:], in0=LW[:, 4:8], scalar=0.5, in1=LW[:, 0:4], op0=ALU.mult, op1=ALU.add),
                             reads=[LW.b()], writes=[tt.b()])
                        for k in range(NITER):
                            S.op("dve", lambda e: e.tensor_tensor(cm[:], Isc[:], tt[:].unsqueeze(2).to_broadcast([128, 4, 129]), op=ALU.is_ge),
                                 reads=[Isc.b(), tt.b()], writes=[cm.b()])
                            S.op("dve", lambda e: e.tensor_reduce(cn[:], cm[:], AX.X, ALU.add), reads=[cm.b()], writes=[cn.b()])
                            pbc = PB[3 + (k % 2)]
                            S.op("pe", lambda e, pbc=pbc: e.matmul(pbc[:, 0:4], lhsT=identf_ones[:], rhs=cn[:], start=True, stop=True),
                                 reads=[identf_ones.b(), cn.b()], writes=[pbc.b()])
                            S.op("dve", lambda e, pbc=pbc: e.tensor_scalar(sw[:], pbc[:, 0:4], 255.5, -0.5, op0=ALU.is_ge, op1=ALU.add),
                                 reads=[pbc.b()], writes=[sw.b()])
                            S.op("dve", lambda e: e.tensor_tensor(sw[:], sw[:], LW[:, 4:8], op=ALU.mult), reads=[sw.b(), LW.b()], writes=[sw.b()])
                            S.op("dve", lambda e, k=k: e.scalar_tensor_tensor(out=tt[:], in0=sw[:], scalar=float(0.5 ** (k + 1)), in1=tt[:],
                                                                              op0=ALU.mult, op1=ALU.add),
                                 reads=[sw.b(), tt.b()], writes=[tt.b()])
                        S.op("dve", lambda e: e.scalar_tensor_tensor(out=tt[:], in0=LW[:, 4:8], scalar=float(-(0.5 ** (NITER + 1))), in1=tt[:],
                                                                     op0=ALU.mult, op1=ALU.add),
                             reads=[LW.b(), tt.b()], writes=[tt.b()])
                        S.op("dve", lambda e: e.tensor_tensor(Mk[:], Isc[:], tt[:].unsqueeze(2).to_broadcast([128, 4, 129]), op=ALU.is_ge),
                             reads=[Isc.b(), tt.b()], writes=[Mk.b()])

                        i17 = iot[:, 0:17]
                        i16v = iot[:, 17:33]
                        icol = iot[:, 33:161]
                        def sel(s_):
                            Ms = Mk[:, s_, 0:128]
                            S.op("dve", lambda e, Ms=Ms: e.tensor_tensor_scan(incl[:], identf_ones[:], Ms, 0.0, op0=ALU.mult, op1=ALU.add),
                                 reads=[identf_ones.b(), Mk.b()], writes=[incl.b()])
                            pbo_ = PB[0]
                            S.op("pe", lambda e: e.matmul(pbo_[:, 0:1], lhsT=sut[:], rhs=incl[:, 127:128], start=True, stop=True),
                                 reads=[sut.b(), incl.b()], writes=[pbo_.b()])
                            S.op("dve", lambda e, Ms=Ms: e.tensor_tensor(rank[:], incl[:], Ms, op=ALU.subtract), reads=[incl.b(), Mk.b()], writes=[rank.b()])
                            S.op("dve", lambda e: e.tensor_copy(cn[:, 0:1], pbo_[:, 0:1]), reads=[pbo_.b(), cn.b()], writes=[cn.b()])
                            S.op("dve", lambda e: e.tensor_scalar(rank[:], rank[:], cn[:, 0:1], 1.0, op0=ALU.add, op1=ALU.add) if False else
                                 e.tensor_scalar(rank[:], rank[:], cn[:, 0:1], None, op0=ALU.add),
                                 reads=[rank.b(), cn.b()], writes=[rank.b()])
                            S.op("dve", lambda e, Ms=Ms: e.scalar_tensor_tensor(out=rank[:], in0=rank[:], scalar=1.0, in1=Ms, op0=ALU.add, op1=ALU.mult),
                                 reads=[rank.b(), Mk.b()], writes=[rank.b()])
                            S.op("dve", lambda e: e.tensor_scalar(rank[:], rank[:], -1.0, None, op0=ALU.add), reads=[rank.b()], writes=[rank.b()])
                            S.op("dve", lambda e: e.tensor_tensor(GE[:], rank[:].unsqueeze(2).to_broadcast([128, 128, 17]),
                                                                  i17.unsqueeze(1).to_broadcast([128, 128, 17]), op=ALU.is_ge),
                                 reads=[rank.b(), iot.b()], writes=[GE.b()])
                            S.op("pool", lambda e: e.tensor_tensor(Aoh[:], GE[:, :, 0:16], GE[:, :, 1:17], op=ALU.subtract), reads=[GE.b()], writes=[Aoh.b()])
                            S.op("dve", lambda e: e.tensor_reduce(hi_[:], GE[:, :, 1:17], AX.X, ALU.add), reads=[GE.b()], writes=[hi_.b()])
                            S.op("dve", lambda e: e.scalar_tensor_tensor(out=lo_[:], in0=hi_[:], scalar=-16.0, in1=rank[:], op0=ALU.mult, op1=ALU.add),
                                 reads=[hi_.b(), rank.b()], writes=[lo_.b()])
                            S.op("dve", lambda e: e.tensor_tensor(Boh[:], lo_[:].unsqueeze(2).to_broadcast([128, 128, 16]),
                                                                  i16v.unsqueeze(1).to_broadcast([128, 128, 16]), op=ALU.is_equal),
                                 reads=[lo_.b(), iot.b()], writes=[Boh.b()])
                            S.op("pool", lambda e: e.tensor_tensor(Bv3[:, :, 0, :], Boh[:], icol.unsqueeze(2).to_broadcast([128, 128, 16]), op=ALU.mult),
                                 reads=[Boh.b(), iot.b()], writes=[Bv3.b()])
                            S.op("dve", lambda e, s_=s_: e.tensor_scalar(Bv3[:, :, 1, :], Boh[:], pt_f[:, 0, s_:s_ + 1], None, op0=ALU.mult),
                                 reads=[Boh.b(), pt_f.b(), Bv3.b()], writes=[Bv3.b()])
                            S.op("dve", lambda e, s_=s_: e.tensor_scalar(Bv3[:, :, 2, :], Boh[:], pt_f[:, 1, s_:s_ + 1], None, op0=ALU.mult),
                                 reads=[Boh.b(), pt_f.b(), Bv3.b()], writes=[Bv3.b()])
                            pbi = PB[1]
                            for col in range(128):
                                S.op("pe", lambda e, col=col: e.matmul(pbi[0:16, 0:48], lhsT=Aoh[:, col, :], rhs=Bv3[:, col, :, :].rearrange("p k s -> p (k s)"),
                                                                       start=(col == 0), stop=(col == 127)),
                                     reads=[Aoh.b(), Bv3.b()], writes=[pbi.b()])
                            S.op("dve", lambda e: e.tensor_copy(dg_sb[:], pbi[0:16, 0:48]), reads=[pbi.b()], writes=[dg_sb.b()])
                            S.op("dve", lambda e: e.scalar_tensor_tensor(out=r16[:], in0=dg_sb[:, 32:48], scalar=128.0, in1=dg_sb[:, 16:32], op0=ALU.mult, op1=ALU.add),
                                 reads=[dg_sb.b()], writes=[r16.b()])
                            S.op("dve", lambda e: e.scalar_tensor_tensor(out=r16[:], in0=r16[:], scalar=128.0, in1=dg_sb[:, 0:16], op0=ALU.mult, op1=ALU.add),
                                 reads=[dg_sb.b(), r16.b()], writes=[r16.b()])
                            S.op("dve", lambda e: e.tensor_copy(i16[:], r16[:]), reads=[r16.b()], writes=[i16.b()])
                            S.dma("sp", lambda e, s_=s_: e.dma_start(out=scr[s_:s_ + 1, :].rearrange("o (a b) -> (o a) b", a=16), in_=i16[:]),
                                  reads=[i16.b()], writes=[SCRD])
                            S.dma("sp", lambda e, s_=s_: e.dma_start(out=i128[:, s_, :], in_=scr[s_:s_ + 1, :].rearrange("o (h p) -> p (o h)", p=128),
                                                                     allow_slow_non_contiguous=True),
                                  reads=[SCRD], writes=[i128.b(s_)])
                        def gath(s_):
                            for hf in range(2):
                                S.dma("pool", lambda e, s_=s_, hf=hf: e.indirect_dma_start(
                                    out=kg[:, hf, :], out_offset=None, in_=ck,
                                    in_offset=bass.IndirectOffsetOnAxis(ap=i128[:, s_, hf:hf + 1], axis=0)),
                                    reads=[i128.b(s_)], writes=[kg.b()])
                                S.dma("pool", lambda e, s_=s_, hf=hf: e.indirect_dma_start(
                                    out=vg[:, hf, :], out_offset=None, in_=cv,
                                    in_offset=bass.IndirectOffsetOnAxis(ap=i128[:, s_, hf:hf + 1], axis=0)),
                                    reads=[i128.b(s_)], writes=[vg.b()])
                            S.dma("pool", lambda e, s_=s_: e.dma_start(out=kg[0:1, 2, :], in_=ks_out[s_:s_ + 1, :]), reads=[KSD], writes=[kg.b()])
                            S.dma("pool", lambda e, s_=s_: e.dma_start(out=vg[0:1, 2, :], in_=vs_out[s_:s_ + 1, :]), reads=[VSD], writes=[vg.b()])

                        def att(s_):
                            pbk = PB[2]
                            pbkv = pbk[:].bitcast(BF16)
                            for kvh in range(2):
                                for tl in range(3):
                                    S.op("pe", lambda e, kvh=kvh, tl=tl: e.transpose(pbkv[:, (kvh * 3 + tl) * 128:(kvh * 3 + tl + 1) * 128],
                                                                                     kg[:, tl, kvh * 128:(kvh + 1) * 128], ident[:]),
                                         reads=[kg.b(), ident.b()], writes=[pbk.b()])
                            evac(KgT[:].rearrange("p k n -> p (k n)"), pbkv[:, 0:768], [pbk.b()], [KgT.b()])
                            pb4 = PB[3]
                            S.op("pe", lambda e, s_=s_: e.matmul(pb4[0:4, 0:1], lhsT=identf_ones[:, 0:4], rhs=Mk[:, s_, 128:129], start=True, stop=True),
                                 reads=[identf_ones.b(), Mk.b()], writes=[pb4.b()])
                            S.op("dve", lambda e: e.tensor_scalar(BIAS[:, 255:256], pb4[0:4, 0:1], NEG, None, op0=ALU.mult), reads=[pb4.b(), BIAS.b()], writes=[BIAS.b()])
                            S.op("dve", lambda e: e.tensor_scalar(BIAS[:, 256:257], pb4[0:4, 0:1], -1.0, -NEG, op0=ALU.add, op1=ALU.mult),
                                 reads=[pb4.b(), BIAS.b()], writes=[BIAS.b()])
                            for kvh in range(2):
                                pbl = PB[4 + kvh]
                                S.op("pe", lambda e, kvh=kvh, s_=s_, pbl=pbl: e.matmul(pbl[0:4, 0:384], lhsT=QT[:, T8, kvh * 4:kvh * 4 + 4, s_], rhs=KgT[:, kvh, :],
                                                                                      start=True, stop=True),
                                     reads=[QT.b(T8), KgT.b()], writes=[pbl.b()])
                                S.op("dve", lambda e, kvh=kvh, pbl=pbl: e.scalar_tensor_tensor(out=lg[:, kvh, :], in0=pbl[0:4, 0:384], scalar=SC_ATT, in1=BIAS[:],
                                                                                               op0=ALU.mult, op1=ALU.add),
                                     reads=[pbl.b(), BIAS.b()], writes=[lg.b()])
                                S.op("dve", lambda e, kvh=kvh: e.tensor_reduce(s4[:, kvh:kvh + 1], lg[:, kvh, :], AX.X, ALU.max), reads=[lg.b()], writes=[s4.b()])
                                S.op("dve", lambda e, kvh=kvh: e.tensor_scalar(s4[:, 2 + kvh:3 + kvh], s4[:, kvh:kvh + 1], -1.0, None, op0=ALU.mult),
                                     reads=[s4.b()], writes=[s4.b()])
                                S.op("act", lambda e, kvh=kvh: e.activation(lg[:, kvh, :], lg[:, kvh, :], AF.Exp, bias=s4[:, 2 + kvh:3 + kvh],
                                                                            accum_out=s4[:, 4 + kvh:5 + kvh]),
                                     reads=[lg.b(), s4.b()], writes=[lg.b(), s4.b()])
                                S.op("dve", lambda e, kvh=kvh: e.reciprocal(s4[:, 6 + kvh:7 + kvh], s4[:, 4 + kvh:5 + kvh]), reads=[s4.b()], writes=[s4.b()])
                                S.op("dve", lambda e, kvh=kvh: e.tensor_scalar(pbf[:, kvh, :], lg[:, kvh, :], s4[:, 6 + kvh:7 + kvh], None, op0=ALU.mult),
                                     reads=[lg.b(), s4.b()], writes=[pbf.b()])
                                pbp = PB[6]
                                pbpv = pbp[:].bitcast(BF16)
                                for tl in range(3):
                                    S.op("pe", lambda e, kvh=kvh, tl=tl: e.transpose(pbpv[:, (kvh * 3 + tl) * 4:(kvh * 3 + tl) * 4 + 4],
                                                                                     pbf[:, kvh, tl * 128:(tl + 1) * 128], ident[0:4, 0:4]),
                                         reads=[pbf.b(), ident.b()], writes=[pbp.b()])
                                evac(PTs[:, kvh, :, :].rearrange("p t h -> p (t h)"), pbpv[:, kvh * 12:kvh * 12 + 12], [pbp.b()], [PTs.b()])
                                pbO = PB[7]
                                for tl in range(3):
                                    S.op("pe", lambda e, kvh=kvh, tl=tl: e.matmul(pbO[:, kvh * 4:kvh * 4 + 4], lhsT=vg[:, tl, kvh * 128:(kvh + 1) * 128],
                                                                                  rhs=PTs[:, kvh, tl, :], start=(tl == 0), stop=(tl == 2)),
                                         reads=[vg.b(), PTs.b()], writes=[pbO.b()])
                                S.op("dve", lambda e, kvh=kvh, s_=s_: e.tensor_tensor(ATT_T[:, T8, kvh * 4:kvh * 4 + 4, s_], pbO[:, kvh * 4:kvh * 4 + 4],
                                                                                      AZT[:, T8, kvh * 4:kvh * 4 + 4, s_], op=ALU.mult),
                                     reads=[pbO.b(), AZT.b(T8), ATT_T.b(T8)], writes=[ATT_T.b(T8)])
                        sel(0)
                        gath(0)
                        for s_ in range(4):
                            if s_ + 1 < 4:
                                sel(s_ + 1)
                            att(s_)
                            if s_ + 1 < 4:
                                gath(s_ + 1)
        S.barrier()
        ML_T = sb("ML_T", [128, NT2, 8, 128], BF16)
        if NT2 > NOWN:
            S.op("pool", lambda e: e.memset(ML_T[:, NOWN, :, :], 0.0), writes=[ML_T.b(NOWN)])
        p2b = contextlib.ExitStack()
        with p2b:
            MQT = sb("MQT", [128, NT2, 4, 128], BF16, p2b)
            MKT = sb("MKT", [128, NT2, 4, 128], BF16, p2b)
            MK = sb("MK", [128, NT2, 512], BF16, p2b)
            MV = sb("MV", [128, NT2, 1024], BF16, p2b)
            MOZ = sb("MOZ", [128, NT2, 1024], BF16, p2b)
            GI = sb("GI", [4, TOK2], F32, p2b)
            GF = sb("GF", [4, TOK2], F32, p2b)
            mlnw_bc = sb("mlnw_bc", [128, 1024], F32, p2b)
            MQS = sb("MQS", [4, 512], BF16, p2b)
            GS = sb("GS", [4, 8], F32, p2b)
            S.dma("sp", lambda e: e.dma_start(out=mlnw_bc[:], in_=mlnorm_w.partition_broadcast(128)), writes=[mlnw_bc.b()])
            pj = contextlib.ExitStack()
            with pj:
                hT2 = sb("hT2b", [128, NCH, TOK2], BF16, pj)
                build_hT2(pj, hT2)
                S.barrier()
                Wb = sb("Wbb", [128, 2, NCH, 512], BF16, pj)
                ztmp = sb("ztmp", [128, 2, 512], BF16, pj)
                P = Proj(hT2, Wb, [PB[2], PB[3], PB[4], PB[5]])

                def to_tiles4(dst):
                    def c(j, t0, n, pb):
                        evac(dst[:, t0 // 128:(t0 + n) // 128, j, :], tview(pb[:, 0:n], n),
                             [pb.b()], [dst.b(tt) for tt in range(t0 // 128, (t0 + n) // 128)])
                    return c

                P.feat(C_MQ, 512, to_tiles4(MQT))
                P.feat(C_MK, 512, to_tiles4(MKT))
                P.tok(C_MK, 512, lambda t, pb: evac(MK[:, t, :], pb[:, 0:512], [pb.b()], [MK.b(t)]))
                for hf in range(2):
                    P.tok(C_MV + hf * 512, 512, lambda t, pb, hf=hf: evac(MV[:, t, hf * 512:(hf + 1) * 512], pb[:, 0:512],
                                                                           [pb.b()], [MV.b(t)]))
                for hf in range(2):
                    P.tok(C_MO + hf * 512, 512, lambda t, pb, hf=hf: evac(MOZ[:, t, hf * 512:(hf + 1) * 512], pb[:, 0:512],
                                                                           [pb.b()], [MOZ.b(t)], func=AF.Sigmoid))
                for hf in range(2):
                    def cz(t, pb, hf=hf):
                        zs = t % 2
                        S.op("act", lambda e: e.activation(ztmp[:, zs, :], pb[:, 0:512], AF.Silu), reads=[pb.b()], writes=[ztmp.b(zs)])
                        S.op("pool", lambda e: e.tensor_tensor(MOZ[:, t, hf * 512:(hf + 1) * 512], MOZ[:, t, hf * 512:(hf + 1) * 512],
                                                               ztmp[:, zs, :], op=ALU.mult),
                             reads=[ztmp.b(zs), MOZ.b(t)], writes=[MOZ.b(t)])
                    P.tok(C_MZ + hf * 512, 512, cz)

                def cg(j, t0, n, pb):
                    dst = GI if j == 0 else GF
                    evac(dst[:, t0:t0 + n], pb[0:4, 0:n], [pb.b()], [dst.b()])
                P.feat(C_MI, 8, cg, psub=4)
                if NT2 > NOWN:
                    P.tok(C_MQ, 512, lambda t, pb: evac(MQS[:], pb[0:4, 0:512], [pb.b()], [MQS.b()]), tiles=[NOWN])
                    P.tok(C_MI, 8, lambda t, pb: evac(GS[:], pb[0:4, 0:8], [pb.b()], [GS.b()]), tiles=[NOWN])

            S.barrier()
            ms = contextlib.ExitStack()
            with ms:
                G2s = [make_gate_tiles(ms, "g2a_"), make_gate_tiles(ms, "g2b_")]
                Kp2 = sb("Kp2", [128, 4, 128], BF16, ms)
                Vaug2 = sb("Vaug2", [128, 2, 4, 257], BF16, ms)
                Cbf = sb("Cbf", [128, 4, 257], BF16, ms)
                Sm = sb("Sm", [128, 4, 128], BF16, ms)
                hh = sb("hh", [128, 4, 256], F32, ms)
                mixml = sb("mixml", [128, 4, 256], BF16, ms)
                sq = sb("sq", [128, 4, 8], F32, ms)
                junkh = sb("junkh", [128, 4, 256], BF16, ms)
                S.op("dve", lambda e: e.memset(Vaug2[:], 1.0), writes=[Vaug2.b((g, h)) for g in range(2) for h in range(4)])

                def gates(t):
                    tsl = slice(t * 128, (t + 1) * 128)
                    G2 = G2s[t % 2]
                    gate_rows(GI[:, tsl], GF[:, tsl], GI.b(), GF.b(), None, None, G2)
                    gate_cols(G2, PB[6], True, c0=448)
                    S.op("pool", lambda e, t=t: e.tensor_copy(Vaug2[:, t % 2, :, 0:256], MV[:, t, :].rearrange("p (h v) -> p h v", h=4)),
                         reads=[MV.b(t)], writes=[Vaug2.b((t % 2, h)) for h in range(4)])

                def head(t, h):
                    G2 = G2s[t % 2]
                    g = t % 2
                    bA, bB = PB[2 * h], PB[2 * h + 1]
                    pS = bA[:, 0:128]
                    pC = bA[:, 128:385]
                    pH = bB[:, 0:257]
                    pTv = bB[:].bitcast(BF16)[:, 768:1024]
                    sqh = sq[:, h, :]
                    S.op("dve", lambda e: e.tensor_scalar(Cst[:, h, :], Cst[:, h, :], G2["abc"][:, h:h + 1], None, op0=ALU.mult),
                         reads=[Cst.b(h), G2["abc"].b()], writes=[Cst.b(h)])
                    S.op("act", lambda e: e.copy(Cbf[:, h, :], Cst[:, h, :]), reads=[Cst.b(h)], writes=[Cbf.b(h)])
                    S.op("pe", lambda e: e.matmul(pS, lhsT=MKT[:, t, h, :], rhs=MQT[:, t, h, :], start=True, stop=True),
                         reads=[MKT.b(t), MQT.b(t)], writes=[bA.b()])
                    S.op("dve", lambda e: e.scalar_tensor_tensor(out=Sm[:, h, :], in0=pS, scalar=G2["uT"][:, h:h + 1], in1=tri[:],
                                                                 op0=ALU.mult, op1=ALU.mult),
                         reads=[bA.b(), G2["uT"].b(), tri.b()], writes=[Sm.b(h)])
                    S.op("pe", lambda e: e.matmul(pH, lhsT=Sm[:, h, :], rhs=Vaug2[:, g, h, :], start=True, stop=False),
                         reads=[Sm.b(h), Vaug2.b((g, h))], writes=[bB.b()])
                    S.op("pe", lambda e: e.matmul(pH, lhsT=MQT[:, t, h, :], rhs=Cbf[:, h, :], start=False, stop=True),
                         reads=[MQT.b(t), Cbf.b(h)], writes=[bB.b()])
                    S.op("act", lambda e: e.activation(Kp2[:, h, :], MK[:, t, h * 128:(h + 1) * 128], AF.Copy,
                                                       scale=G2["uT"][:, h:h + 1]),
                         reads=[MK.b(t), G2["uT"].b()], writes=[Kp2.b(h)])
                    S.op("pe", lambda e: e.matmul(pC, lhsT=Kp2[:, h, :], rhs=Vaug2[:, g, h, :], start=True, stop=True),
                         reads=[Kp2.b(h), Vaug2.b((g, h))], writes=[bA.b()])
                    S.op("dve", lambda e: e.tensor_tensor(Cst[:, h, :], Cst[:, h, :], pC, op=ALU.add),
                         reads=[Cst.b(h), bA.b()], writes=[Cst.b(h)])
                    S.op("act", lambda e: e.activation(sqh[:, 0:1], pH[:, 256:257], AF.Abs),
                         reads=[bB.b()], writes=[sq.b(h)])
                    S.op("dve", lambda e: e.tensor_tensor(sqh[:, 0:1], sqh[:, 0:1], G2["clT"][:, h:h + 1], op=ALU.max),
                         reads=[sq.b(h), G2["clT"].b()], writes=[sq.b(h)])
                    S.op("dve", lambda e: e.reciprocal(sqh[:, 1:2], sqh[:, 0:1]), reads=[sq.b(h)], writes=[sq.b(h)])
                    S.op("dve", lambda e: e.tensor_scalar(hh[:, h, :], pH[:, 0:256], sqh[:, 1:2], None, op0=ALU.mult),
                         reads=[bB.b(), sq.b(h)], writes=[hh.b(h)])
                    S.op("act", lambda e: e.activation(junkh[:, h, :], hh[:, h, :], AF.Square, accum_out=sqh[:, 2:3]),
                         reads=[hh.b(h)], writes=[junkh.b(h), sq.b(h)])
                    S.op("act", lambda e: e.activation(sqh[:, 3:4], sqh[:, 2:3], AF.Sqrt, scale=1.0 / 256, bias=epsb[:, 0:1]),
                         reads=[sq.b(h), epsb.b()], writes=[sq.b(h)])
                    S.op("dve", lambda e: e.reciprocal(sqh[:, 4:5], sqh[:, 3:4]), reads=[sq.b(h)], writes=[sq.b(h)])
                    S.op("dve", lambda e: e.scalar_tensor_tensor(out=hh[:, h, :], in0=hh[:, h, :], scalar=sqh[:, 4:5],
                                                                 in1=mlnw_bc[:, h * 256:(h + 1) * 256],
                                                                 op0=ALU.mult, op1=ALU.mult),
                         reads=[hh.b(h), sq.b(h), mlnw_bc.b()], writes=[hh.b(h)])
                    S.op("pool", lambda e: e.tensor_tensor(mixml[:, h, :], hh[:, h, :], MOZ[:, t, h * 256:(h + 1) * 256], op=ALU.mult),
                         reads=[hh.b(h), MOZ.b(t)], writes=[mixml.b(h)])
                    for half in range(2):
                        S.op("pe", lambda e, half=half: e.transpose(pTv[:, half * 128:(half + 1) * 128],
                                                                    mixml[:, h, half * 128:(half + 1) * 128], ident[:]),
                             reads=[mixml.b(h), ident.b()], writes=[bB.b()])
                    evac(ML_T[:, t, h * 2:h * 2 + 2, :], pTv.rearrange("p (k q) -> p k q", q=128), [bB.b()], [ML_T.b(t)])

                gates(0)
                for t in range(NOWN):
                    lists = [S.capture(lambda h=h: head(t, h)) for h in range(4)]
                    if t + 1 < NOWN:
                        lists.append(S.capture(lambda: gates(t + 1)))
                    S.replay_rr(lists, [1, 1, 1, 1, 2][:len(lists)])

            if NT2 > NOWN:
                S.barrier()
                ss = contextlib.ExitStack()
                with ss:
                    T8 = NOWN
                    CS = sb("CS", [128, 32, 128], F32, ss)
                    NS = sb("NS", [4, 512], F32, ss)
                    MS = sb("MS", [4, 4], F32, ss)
                    BI4 = sb("BI4", [4, 4], F32, ss)
                    BF4 = sb("BF4", [4, 4], F32, ss)
                    sc = sb("sc", [4, 16, 4], F32, ss)
                    SC = sb("SC", [4, 5, 4], F32, ss)
                    Rr = sb("Rr", [4, 5, 4, 4], F32, ss)
                    SCB = sb("SCB", [128, 5, 16], F32, ss)
                    prod = sb("prod", [4, 512], F32, ss)
                    Sel = sb("Sel", [4, 4, 128], BF16, ss)
                    QB = sb("QB", [128, 4, 512], F32, ss)
                    KB = sb("KB", [128, 4, 512], F32, ss)
                    CQ = sb("CQ", [128, 32], F32, ss)
                    ctmp = sb("ctmp", [128, 8, 128], F32, ss)
                    VT = sb("VT", [128, 8, 4], F32, ss)
                    MZT = sb("MZT", [128, 8, 4], F32, ss)
                    mlnwT = sb("mlnwT", [128, 8], F32, ss)
                    HS = sb("HS", [128, 4, 4, 2], F32, ss)
                    H2 = sb("H2", [128, 4, 4, 2], F32, ss)
                    DV = sb("DV", [128, 4, 4, 2], F32, ss)
                    tot = sb("tot", [128, 16], F32, ss)
                    NSn = sb("NSn", [4, 512], F32, ss)

                    S.dma("sp", lambda e: e.dma_start(out=CS[:], in_=st_C.rearrange("s h (a p) d -> p (s h a) d", p=128)), writes=[CS.b()])
                    S.dma("sp", lambda e: e.dma_start(out=NS[:], in_=st_n), writes=[NS.b()])
                    S.dma("sp", lambda e: e.dma_start(out=MS[:], in_=st_m), writes=[MS.b()])
                    S.dma("sp", lambda e: e.dma_start(out=BI4[:], in_=b_ig.rearrange("h o -> o h").partition_broadcast(4)), writes=[BI4.b()])
                    S.dma("sp", lambda e: e.dma_start(out=BF4[:], in_=b_fg.rearrange("h o -> o h").partition_broadcast(4)), writes=[BF4.b()])

                    def so(fn, r, w):
                        S.op("dve", fn, reads=r, writes=w)
                    scb = sc.b()
                    so(lambda e: e.tensor_tensor(sc[:, 0, :], GS[:, 0:4], BI4[:], op=ALU.add), [GS.b(), BI4.b()], [scb])
                    so(lambda e: e.tensor_tensor(sc[:, 1, :], GS[:, 4:8], BF4[:], op=ALU.add), [GS.b(), BF4.b()], [scb])
                    S.op("act", lambda e: e.activation(sc[:, 1, :], sc[:, 1, :], AF.Exp, scale=-1.0), reads=[scb], writes=[scb])
                    S.op("act", lambda e: e.activation(sc[:, 1, :], sc[:, 1, :], AF.Ln, bias=one4[:, 0:1]), reads=[scb, one4.b()], writes=[scb])
                    so(lambda e: e.tensor_tensor(sc[:, 2, :], MS[:], sc[:, 1, :], op=ALU.subtract), [scb, MS.b()], [scb])
                    so(lambda e: e.tensor_tensor(sc[:, 3, :], sc[:, 2, :], sc[:, 0, :], op=ALU.max), [scb], [scb])
                    so(lambda e: e.tensor_tensor(sc[:, 4, :], sc[:, 2, :], sc[:, 3, :], op=ALU.subtract), [scb], [scb])
                    so(lambda e: e.tensor_tensor(sc[:, 5, :], sc[:, 0, :], sc[:, 3, :], op=ALU.subtract), [scb], [scb])
                    S.op("act", lambda e: e.activation(sc[:, 4:6, :], sc[:, 4:6, :], AF.Exp), reads=[scb], writes=[scb])
                    S.op("act", lambda e: e.activation(sc[:, 11, :], sc[:, 3, :], AF.Exp, scale=-1.0), reads=[scb], writes=[scb])
                    so(lambda e: e.tensor_tensor(prod[:], MQS[:], MK[0:4, T8, :], op=ALU.mult), [MQS.b(), MK.b(T8)], [prod.b()])
                    so(lambda e: e.tensor_reduce(sc[:, 6, :], prod[:].rearrange("s (h d) -> s h d", h=4), AX.X, ALU.add), [prod.b()], [scb])
                    so(lambda e: e.tensor_tensor(prod[:], MQS[:], NS[:], op=ALU.mult), [MQS.b(), NS.b(), scb], [prod.b()])
                    so(lambda e: e.tensor_reduce(sc[:, 7, :], prod[:].rearrange("s (h d) -> s h d", h=4), AX.X, ALU.add), [prod.b()], [scb])
                    so(lambda e: e.scalar_tensor_tensor(out=sc[:, 8, :], in0=sc[:, 6, :], scalar=float(128.0 ** -0.5), in1=sc[:, 5, :],
                                                        op0=ALU.mult, op1=ALU.mult), [scb], [scb])
                    so(lambda e: e.tensor_tensor(sc[:, 9, :], sc[:, 4, :], sc[:, 7, :], op=ALU.mult), [scb], [scb])
                    so(lambda e: e.tensor_tensor(sc[:, 9, :], sc[:, 9, :], sc[:, 8, :], op=ALU.add), [scb], [scb])
                    so(lambda e: e.tensor_scalar(sc[:, 10, :], sc[:, 9, :], -1.0, None, op0=ALU.mult), [scb], [scb])
                    so(lambda e: e.tensor_tensor(sc[:, 10, :], sc[:, 10, :], sc[:, 9, :], op=ALU.max), [scb], [scb])
                    so(lambda e: e.tensor_tensor(sc[:, 10, :], sc[:, 10, :], sc[:, 11, :], op=ALU.max), [scb], [scb])
                    so(lambda e: e.reciprocal(sc[:, 12, :], sc[:, 10, :]), [scb], [scb])
                    so(lambda e: e.tensor_copy(SC[:, 0, :], sc[:, 4, :]), [scb], [SC.b()])
                    so(lambda e: e.tensor_scalar(SC[:, 1, :], sc[:, 5, :], float(128.0 ** -0.5), None, op0=ALU.mult), [scb], [SC.b()])
                    so(lambda e: e.tensor_tensor(SC[:, 2, :], sc[:, 4, :], sc[:, 12, :], op=ALU.mult), [scb], [SC.b()])
                    so(lambda e: e.tensor_tensor(SC[:, 3, :], sc[:, 8, :], sc[:, 12, :], op=ALU.mult), [scb], [SC.b()])
                    so(lambda e: e.tensor_copy(SC[:, 4, :], sc[:, 3, :]), [scb], [SC.b()])
                    S.dma("sp", lambda e: e.dma_start(out=ms_out, in_=SC[:, 4, :]), reads=[SC.b()])
                    so(lambda e: e.tensor_tensor(NSn[:].rearrange("s (h d) -> s h d", h=4), NS[:].rearrange("s (h d) -> s h d", h=4),
                                                 SC[:, 0, :].unsqueeze(2).to_broadcast([4, 4, 128]), op=ALU.mult), [NS.b(), SC.b()], [NSn.b()])
                    so(lambda e: e.tensor_tensor(prod[:].rearrange("s (h d) -> s h d", h=4), MK[0:4, T8, :].rearrange("s (h d) -> s h d", h=4),
                                                 SC[:, 1, :].unsqueeze(2).to_broadcast([4, 4, 128]), op=ALU.mult), [MK.b(T8), SC.b(), scb], [prod.b()])
                    so(lambda e: e.tensor_tensor(NSn[:], NSn[:], prod[:], op=ALU.add), [prod.b(), NSn.b()], [NSn.b()])
                    S.dma("sp", lambda e: e.dma_start(out=ns_out, in_=NSn[:]), reads=[NSn.b()])
                    so(lambda e: e.tensor_tensor(Rr[:], SC[:].unsqueeze(2).to_broadcast([4, 5, 4, 4]),
                                                 identf[0:4, 0:4].unsqueeze(1).unsqueeze(3).to_broadcast([4, 5, 4, 4]), op=ALU.mult),
                       [SC.b(), identf.b()], [Rr.b()])
                    pbx = PB[0]
                    S.op("pe", lambda e: e.matmul(pbx[:, 0:80], lhsT=ones4f[:], rhs=Rr[:].rearrange("s k a h -> s (k a h)"), start=True, stop=True),
                         reads=[ones4f.b(), Rr.b()], writes=[pbx.b()])
                    evac(SCB[:].rearrange("p k m -> p (k m)"), pbx[:, 0:80], [pbx.b()], [SCB.b()])
                    for s_ in range(4):
                        so(lambda e, s_=s_: e.tensor_copy(Sel[:, s_, :], ident[0:4, s_:s_ + 1].to_broadcast([4, 128])), [ident.b()], [Sel.b()])
                    for s_ in range(4):
                        for which, src_ap, dst in ((0, MQS[:], QB), (1, MK[0:4, T8, :], KB)):
                            pbq = PB[1 + (s_ * 2 + which) % 3]
                            S.op("pe", lambda e, s_=s_, src_ap=src_ap, pbq=pbq: e.matmul(pbq[:, 0:512], lhsT=Sel[:, s_, :], rhs=src_ap, start=True, stop=True),
                                 reads=[Sel.b(), MQS.b(), MK.b(T8)], writes=[pbq.b()])
                            evac(dst[:, s_, :], pbq[:, 0:512], [pbq.b()], [dst.b(s_)])
                    for s_ in range(4):
                        so(lambda e, s_=s_: e.tensor_tensor(ctmp[:].rearrange("p (h a) d -> p h a d", h=4),
                                                            CS[:, s_ * 8:(s_ + 1) * 8, :].rearrange("p (h a) d -> p h a d", h=4),
                                                            QB[:, s_, :].rearrange("p (h d) -> p h d", h=4).unsqueeze(2).to_broadcast([128, 4, 2, 128]),
                                                            op=ALU.mult), [CS.b(), QB.b(s_)], [ctmp.b()])
                        so(lambda e, s_=s_: e.tensor_reduce(CQ[:, s_ * 8:(s_ + 1) * 8], ctmp[:], AX.X, ALU.add), [ctmp.b()], [CQ.b()])
                    pbt = PB[4]
                    pbtv = pbt[:].bitcast(BF16)
                    for c in range(8):
                        S.op("pe", lambda e, c=c: e.transpose(pbtv[:, c * 4:c * 4 + 4], MV[0:4, T8, c * 128:(c + 1) * 128], ident[0:4, 0:4]),
                             reads=[MV.b(T8), ident.b()], writes=[pbt.b()])
                    evac(VT[:].rearrange("p c s -> p (c s)"), pbtv[:, 0:32], [pbt.b()], [VT.b()])
                    for c in range(8):
                        S.op("pe", lambda e, c=c: e.transpose(pbtv[:, c * 4:c * 4 + 4], MOZ[0:4, T8, c * 128:(c + 1) * 128], ident[0:4, 0:4]),
                             reads=[MOZ.b(T8), ident.b()], writes=[pbt.b()])
                    evac(MZT[:].rearrange("p c s -> p (c s)"), pbtv[:, 0:32], [pbt.b()], [MZT.b()])
                    pbw = PB[5]
                    for c in range(8):
                        S.op("pe", lambda e, c=c: e.matmul(pbw[:, c * 4:c * 4 + 4], lhsT=mlnw_bc[0:4, c * 128:(c + 1) * 128], rhs=identf[0:4, 0:4],
                                                            start=True, stop=True),
                             reads=[mlnw_bc.b(), identf.b()], writes=[pbw.b()])
                    evac(mlnwT[:], pbw[:, 0:32].rearrange("p (c k) -> p c k", k=4)[:, :, 0], [pbw.b()], [mlnwT.b()])
                    VTv = VT[:].rearrange("p (h a) s -> p s h a", h=4)
                    MZv = MZT[:].rearrange("p (h a) s -> p s h a", h=4)

                    def bc(k):
                        return SCB[:, k, :].rearrange("p (s h) -> p s h", s=4).unsqueeze(3).to_broadcast([128, 4, 4, 2])
                    so(lambda e: e.tensor_tensor(HS[:], CQ[:].rearrange("p (s h a) -> p s h a", s=4, h=4), bc(2), op=ALU.mult), [CQ.b(), SCB.b()], [HS.b()])
                    so(lambda e: e.tensor_tensor(H2[:], VTv, bc(3), op=ALU.mult), [VT.b(), SCB.b()], [H2.b()])
                    so(lambda e: e.tensor_tensor(HS[:], HS[:], H2[:], op=ALU.add), [HS.b(), H2.b()], [HS.b()])
                    so(lambda e: e.tensor_tensor(DV[:], VTv, bc(1), op=ALU.mult), [VT.b(), SCB.b()], [DV.b()])
                    so(lambda e: e.tensor_tensor(H2[:], HS[:], HS[:], op=ALU.mult), [HS.b()], [H2.b()])
                    pbn = PB[6]
                    S.op("pe", lambda e: e.matmul(pbn[:, 0:32], lhsT=identf_ones[:], rhs=H2[:].rearrange("p s h a -> p (s h a)"), start=True, stop=True),
                         reads=[identf_ones.b(), H2.b()], writes=[pbn.b()])
                    so(lambda e: e.tensor_reduce(tot[:], pbn[:, 0:32].rearrange("p (m a) -> p m a", a=2), AX.X, ALU.add), [pbn.b()], [tot.b()])
                    S.op("act", lambda e: e.activation(tot[:], tot[:], AF.Sqrt, scale=1.0 / 256, bias=epsb[:, 0:1]), reads=[tot.b(), epsb.b()], writes=[tot.b()])
                    so(lambda e: e.reciprocal(tot[:], tot[:]), [tot.b()], [tot.b()])
                    so(lambda e: e.tensor_tensor(HS[:], HS[:], tot[:].rearrange("p (s h) -> p s h", s=4).unsqueeze(3).to_broadcast([128, 4, 4, 2]),
                                                 op=ALU.mult), [HS.b(), tot.b()], [HS.b()])
                    so(lambda e: e.tensor_tensor(HS[:], HS[:], mlnwT[:].rearrange("p (h a) -> p h a", h=4).unsqueeze(1).to_broadcast([128, 4, 4, 2]),
                                                 op=ALU.mult), [HS.b(), mlnwT.b()], [HS.b()])
                    so(lambda e: e.tensor_tensor(ML_T[:, T8, :, 0:4].rearrange("p (h a) s -> p s h a", h=4), HS[:], MZv, op=ALU.mult),
                       [HS.b(), MZT.b()], [ML_T.b(T8)])
                    for s_ in range(4):
                        for hh_ in range(4):
                            for a_ in range(2):
                                idx = s_ * 8 + hh_ * 2 + a_
                                so(lambda e, s_=s_, hh_=hh_, a_=a_: e.tensor_scalar(ctmp[:, 0, :], KB[:, s_, hh_ * 128:(hh_ + 1) * 128],
                                                                                    DV[:, s_, hh_, a_:a_ + 1], None, op0=ALU.mult),
                                   [KB.b(s_), DV.b(), ctmp.b()], [ctmp.b()])
                                so(lambda e, s_=s_, hh_=hh_, idx=idx: e.scalar_tensor_tensor(out=CS[:, idx, :], in0=CS[:, idx, :],
                                                                                             scalar=SCB[:, 0, s_ * 4 + hh_:s_ * 4 + hh_ + 1], in1=ctmp[:, 0, :],
                                                                                             op0=ALU.mult, op1=ALU.add),
                                   [CS.b(), SCB.b(), ctmp.b(), CQ.b()], [CS.b()])
                    S.dma("sp", lambda e: e.dma_start(out=Cs_out.rearrange("s h (a p) d -> p (s h a) d", p=128), in_=CS[:]), reads=[CS.b()])
        S.barrier()
        ost = contextlib.ExitStack()
        with ost:
            Cn = sb("Cn", [128, 4, 2, 128], F32, ost)
            nrow = sb("nrow", [1, 4, 128], F32, ost)
            mrow = sb("mrow", [4, 1], F32, ost)
            for h in (range(4) if not _os.environ.get('KNOSTATE') else []):
                for half in range(2):
                    pc = PB[6 + (half % 2)]
                    S.op("pe", lambda e, h=h, half=half, pc=pc: e.matmul(pc[:, 0:128], lhsT=Cst[:, h, half * 128:(half + 1) * 128],
                                                                         rhs=identf[:], start=True, stop=True),
                         reads=[Cst.b(h), identf.b()], writes=[pc.b()])
                    S.op("act", lambda e, h=h, half=half, pc=pc: e.copy(Cn[:, h, half, :], pc[:, 0:128]),
                         reads=[pc.b()], writes=[Cn.b()])
                pc = PB[5]
                S.op("pe", lambda e, h=h, pc=pc: e.matmul(pc[0:1, 0:128], lhsT=Cst[:, h, 256:257], rhs=identf[:], start=True, stop=True),
                     reads=[Cst.b(h), identf.b()], writes=[pc.b()])
                S.op("act", lambda e, h=h, pc=pc: e.copy(nrow[:, h, :], pc[0:1, 0:128]), reads=[pc.b()], writes=[nrow.b()])
            S.op("dve", lambda e: e.tensor_tensor(mrow[:], Brow[:], Mrow[:], op=ALU.add), reads=[Brow.b(), Mrow.b()], writes=[mrow.b()])
            S.dma("sp", lambda e: e.dma_start(out=C_out.rearrange("h (a p) d -> p h a d", p=128), in_=Cn[:]), reads=[Cn.b()])
            S.dma("sp", lambda e: e.dma_start(out=n_out.rearrange("(o h) d -> o h d", o=1), in_=nrow[:]), reads=[nrow.b()])
            S.dma("sp", lambda e: e.dma_start(out=m_out, in_=mrow[:]), reads=[mrow.b()])
        S.barrier()
        p2c = contextlib.ExitStack()
        with p2c:
            ypre = sb("ypre", [128, NT2, D], F32, p2c)
            fnw_bc = sb("fnw_bc", [128, D], F32, p2c)
            Wbo = sb("Wbo", [128, 2, NCH, 512], BF16, p2c)
            junk3 = sb("junk3", [128, D], BF16, p2c)
            sq3 = sb("sq3", [128, 4], F32, p2c)
            S.dma("sp", lambda e: e.dma_start(out=fnw_bc[:], in_=fnorm_w.partition_broadcast(128)), writes=[fnw_bc.b()])
            for t in range(NT2):
                S.dma("sp", lambda e, t=t: e.dma_start(out=ypre[:, t, :], in_=xrows2(t)), writes=[ypre.b(t)])
            Po = Proj(None, Wbo, [PB[0], PB[1], PB[2], PB[3]], src=w_out_v)
            for g in range(4):
                sl = Po.load(g * 512, 512)
                gsl = slice(g * 512, (g + 1) * 512)
                for t in range(NT2):
                    pb = Po._bank()
                    for fc in range(16):
                        S.op("pe", lambda e, t=t, fc=fc, pb=pb, sl=sl: e.matmul(
                            pb[:, 0:512], lhsT=(ATT_T[:, t, fc, :] if fc < 8 else ML_T[:, t, fc - 8, :]), rhs=Wbo[:, sl, fc, :],
                            start=(fc == 0), stop=(fc == 15)),
                            reads=[ATT_T.b(t), ML_T.b(t), Wbo.b(sl)], writes=[pb.b()])
                    S.op("dve", lambda e, t=t, gsl=gsl, pb=pb: e.tensor_tensor(ypre[:, t, gsl], ypre[:, t, gsl], pb[:, 0:512], op=ALU.add),
                         reads=[pb.b(), ypre.b(t)], writes=[ypre.b(t)])
            for t in range(NT2):
                S.op("act", lambda e, t=t: e.activation(junk3[:], ypre[:, t, :], AF.Square, accum_out=sq3[:, 0:1]),
                     reads=[ypre.b(t)], writes=[junk3.b(), sq3.b()])
                S.op("act", lambda e: e.activation(sq3[:, 1:2], sq3[:, 0:1], AF.Sqrt, scale=1.0 / D, bias=epsb[:, 0:1]),
                     reads=[sq3.b(), epsb.b()], writes=[sq3.b()])
                S.op("dve", lambda e: e.reciprocal(sq3[:, 2:3], sq3[:, 1:2]), reads=[sq3.b()], writes=[sq3.b()])
                S.op("dve", lambda e, t=t: e.scalar_tensor_tensor(out=ypre[:, t, :], in0=ypre[:, t, :], scalar=sq3[:, 2:3], in1=fnw_bc[:],
                                                                  op0=ALU.mult, op1=ALU.mult),
                     reads=[ypre.b(t), sq3.b(), fnw_bc.b()], writes=[ypre.b(t)])
                dst = y_out[t * 128:(t + 1) * 128, :] if t < NOWN else ys_out
                S.dma("sp", lambda e, t=t, dst=dst: e.dma_start(out=dst, in_=ypre[:, t, :]), reads=[ypre.b(t)])
        S.finish("sp")
        S.emit()
    return nc


def host_consts():
    idn = np.eye(128, dtype=np.float32)
    s = np.arange(128)
    tri = (s[:, None] <= s[None, :]).astype(np.float32)
    tribias = np.where(s[None, :] <= s[:, None], 0.0, NEG).astype(np.float32)
    pw = (0.5 ** np.arange(1, NITER + 1)).astype(np.float32)[None, :]
    sut = (s[:, None] < s[None, :]).astype(np.float32)
    iota = np.concatenate([16.0 * np.arange(17), np.arange(16), np.arange(128)]).astype(np.float32)[None, :]
    return {"c_ident": idn, "c_tri": tri, "c_tribias": tribias, "c_pow": pw, "c_sut": sut, "c_iota": iota}


def make_in_maps(inputs, cores):
    xp = np.asarray(inputs["x_prompt"], np.float32)
    consts = host_consts()
    cidx_h = np.asarray(inputs["cache_idx_k"], np.float32).reshape(5120, 8192)
    ck_h = np.asarray(inputs["cache_k"], np.float32).reshape(655360, 256)
    cv_h = np.asarray(inputs["cache_v"], np.float32).reshape(655360, 256)
    maps = []
    for c in cores:
        b, j = c // 4, c % 4
        xkc = np.zeros((NT1 * 128, D), np.float32)
        nreal = 1024 * (j + 1)
        xkc[NT1 * 128 - nreal:] = xp[b, :nreal]
        vld = np.zeros((1, NT1 * 128), np.float32)
        vld[0, NT1 * 128 - nreal:] = 1.0
        xsc = np.zeros((128, D), np.float32)
        xsc[0:4] = np.asarray(inputs["x_sample"], np.float32)[4 * c:4 * c + 4, 0]
        m = {
            "xk": xkc, "valid": vld, "xs": xsc,
            "ptabT": np.ascontiguousarray(np.asarray(inputs["page_table"], np.int32)[4 * c:4 * c + 4].T),
            "cidx": cidx_h, "ck": ck_h, "cv": cv_h,
            "st_C": np.ascontiguousarray(np.asarray(inputs["state_C"], np.float32)[0, 4 * c:4 * c + 4]),
            "st_n": np.ascontiguousarray(np.asarray(inputs["state_n"], np.float32)[0, 4 * c:4 * c + 4]).reshape(4, 512),
            "st_m": np.ascontiguousarray(np.asarray(inputs["state_m"], np.float32)[0, 4 * c:4 * c + 4]),
            "w_in": np.asarray(inputs["w_in"], np.float32)[0],
            "w_out": np.asarray(inputs["w_out"], np.float32)[0],
            "norm_w": np.asarray(inputs["norm_w"], np.float32).reshape(1, D),
            "fnorm_w": np.asarray(inputs["final_norm_w"], np.float32).reshape(1, D),
            "mlnorm_w": np.asarray(inputs["ml_norm_w"], np.float32).reshape(1, 1024),
            "b_ig": np.asarray(inputs["b_igate"], np.float32).reshape(4, 1),
            "b_fg": np.asarray(inputs["b_fgate"], np.float32).reshape(4, 1),
        }
        m.update(consts)
        maps.append(m)
    return maps


_NC_CACHE = {}


def kernel(**inputs):
    cores = list(range(8))
    if "nc" not in _NC_CACHE:
        _NC_CACHE["nc"] = build_program()
    nc = _NC_CACHE["nc"]
    maps = make_in_maps(inputs, cores)
    res = run_bass_kernel_spmd(nc, maps, core_ids=cores)
    R = res.results
    f32 = np.float32
    y_prompt = np.zeros((2, 4096, D), f32)
    k_prompt = np.zeros((1, 2, 4096, 2, 128), f32)
    v_prompt = np.zeros((1, 2, 4096, 2, 128), f32)
    ik_prompt = np.zeros((1, 2, 4096, 64), f32)
    C_prompt = np.zeros((1, 2, 4, 256, 128), f32)
    n_prompt = np.zeros((1, 2, 4, 128), f32)
    m_prompt = np.zeros((1, 2, 4), f32)
    for c in cores:
        b, j = c // 4, c % 4
        sl = slice(1024 * j, 1024 * (j + 1))
        y_prompt[b, sl] = R[c]["y_out"]
        k_prompt[0, b, sl] = R[c]["k_out"].reshape(1024, 2, 128)
        v_prompt[0, b, sl] = R[c]["v_out"].reshape(1024, 2, 128)
        ik_prompt[0, b, sl] = R[c]["ki_out"]
        if j == 3:
            C_prompt[0, b] = R[c]["C_out"]
            n_prompt[0, b] = R[c]["n_out"]
            m_prompt[0, b] = R[c]["m_out"].reshape(4)
    y_sample = np.zeros((32, 1, D), f32)
    k_sample = np.zeros((1, 32, 1, 2, 128), f32)
    v_sample = np.zeros((1, 32, 1, 2, 128), f32)
    ik_sample = np.zeros((1, 32, 1, 64), f32)
    C_sample = np.zeros((1, 32, 4, 256, 128), f32)
    n_sample = np.zeros((1, 32, 4, 128), f32)
    m_sample = np.zeros((1, 32, 4), f32)
    for c in cores:
        ss = slice(4 * c, 4 * c + 4)
        y_sample[ss, 0] = R[c]["ys_out"][0:4]
        k_sample[0, ss, 0] = R[c]["ks_out"].reshape(4, 2, 128)
        v_sample[0, ss, 0] = R[c]["vs_out"].reshape(4, 2, 128)
        ik_sample[0, ss, 0] = R[c]["kis_out"]
        C_sample[0, ss] = R[c]["Cs_out"]
        n_sample[0, ss] = R[c]["ns_out"].reshape(4, 4, 128)
        m_sample[0, ss] = R[c]["ms_out"]
    return (y_prompt, y_sample, k_prompt, v_prompt, ik_prompt, C_prompt, n_prompt, m_prompt,
            k_sample, v_sample, ik_sample, C_sample, n_sample, m_sample)
```
